# Optimizing a Trainium2 kernel written in Bass

```python
import jax, jax.numpy as jnp
from jax import lax
import numpy as np

D_MODEL = 2048
BATCH = 2
SEQ = 8192
DEPTH = 4

GRID_W = 64
CTX_LEN = 256
HEAD_DIM = 128
A_HEADS = 8
A_KV_HEADS = 2
B_HEADS = 8
B_KV_HEADS = 2
C_HEADS = 8
C_QK_DIM = HEAD_DIM // 2
BRANCH_WIDTH = 1024
N_BRANCH = 3
WINDOW = 128
Q_BLOCK = 128
D_FF = 5632
HALF_STEP = 0.5
ROPE_THETA = 10000.0
NORM_EPS = 1e-6
NEG_INF = -1e30
N_MOD = 9
IN_SPLITS = (A_HEADS * HEAD_DIM, A_KV_HEADS * HEAD_DIM, A_KV_HEADS * HEAD_DIM,
             B_HEADS * HEAD_DIM, B_KV_HEADS * HEAD_DIM, B_KV_HEADS * HEAD_DIM,
             C_HEADS * HEAD_DIM, C_HEADS * HEAD_DIM, C_HEADS * HEAD_DIM,
             N_BRANCH * D_MODEL)
IN_WIDTH = (2 * A_KV_HEADS + A_HEADS) * HEAD_DIM + (2 * B_KV_HEADS + B_HEADS) * HEAD_DIM + 3 * C_HEADS * HEAD_DIM + N_BRANCH * D_MODEL

kernel_name = "hybrid_gated_attn_diffusion_trunk"


def rms_norm(x, g):
    xf = x.astype(jnp.float32)
    y = xf * lax.rsqrt(jnp.mean(xf * xf, axis=-1, keepdims=True) + NORM_EPS)
    return (y * g.astype(jnp.float32)).astype(x.dtype)


def ada_norm(x, g, shift, scale):
    return rms_norm(x, g) * (1 + scale) + shift


def swiglu(h, w1, w2):
    gate, up = jnp.split(h @ w1, 2, axis=-1)
    return (jax.nn.silu(gate) * up) @ w2


def split_columns(p):
    return jnp.split(p, np.cumsum(IN_SPLITS)[:-1].tolist(), axis=-1)


def split_heads(t, n_heads):
    b, l, _ = t.shape
    return t.reshape(b, l, n_heads, -1).transpose(0, 2, 1, 3)


def merge_heads(t):
    b, h, l, d = t.shape
    return t.transpose(0, 2, 1, 3).reshape(b, l, h * d)


def group_queries(q, n_kv):
    b, h, l, d = q.shape
    return q.reshape(b, n_kv, h // n_kv, l, d)


def ungroup(o):
    b, k, g, l, d = o.shape
    return o.transpose(0, 3, 1, 2, 4).reshape(b, l, k * g * d)


def to_blocks(q):
    nb = q.shape[-2] // Q_BLOCK
    return jnp.moveaxis(q.reshape(*q.shape[:-2], nb, Q_BLOCK, q.shape[-1]), -3, 0)


def from_blocks(o):
    o = jnp.moveaxis(o, 0, -3)
    return o.reshape(*o.shape[:-3], o.shape[-3] * o.shape[-2], o.shape[-1])


def axial_rope_tables(n_tok, rot_dim):
    rows = n_tok // GRID_W
    row = jnp.repeat(jnp.arange(rows, dtype=jnp.int32), GRID_W).astype(jnp.float32)
    col = jnp.tile(jnp.arange(GRID_W, dtype=jnp.int32), rows).astype(jnp.float32)
    d_ax = rot_dim // 2
    inv = ROPE_THETA ** (-jnp.arange(0, d_ax, 2, dtype=jnp.float32) / d_ax)
    ang_r = row[:, None] * inv[None, :]
    ang_c = col[:, None] * inv[None, :]
    ang = jnp.concatenate([ang_r, ang_r, ang_c, ang_c], axis=-1)
    return jnp.cos(ang), jnp.sin(ang)


def apply_rope(x, tables):
    cos, sin = tables
    a, b, c, d = jnp.split(x, 4, axis=-1)
    rot = jnp.concatenate([-b, a, -d, c], axis=-1)
    return x * cos.astype(x.dtype) + rot * sin.astype(x.dtype)


def softmax_attend(q, k, v, scale):
    s = jnp.einsum('bkgqd,bksd->bkgqs', q, k, preferred_element_type=jnp.float32) * scale
    p = jax.nn.softmax(s, axis=-1).astype(v.dtype)
    return jnp.einsum('bkgqs,bksd->bkgqd', p, v)


def sink_softmax_attend(q, k, v, scale, sink, mask):
    s = jnp.einsum('bkgqd,bksd->bkgqs', q, k, preferred_element_type=jnp.float32) * scale
    if mask is not None:
        s = jnp.where(mask, s, NEG_INF)
    m = jnp.maximum(jnp.max(s, axis=-1, keepdims=True), sink)
    e = jnp.exp(s - m)
    p = e / (jnp.sum(e, axis=-1, keepdims=True) + jnp.exp(sink - m))
    return jnp.einsum('bkgqs,bksd->bkgqd', p.astype(v.dtype), v)


def diff_attend(q1, q2, k1, k2, v, lam, scale):
    s1 = jnp.einsum('bhqd,bhsd->bhqs', q1, k1, preferred_element_type=jnp.float32) * scale
    s2 = jnp.einsum('bhqd,bhsd->bhqs', q2, k2, preferred_element_type=jnp.float32) * scale
    p = jax.nn.softmax(s1, axis=-1) - lam * jax.nn.softmax(s2, axis=-1)
    return jnp.einsum('bhqs,bhsd->bhqd', p.astype(v.dtype), v)


def branch_global(pc, pl, qk_norm, rope, ctx_out):
    def prep(p, use_rope):
        q = rms_norm(split_heads(p[0], A_HEADS), qk_norm[0])
        k = rms_norm(split_heads(p[1], A_KV_HEADS), qk_norm[1])
        v = split_heads(p[2], A_KV_HEADS)
        if use_rope:
            q, k = apply_rope(q, rope), apply_rope(k, rope)
        return group_queries(q, A_KV_HEADS), k, v
    scale = HEAD_DIM ** -0.5
    qc, kc, vc = prep(pc, False)
    ql, kl, vl = prep(pl, True)
    k_all = jnp.concatenate([kc, kl], axis=2)
    v_all = jnp.concatenate([vc, vl], axis=2)
    o_lat = from_blocks(lax.map(lambda qb: softmax_attend(qb, k_all, v_all, scale), to_blocks(ql)))
    o_ctx = ungroup(softmax_attend(qc, kc, vc, scale)) if ctx_out else None
    return o_ctx, ungroup(o_lat)


def branch_window(pc, pl, sink, rope, ctx_out):
    scale = HEAD_DIM ** -0.5
    qc = group_queries(split_heads(pc[0], B_HEADS), B_KV_HEADS)
    kc, vc = split_heads(pc[1], B_KV_HEADS), split_heads(pc[2], B_KV_HEADS)
    ql = group_queries(apply_rope(split_heads(pl[0], B_HEADS), rope), B_KV_HEADS)
    kl = apply_rope(split_heads(pl[1], B_KV_HEADS), rope)
    vl = split_heads(pl[2], B_KV_HEADS)
    sink_g = sink.astype(jnp.float32).reshape(B_KV_HEADS, -1)[None, :, :, None, None]
    n = kl.shape[2]
    nb = n // Q_BLOCK
    span = Q_BLOCK + 2 * WINDOW
    pad = ((0, 0), (0, 0), (WINDOW, WINDOW), (0, 0))
    kp, vp = jnp.pad(kl, pad), jnp.pad(vl, pad)
    rel = jnp.arange(span)[None, :] - WINDOW - jnp.arange(Q_BLOCK)[:, None]
    in_band = jnp.abs(rel) <= WINDOW
    ctx_ok = jnp.ones((Q_BLOCK, kc.shape[2]), dtype=bool)

    def one(args):
        start, qb = args
        kb = lax.dynamic_slice_in_dim(kp, start, span, axis=2)
        vb = lax.dynamic_slice_in_dim(vp, start, span, axis=2)
        kpos = start - WINDOW + jnp.arange(span)
        ok = in_band & ((kpos >= 0) & (kpos < n))[None, :]
        mask = jnp.concatenate([ctx_ok, ok], axis=1)
        return sink_softmax_attend(qb, jnp.concatenate([kc, kb], axis=2),
                                   jnp.concatenate([vc, vb], axis=2), scale, sink_g, mask)

    starts = jnp.arange(nb, dtype=jnp.int32) * Q_BLOCK
    o_lat = from_blocks(lax.map(one, (starts, to_blocks(ql))))
    o_ctx = ungroup(sink_softmax_attend(qc, kc, vc, scale, sink_g, None)) if ctx_out else None
    return o_ctx, ungroup(o_lat)


def branch_diff(pc, pl, lam, subln, lambda_init, rope, ctx_out):
    def prep(p, use_rope):
        q, k, v = split_heads(p[0], C_HEADS), split_heads(p[1], C_HEADS), split_heads(p[2], C_HEADS)
        q1, q2 = jnp.split(q, 2, axis=-1)
        k1, k2 = jnp.split(k, 2, axis=-1)
        if use_rope:
            q1, q2, k1, k2 = (apply_rope(t, rope) for t in (q1, q2, k1, k2))
        return q1, q2, k1, k2, v
    scale = C_QK_DIM ** -0.5
    lf = lam.astype(jnp.float32)
    lam_val = jnp.exp(jnp.sum(lf[0] * lf[1])) - jnp.exp(jnp.sum(lf[2] * lf[3])) + lambda_init
    q1c, q2c, k1c, k2c, vc = prep(pc, False)
    q1l, q2l, k1l, k2l, vl = prep(pl, True)
    k1a = jnp.concatenate([k1c, k1l], axis=2)
    k2a = jnp.concatenate([k2c, k2l], axis=2)
    va = jnp.concatenate([vc, vl], axis=2)
    o_lat = from_blocks(lax.map(lambda a: diff_attend(a[0], a[1], k1a, k2a, va, lam_val, scale),
                                (to_blocks(q1l), to_blocks(q2l))))
    post = lambda o: merge_heads(rms_norm(o, subln) * (1 - lambda_init))
    o_ctx = post(diff_attend(q1c, q2c, k1c, k2c, vc, lam_val, scale)) if ctx_out else None
    return o_ctx, post(o_lat)


def merge_branches(outs, gate_cols, w_branch, w_out):
    gates = jnp.split(jax.nn.sigmoid(gate_cols), N_BRANCH, axis=-1)
    y = gates[0] * (outs[0] @ w_branch[0])
    for r in range(1, N_BRANCH):
        y = y + gates[r] * (outs[r] @ w_branch[r])
    return y @ w_out


def token_mixer(h_ctx, h_lat, w_in, qk_norm_a, sink_b, lam_c, subln_c, w_branch, w_out,
                lambda_init, rope_ab, rope_c, ctx_out):
    pc = split_columns(h_ctx @ w_in)
    pl = split_columns(h_lat @ w_in)
    a_ctx, a_lat = branch_global(pc[0:3], pl[0:3], qk_norm_a, rope_ab, ctx_out)
    b_ctx, b_lat = branch_window(pc[3:6], pl[3:6], sink_b, rope_ab, ctx_out)
    d_ctx, d_lat = branch_diff(pc[6:9], pl[6:9], lam_c, subln_c, lambda_init, rope_c, ctx_out)
    y_lat = merge_branches((a_lat, b_lat, d_lat), pl[9], w_branch, w_out)
    y_ctx = merge_branches((a_ctx, b_ctx, d_ctx), pc[9], w_branch, w_out) if ctx_out else None
    return y_ctx, y_lat


def ffn_half_step(x, mod, g_pre, g_post, w1, w2):
    shift, scale, gate = mod
    h = ada_norm(x, g_pre, shift, scale)
    return x + HALF_STEP * gate * rms_norm(swiglu(h, w1, w2), g_post)


def setup_inputs(seed: int = 0) -> dict:
    key = jax.random.key(seed)
    ks = jax.random.split(key, 17)
    f32 = jnp.float32

    def normal(k, shape, std):
        return jax.random.normal(k, shape, f32) * std

    return {
        "x": normal(ks[0], (BATCH, SEQ, D_MODEL), 1.0),
        "c": normal(ks[1], (BATCH, D_MODEL), 1.0),
        "ctx": normal(ks[2], (BATCH, CTX_LEN, D_MODEL), 1.0),
        "c_ctx": normal(ks[3], (D_MODEL,), 1.0),
        "w_ada": normal(ks[4], (DEPTH, D_MODEL, N_MOD * D_MODEL), 0.5 * D_MODEL ** -0.5),
        "b_ada": normal(ks[5], (DEPTH, N_MOD * D_MODEL), 0.01),
        "norm_pre": 1.0 + normal(ks[6], (DEPTH, 3, D_MODEL), 0.02),
        "norm_post": 1.0 + normal(ks[7], (DEPTH, 3, D_MODEL), 0.02),
        "w_ff_in": normal(ks[8], (DEPTH, 2, D_MODEL, 2 * D_FF), D_MODEL ** -0.5),
        "w_ff_out": normal(ks[9], (DEPTH, 2, D_FF, D_MODEL), D_FF ** -0.5),
        "w_in": normal(ks[10], (DEPTH, D_MODEL, IN_WIDTH), D_MODEL ** -0.5),
        "qk_norm_a": 1.0 + normal(ks[11], (DEPTH, 2, HEAD_DIM), 0.02),
        "sink_b": normal(ks[12], (DEPTH, B_HEADS), 0.5),
        "lam_c": normal(ks[13], (DEPTH, 4, C_QK_DIM), 0.1),
        "subln_c": 1.0 + normal(ks[14], (DEPTH, HEAD_DIM), 0.02),
        "w_branch": normal(ks[15], (DEPTH, N_BRANCH, BRANCH_WIDTH, D_MODEL), BRANCH_WIDTH ** -0.5),
        "w_out": normal(ks[16], (DEPTH, D_MODEL, D_MODEL), D_MODEL ** -0.5),
    }


def reference(x, c, ctx, c_ctx, w_ada, b_ada, norm_pre, norm_post, w_ff_in, w_ff_out, w_in,
              qk_norm_a, sink_b, lam_c, subln_c, w_branch, w_out):
    n_lat = x.shape[1]
    rope_ab = axial_rope_tables(n_lat, HEAD_DIM)
    rope_c = axial_rope_tables(n_lat, C_QK_DIM)
    x_lat, x_ctx = x, ctx
    for l in range(DEPTH):
        last = l == DEPTH - 1
        lambda_init = 0.8 - 0.6 * float(np.exp(-0.3 * l))
        mod_lat = jnp.split((jax.nn.silu(c) @ w_ada[l] + b_ada[l])[:, None, :], N_MOD, axis=-1)
        mod_ctx = jnp.split((jax.nn.silu(c_ctx) @ w_ada[l] + b_ada[l])[None, None, :], N_MOD, axis=-1)
        x_ctx = ffn_half_step(x_ctx, mod_ctx[0:3], norm_pre[l, 0], norm_post[l, 0], w_ff_in[l, 0], w_ff_out[l, 0])
        x_lat = ffn_half_step(x_lat, mod_lat[0:3], norm_pre[l, 0], norm_post[l, 0], w_ff_in[l, 0], w_ff_out[l, 0])
        h_ctx = ada_norm(x_ctx, norm_pre[l, 1], mod_ctx[3], mod_ctx[4])
        h_lat = ada_norm(x_lat, norm_pre[l, 1], mod_lat[3], mod_lat[4])
        y_ctx, y_lat = token_mixer(h_ctx, h_lat, w_in[l], qk_norm_a[l], sink_b[l], lam_c[l], subln_c[l],
                                   w_branch[l], w_out[l], lambda_init, rope_ab, rope_c, not last)
        x_lat = x_lat + mod_lat[5] * rms_norm(y_lat, norm_post[l, 1])
        x_lat = ffn_half_step(x_lat, mod_lat[6:9], norm_pre[l, 2], norm_post[l, 2], w_ff_in[l, 1], w_ff_out[l, 1])
        if not last:
            x_ctx = x_ctx + mod_ctx[5] * rms_norm(y_ctx, norm_post[l, 1])
            x_ctx = ffn_half_step(x_ctx, mod_ctx[6:9], norm_pre[l, 2], norm_post[l, 2], w_ff_in[l, 1], w_ff_out[l, 1])
    return x_lat
```

```python
import contextlib
import numpy as np
import ml_dtypes
import concourse.bass as bass
import concourse.mybir as mybir
from concourse.bass_utils import run_bass_kernel_spmd

F32 = mybir.dt.float32
BF16 = mybir.dt.bfloat16
AF = mybir.ActivationFunctionType
ALU = mybir.AluOpType
AX = mybir.AxisListType
NPBF = ml_dtypes.bfloat16

D = 2048
KC = 16
DFF = 5632
JC = 44
EPS = 1e-6
DEPTH = 4
NCORE = 2
NR = 1
SEQ = 8192
TL = SEQ // NR
CTX = 256
CTXL = CTX // NR
NCT = (CTXL + 127) // 128
TLC = TL + CTXL
GRID_W = 64
HD = 128
INW = 12288
W = 512
NQC = 24
NKC = 12
NVC = 12
NTI = TL // 128 + NCT
DEBUG = False


def set_scale(seq):
    global SEQ, TL, TLC, NTI
    SEQ = seq
    TL = SEQ // NR
    TLC = TL + CTXL
    NTI = TL // 128 + NCT


class Buf:
    __slots__ = ("name", "last_w", "readers")

    def __init__(self, name=""):
        self.name = name
        self.last_w = None
        self.readers = {}


class Prog:
    ENGINES = ("pe", "act", "dve", "pool", "sp")

    def __init__(self, nc, es):
        self.nc = nc
        self.es = es
        self.ops = []
        self.emap = {"pe": nc.tensor, "act": nc.scalar, "dve": nc.vector, "pool": nc.gpsimd, "sp": nc.sync}
        self.base = 0
        self.chan = []
        self.val = []
        self.waited = {}
        self.last_on_chan = {}
        self.count = {}
        self.sems = {}
        self.n_ops = 0

    def I(self, engine, method, reads, writes, *args, **kw):
        f = getattr(self.emap[engine], method)
        self.ops.append((engine, (lambda: f(*args, **kw)), tuple(reads), tuple(writes), None))

    def dma(self, engine, out, in_, reads, writes, stream):
        f = self.emap[engine].dma_start
        self.ops.append((engine, (lambda: f(out=out, in_=in_)), tuple(reads), tuple(writes), stream))

    def barrier(self):
        self.ops.append(("barrier", None, (), (), None))

    def flush(self):
        nc = self.nc
        ops = self.ops
        n = len(ops)
        base = self.base
        chan = self.chan
        val = self.val
        waited = self.waited
        last_on_chan = self.last_on_chan
        for o in ops:
            chan.append(o[4] if o[4] is not None else o[0])
            val.append(0)
        waits = [None] * n
        signal = [False] * n
        for li, (eng, emit, r, w, stream) in enumerate(ops):
            i = base + li
            if eng == "barrier":
                wl = {}
                for c, j in last_on_chan.items():
                    need = False
                    for e in self.ENGINES:
                        if waited.get((e, c), -1) < j:
                            waited[(e, c)] = j
                            need = True
                    if need:
                        wl[c] = j
                        if j >= base:
                            signal[j - base] = True
                waits[li] = wl
                continue
            deps = {}
            for b in r:
                j = b.last_w
                if j is not None:
                    c = chan[j]
                    if deps.get(c, -1) < j:
                        deps[c] = j
            for b in w:
                j = b.last_w
                if j is not None:
                    c = chan[j]
                    if deps.get(c, -1) < j:
                        deps[c] = j
                for c, j in b.readers.items():
                    if deps.get(c, -1) < j:
                        deps[c] = j
            my = chan[i]
            wl = None
            for c, j in deps.items():
                if c == my and (c == "pe" or stream is not None):
                    continue
                if waited.get((eng, c), -1) >= j:
                    continue
                waited[(eng, c)] = j
                assert j >= base, "dependency on an un-signalled op from a previous phase"
                signal[j - base] = True
                if wl is None:
                    wl = {}
                wl[c] = j
            waits[li] = wl
            for b in r:
                b.readers[my] = i
            for b in w:
                b.last_w = i
                b.readers = {}
            last_on_chan[my] = i
        count = self.count
        for li, (eng, emit, r, w, stream) in enumerate(ops):
            if eng == "barrier":
                continue
            i = base + li
            c = chan[i]
            if stream is not None:
                count[c] = count.get(c, 0) + 16
                val[i] = count[c]
            elif signal[li]:
                count[c] = count.get(c, 0) + 1
                val[i] = count[c]
        sems = self.sems
        for c in count:
            if c not in sems:
                sems[c] = self.es.enter_context(nc.semaphore("s_" + c))
        for li, (eng, emit, r, w, stream) in enumerate(ops):
            i = base + li
            if eng == "barrier":
                for e in self.ENGINES:
                    ee = self.emap[e]
                    for c, j in waits[li].items():
                        ee.wait_ge(sems[c], val[j])
                continue
            e = self.emap[eng]
            if waits[li]:
                for c, j in waits[li].items():
                    e.wait_ge(sems[c], val[j])
            ins = emit()
            if stream is not None:
                ins.then_inc(sems[chan[i]], 16)
            elif signal[li]:
                ins.then_inc(sems[chan[i]], 1)
        self.base += n
        self.n_ops += n
        self.ops = []

    def finish(self):
        self.barrier()
        self.flush()
        for c, v in self.count.items():
            self.nc.sync.wait_ge(self.sems[c], v)
        return dict(n_ops=self.n_ops, n_sems=len(self.sems))


class Ring:
    def __init__(self, items):
        self.items = items
        self.i = 0

    def next(self):
        it = self.items[self.i % len(self.items)]
        self.i += 1
        return it


class K:
    def __init__(self, nc, es):
        self.nc = nc
        self.es = es
        self.P = Prog(nc, es)
        self.dbufs = {}
        self.uid = 0

    def sb(self, shape, dt, name=None):
        self.uid += 1
        return self.es_cur.enter_context(self.nc.sbuf_tensor(f"{name or 't'}_{self.uid}", list(shape), dt))

    def ps(self, shape, dt=F32, name=None):
        self.uid += 1
        return self.es_cur.enter_context(self.nc.psum_tensor(f"{name or 'p'}_{self.uid}", list(shape), dt))

    def ring(self, n, shape, dt, name=None, psum=False):
        f = self.ps if psum else self.sb
        return Ring([(f(shape, dt, name), Buf(name or "")) for _ in range(n)])

    def dbuf(self, key):
        b = self.dbufs.get(key)
        if b is None:
            b = self.dbufs[key] = Buf(str(key))
        return b

    def wscr(self, name, n_units, unit_shape):
        d = self.__dict__.setdefault("_wscr", {})
        if name not in d:
            d[name] = self.nc.dram_tensor("wscr_" + name, [n_units, 128] + list(unit_shape), BF16, kind="Internal").ap()
        return d[name]

    def wload(self, name, u, slot, b_slot, si, first, cast_parts):
        P = self.P
        scr = self._wscr[name]
        wb = self.dbuf(("wscr", name, u))
        if first:
            for dst, src in cast_parts:
                P.dma("pool", dst, src, [], [b_slot], f"{name}_{si}")
            pend = self.__dict__.setdefault("_wpend", {})
            prev = pend.get(name)
            if prev is not None:
                prev()
            pend[name] = lambda: P.dma("pool", scr[u], slot[:], [b_slot], [wb], f"{name}s_{si}")
        else:
            self.wflush(name)
            P.dma("pool", slot[:], scr[u], [wb], [b_slot], f"{name}_{si}")

    def wflush(self, name):
        pend = self.__dict__.setdefault("_wpend", {})
        prev = pend.pop(name, None)
        if prev is not None:
            prev()

    def dram_in(self, name, shape, dt):
        return self.nc.dram_tensor(name, list(shape), dt, kind="ExternalInput").ap()

    def dram_out(self, name, shape, dt):
        return self.nc.dram_tensor(name, list(shape), dt, kind="ExternalOutput").ap()

    @contextlib.contextmanager
    def phase(self):
        with contextlib.ExitStack() as pes:
            old = getattr(self, "es_cur", None)
            self.es_cur = pes
            yield
            self.P.barrier()
            self.P.flush()
            self.es_cur = old


def tiles_list():
    return [(t * W, W, False) for t in range(TL // W)] + [(TL, CTXL, True)]


def emit_rstd(k, stat_ps, b_stat, rstd, b_rstd, tmp, b_tmp, w, nfeat):
    P = k.P
    P.I("dve", "tensor_scalar", [b_stat], [b_tmp], out=tmp[:, :w], in0=stat_ps[:, :w], scalar1=1.0 / nfeat, scalar2=EPS,
        op0=ALU.mult, op1=ALU.add)
    P.I("act", "sqrt", [b_tmp], [b_tmp], out=tmp[:, :w], in_=tmp[:, :w])
    P.I("dve", "reciprocal", [b_tmp], [b_rstd], out=rstd[:, :w], in_=tmp[:, :w])


def setup_consts(k):
    P = k.P
    k.ones = k.sb([128, 128], BF16, "ones")
    k.b_const = Buf("const")
    P.I("dve", "memset", [], [k.b_const], k.ones[:], 1.0)
    k.ones32 = k.sb([128, 128], F32, "ones32")
    P.I("dve", "memset", [], [k.b_const], k.ones32[:], 1.0)


def setup_vecs(k, modv_d, npre_d, npost_d):
    P = k.P
    modv = k.sb([128, 2, 9, KC], F32, "modv")
    npre = k.sb([128, 3, KC], F32, "npre")
    npost = k.sb([128, 3, KC], F32, "npost")
    gs = k.sb([128, 2, 3, KC], F32, "gs")
    hg = k.sb([128, 2, 3, KC], F32, "hg")
    b_in = Buf("vec_in")
    k.b_vec = Buf("vec")
    P.dma("sp", modv[:], modv_d, [], [b_in], "ld_modv")
    P.dma("sp", npre[:], npre_d, [], [b_in], "ld_npre")
    P.dma("sp", npost[:], npost_d, [], [b_in], "ld_npost")
    vec = {}
    for v in range(2):
        for s in range(3):
            P.I("dve", "scalar_tensor_tensor", [b_in], [k.b_vec], out=gs[:, v, s, :], in0=modv[:, v, 3 * s + 1, :], scalar=1.0,
                in1=npre[:, s, :], op0=ALU.add, op1=ALU.mult)
            P.I("dve", "scalar_tensor_tensor", [b_in], [k.b_vec], out=hg[:, v, s, :], in0=modv[:, v, 3 * s + 2, :],
                scalar=(1.0 if s == 1 else 0.5), in1=npost[:, s, :], op0=ALU.mult, op1=ALU.mult)
            vec[(s, v)] = dict(gs=gs[:, v, s, :], sh=modv[:, v, 3 * s, :], hg=hg[:, v, s, :])
    k.b_vec_in = b_in
    return vec


def norm_tile(k, xT_d, xkey, off, w, V, xy, b_xy, h, b_h, sq, tmpf, stat, rstd, b_rstd, rtmp, b_rtmp):
    P = k.P
    P.dma("sp", xy[:, :, :w], xT_d[:, off:off + w].rearrange("(kc p) t -> p kc t", p=128), [k.dbuf((xkey, off))], b_xy, "ld_xy")
    st, b_st = stat
    for kk in range(KC):
        s_, b_s = sq.next()
        P.I("act", "activation", [b_xy[kk]], [b_s], out=s_[:, :w], in_=xy[:, kk, :w], func=AF.Square)
        P.I("pe", "matmul", [b_s, k.b_const], [b_st], st[:, :w], lhsT=k.ones[:], rhs=s_[:, :w], start=(kk == 0), stop=(kk == KC - 1))
    emit_rstd(k, st, b_st, rstd, b_rstd, rtmp, b_rtmp, w, D)
    for kk in range(KC):
        t_, b_t = tmpf.next()
        P.I("dve", "tensor_tensor", [b_xy[kk], b_rstd], [b_t], out=t_[:, :w], in0=xy[:, kk, :w], in1=rstd[:, :w], op=ALU.mult)
        P.I("act", "activation", [b_t, k.b_vec, k.b_vec_in], [b_h[kk]], out=h[:, kk, :w], in_=t_[:, :w], func=AF.Identity,
            bias=V["sh"][:, kk:kk + 1], scale=V["gs"][:, kk:kk + 1])


def residual_update(k, xin_d, xin_key, xout_d, xout_key, off, w, V, xy, b_xy, rstd, b_rstd, tmpf, xst, xo):
    P = k.P
    lds = []

    def ld(n):
        x_, b_x = xst.next()
        P.dma("sp", x_[:, :w], xin_d[n * 128:(n + 1) * 128, off:off + w], [k.dbuf((xin_key, off))], [b_x], f"xst{(xst.i - 1) % len(xst.items)}")
        lds.append((x_, b_x))
    ld(0)
    ld(1)
    for n in range(KC):
        x_, b_x = lds[n]
        t_, b_t = tmpf.next()
        P.I("dve", "tensor_tensor", [b_xy[n], b_rstd], [b_t], out=t_[:, :w], in0=xy[:, n, :w], in1=rstd[:, :w], op=ALU.mult)
        o_, b_o = xo.next()
        oi = (xo.i - 1) % len(xo.items)
        P.I("dve", "scalar_tensor_tensor", [b_t, b_x, k.b_vec], [b_o], out=o_[:, :w], in0=t_[:, :w], scalar=V["hg"][:, n:n + 1],
            in1=x_[:, :w], op0=ALU.mult, op1=ALU.add)
        if n + 2 < KC:
            ld(n + 2)
        P.dma("sp", xout_d[n * 128:(n + 1) * 128, off:off + w], o_[:, :w], [b_o], [k.dbuf((xout_key, off))], f"xo{oi}")


def ffn_sublayer(k, xin_d, xin_key, xout_d, xout_key, w1_d, w2_d, vec, s):
    P = k.P
    with k.phase():
        xy = k.sb([128, KC, W], F32, "xy"); b_xy = [Buf() for _ in range(KC)]
        h = k.sb([128, KC, W], BF16, "h"); b_h = [Buf() for _ in range(KC)]
        G = k.sb([128, JC, W], BF16, "G"); b_G = [Buf() for _ in range(JC)]
        w1r = k.ring(2, [128, KC, 2, 256], BF16, "w1r")
        w2r = k.ring(2, [128, JC, 256], BF16, "w2r")
        sq = k.ring(2, [128, W], BF16, "sq")
        tmpf = k.ring(2, [128, W], F32, "tmpf")
        sg = k.ring(2, [128, W], F32, "sg")
        rstd = k.sb([128, W], F32, "rstd"); b_rstd = Buf()
        rtmp = k.sb([128, W], F32, "rtmp"); b_rtmp = Buf()
        xst = k.ring(3, [128, W], F32, "xst")
        xo = k.ring(2, [128, W], F32, "xo")
        gp = Ring([(k.ps([128, W]), Buf(), k.ps([128, W]), Buf()) for _ in range(2)])
        yp = k.ring(3, [128, W], F32, "yp", psum=True)
        stat = (k.ps([128, W]), Buf())
        w1v = w1_d.rearrange("(kc p) n -> p kc n", p=128)
        w2v = w2_d.rearrange("(jc p) n -> p jc n", p=128)
        k.wscr("w1", JC // 2, [KC, 2, 256]); k.wscr("w2", KC // 2, [JC, 256])
        for ti, (off, w, is_ctx) in enumerate(tiles_list()):
            first = ti == 0
            V = vec[(s, 1 if is_ctx else 0)]
            norm_tile(k, xin_d, xin_key, off, w, V, xy, b_xy, h, b_h, sq, tmpf, stat, rstd, b_rstd, rtmp, b_rtmp)
            for jp in range(JC // 2):
                wt, b_w = w1r.next()
                si = (w1r.i - 1) % 2
                k.wload("w1", jp, wt, b_w, si, first, [(wt[:, :, 0, :], w1v[:, :, jp * 256:(jp + 1) * 256]),
                                                        (wt[:, :, 1, :], w1v[:, :, DFF + jp * 256:DFF + (jp + 1) * 256])])
                for jj in range(2):
                    j = jp * 2 + jj
                    pg, b_pg, pu, b_pu = gp.next()
                    for kk in range(KC):
                        P.I("pe", "matmul", [b_w, b_h[kk]], [b_pg], pg[:, :w], lhsT=wt[:, kk, 0, jj * 128:(jj + 1) * 128],
                            rhs=h[:, kk, :w], start=(kk == 0), stop=(kk == KC - 1))
                    for kk in range(KC):
                        P.I("pe", "matmul", [b_w, b_h[kk]], [b_pu], pu[:, :w], lhsT=wt[:, kk, 1, jj * 128:(jj + 1) * 128],
                            rhs=h[:, kk, :w], start=(kk == 0), stop=(kk == KC - 1))
                    s_, b_s = sg.next()
                    P.I("act", "activation", [b_pg], [b_s], out=s_[:, :w], in_=pg[:, :w], func=AF.Silu)
                    P.I("dve", "tensor_tensor", [b_pu, b_s], [b_G[j]], out=G[:, j, :w], in0=pu[:, :w], in1=s_[:, :w], op=ALU.mult)
            st, b_st = stat
            k.wflush("w1")
            for npair in range(KC // 2):
                wt, b_w = w2r.next()
                si = (w2r.i - 1) % 2
                k.wload("w2", npair, wt, b_w, si, first, [(wt[:], w2v[:, :, npair * 256:(npair + 1) * 256])])
                for nn in range(2):
                    n = npair * 2 + nn
                    y_, b_y = yp.next()
                    for j in range(JC):
                        P.I("pe", "matmul", [b_w, b_G[j]], [b_y], y_[:, :w], lhsT=wt[:, j, nn * 128:(nn + 1) * 128],
                            rhs=G[:, j, :w], start=(j == 0), stop=(j == JC - 1))
                    P.I("dve", "tensor_copy", [b_y], [b_xy[n]], out=xy[:, n, :w], in_=y_[:, :w])
                    q_, b_q = sq.next()
                    P.I("act", "activation", [b_xy[n]], [b_q], out=q_[:, :w], in_=xy[:, n, :w], func=AF.Square)
                    P.I("pe", "matmul", [b_q, k.b_const], [b_st], st[:, :w], lhsT=k.ones[:], rhs=q_[:, :w], start=(n == 0), stop=(n == KC - 1))
            k.wflush("w2")
            emit_rstd(k, st, b_st, rstd, b_rstd, rtmp, b_rtmp, w, D)
            residual_update(k, xin_d, xin_key, xout_d, xout_key, off, w, V, xy, b_xy, rstd, b_rstd, tmpf, xst, xo)


def chunk_role(c):
    if c < 8: return ("q", "a", c)
    if c < 10: return ("k", "a", c - 8)
    if c < 12: return ("v", "a", c - 10)
    if c < 20: return ("q", "b", c - 12)
    if c < 22: return ("k", "b", c - 20)
    if c < 24: return ("v", "b", c - 22)
    if c < 32: return ("q", "c", c - 24)
    if c < 40: return ("k", "c", c - 32)
    if c < 48: return ("v", "c", c - 40)
    return ("g", None, c - 48)


QBASE = {"a": 0, "b": 8, "c": 16}
KBASE = {"a": 0, "b": 2, "c": 4}


def inproj_stage(k, x_d, xkey, win_d, vec, T):
    P = k.P
    with k.phase():
        xy = k.sb([128, KC, W], F32, "xy"); b_xy = [Buf() for _ in range(KC)]
        h = k.sb([128, KC, W], BF16, "h"); b_h = [Buf() for _ in range(KC)]
        wr = k.ring(3, [128, KC, 256], BF16, "wr")
        sq = k.ring(2, [128, W], BF16, "sq")
        tmpf = k.ring(2, [128, W], F32, "tmpf")
        rstd = k.sb([128, W], F32, "rstd"); b_rstd = Buf()
        rtmp = k.sb([128, W], F32, "rtmp"); b_rtmp = Buf()
        hrs = k.sb([128, W], F32, "hrs"); b_hrs = Buf()
        hrt = k.sb([128, W], F32, "hrt"); b_hrt = Buf()
        rope = k.sb([128, 4, W], F32, "rope"); b_rope = Buf()
        rmat = k.sb([128, 2, 128], F32, "rmat"); b_rmat = Buf()
        qkn = k.sb([128, 2], F32, "qkn"); b_qkn = Buf()
        qn = k.ring(2, [128, W], F32, "qn")
        t1 = k.ring(2, [128, W], F32, "t1")
        t2 = k.ring(2, [128, W], F32, "t2")
        ob = k.ring(3, [128, W], BF16, "ob")
        vt = k.ring(2, [128, 256], BF16, "vt")
        acc = k.ring(3, [128, W], F32, "acc", psum=True)
        stat = (k.ps([128, W]), Buf())
        hst = stat
        rot = k.ring(2, [128, W], F32, "rot", psum=True)
        vps = k.ring(2, [128, 256], F32, "vps", psum=True)
        P.dma("sp", rmat[:], T["rmat"].rearrange("a p m -> p a m"), [], [b_rmat], "ld_rmat")
        P.dma("sp", qkn[:], T["qkn"], [], [b_qkn], "ld_qkn")
        wv = win_d.rearrange("(kc p) n -> p kc n", p=128)
        k.wscr("win", 48, [KC, 256])
        for ti, (off, w, is_ctx) in enumerate(tiles_list()):
            first = ti == 0
            k.wflush("win")
            V = vec[(1, 1 if is_ctx else 0)]
            norm_tile(k, x_d, xkey, off, w, V, xy, b_xy, h, b_h, sq, tmpf, stat, rstd, b_rstd, rtmp, b_rtmp)
            if not is_ctx:
                P.dma("sp", rope[:, :, :w], T["rope"][:, :, off:off + w].rearrange("a p t -> p a t"), [], [b_rope], "ld_rope")
            for u in range(48):
                wt, b_w = wr.next()
                wi = (wr.i - 1) % 3
                k.wload("win", u, wt, b_w, wi, first, [(wt[:], wv[:, :, u * 256:(u + 1) * 256])])
                role0 = chunk_role(2 * u)
                if role0[0] == "v":
                    vbase = {"a": 0, "b": 2, "c": 4}[role0[1]] + role0[2]
                    for tb in range((w + 127) // 128):
                        m = min(128, w - tb * 128)
                        vp, b_vp = vps.next()
                        for kk in range(KC):
                            P.I("pe", "matmul", [b_w, b_h[kk]], [b_vp], vp[:m, :], lhsT=h[:, kk, tb * 128:tb * 128 + m], rhs=wt[:, kk, :],
                                start=(kk == 0), stop=(kk == KC - 1))
                        v_, b_v = vt.next()
                        vi = (vt.i - 1) % 2
                        P.I("act", "copy", [b_vp], [b_v], out=v_[:m, :], in_=vp[:m, :])
                        ti = (off // 128 + tb)
                        P.dma("sp", T["vimg"][vbase:vbase + 2, :m, ti, :].rearrange("c p d -> p c d"),
                              v_[:m, :].rearrange("p (c d) -> p c d", c=2), [b_v], [k.dbuf("vimg")], f"vt{vi}")
                    continue
                for cc in range(2):
                    c = 2 * u + cc
                    kind, br, idx = chunk_role(c)
                    a_, b_a = acc.next()
                    for kk in range(KC):
                        P.I("pe", "matmul", [b_w, b_h[kk]], [b_a], a_[:, :w], lhsT=wt[:, kk, cc * 128:(cc + 1) * 128], rhs=h[:, kk, :w],
                            start=(kk == 0), stop=(kk == KC - 1))
                    o_, b_o = ob.next()
                    oi = (ob.i - 1) % 3
                    if kind == "g":
                        P.I("act", "activation", [b_a], [b_o], out=o_[:, :w], in_=a_[:, :w], func=AF.Sigmoid)
                        P.dma("sp", T["gT"][idx * 128:(idx + 1) * 128, off:off + w], o_[:, :w], [b_o], [k.dbuf("gT")], f"ob{oi}")
                        continue
                    q_, b_q = qn.next()
                    if br == "a":
                        s_, b_s = sq.next()
                        P.I("act", "activation", [b_a], [b_s], out=s_[:, :w], in_=a_[:, :w], func=AF.Square)
                        P.I("pe", "matmul", [b_s, k.b_const], [hst[1]], hst[0][:, :w], lhsT=k.ones[:], rhs=s_[:, :w], start=True, stop=True)
                        emit_rstd(k, hst[0], hst[1], hrs, b_hrs, hrt, b_hrt, w, HD)
                        col = 0 if kind == "q" else 1
                        P.I("dve", "scalar_tensor_tensor", [b_a, b_hrs, b_qkn], [b_q], out=q_[:, :w], in0=a_[:, :w], scalar=qkn[:, col:col + 1],
                            in1=hrs[:, :w], op0=ALU.mult, op1=ALU.mult)
                    else:
                        P.I("act", "copy", [b_a], [b_q], out=q_[:, :w], in_=a_[:, :w])
                    if is_ctx:
                        P.I("dve", "tensor_copy", [b_q], [b_o], out=o_[:, :w], in_=q_[:, :w])
                    else:
                        ri = 1 if br == "c" else 0
                        r_, b_r = rot.next()
                        P.I("pe", "matmul", [b_q, b_rmat], [b_r], r_[:, :w], lhsT=rmat[:, ri, :], rhs=q_[:, :w], start=True, stop=True)
                        a1, b_1 = t1.next()
                        a2, b_2 = t2.next()
                        P.I("dve", "tensor_tensor", [b_q, b_rope], [b_1], out=a1[:, :w], in0=q_[:, :w], in1=rope[:, 2 * ri, :w], op=ALU.mult)
                        P.I("dve", "tensor_tensor", [b_r, b_rope], [b_2], out=a2[:, :w], in0=r_[:, :w], in1=rope[:, 2 * ri + 1, :w], op=ALU.mult)
                        P.I("dve", "tensor_tensor", [b_1, b_2], [b_o], out=o_[:, :w], in0=a1[:, :w], in1=a2[:, :w], op=ALU.add)
                    if kind == "q":
                        row = (QBASE[br] + idx) * 128
                        P.dma("sp", T["qT"][row:row + 128, off:off + w], o_[:, :w], [b_o], [k.dbuf("qT")], f"ob{oi}")
                    else:
                        row = (KBASE[br] + idx) * 128
                        P.dma("sp", T["kT"][row:row + 128, off:off + w], o_[:, :w], [b_o], [k.dbuf("kT")], f"ob{oi}")


def attn_stage(k, T):
    P = k.P
    NLT = TL // 128
    with k.phase():
        Kr = k.ring(2, [128, NR, TLC], BF16, "K")
        Vr = k.ring(2, [128, NR, NTI, 128], BF16, "V")
        qr = k.ring(3, [128, W], BF16, "q")
        pr = k.ring(6, [128, W], BF16, "p")
        pm = k.ring(2, [128, W], BF16, "pm")
        bm = k.sb([128, 6, W], BF16, "bm"); hm = k.sb([128, 2, W], BF16, "hm"); b_bm = Buf()
        obr = k.ring(2, [128, W], BF16, "ob")
        f1 = k.ring(2, [128, W], F32, "f1")
        f2 = k.ring(2, [128, W], F32, "f2")
        f3 = k.ring(2, [128, W], F32, "f3")
        sqb = k.ring(2, [128, W], BF16, "sqb")
        lacc = k.ring(4, [128, W], F32, "lacc")
        rstd = k.sb([128, W], F32, "rstd"); b_rstd = Buf()
        rtmp = k.sb([128, W], F32, "rtmp"); b_rtmp = Buf()
        kh = k.sb([128, 2, 2, 128], BF16, "kh"); vh = k.sb([128, 2, 2, 128], BF16, "vh"); b_halo = Buf()
        esink = k.sb([128, 8], F32, "esink"); b_sink = Buf()
        lam = k.sb([128, 4, 64], F32, "lam"); b_lam = Buf()
        lt = k.sb([128, 2, 64], F32, "lt"); ls = k.sb([128, 2], F32, "ls"); nlam = k.sb([128, 1], F32, "nlam")
        linit = k.sb([128, 1], F32, "linit"); subln = k.sb([128, 1], F32, "subln"); sl = k.sb([128, 1], F32, "sl")
        S = k.ring(4, [128, W], F32, "S", psum=True)
        ACC = k.ring(4, [128, W], F32, "ACC", psum=True)
        scale_ab = float(HD) ** -0.5
        scale_c = float(HD // 2) ** -0.5

        P.dma("sp", esink[:], T["sink"], [], [b_sink], "ld_sink")
        P.I("act", "activation", [b_sink], [b_sink], out=esink[:], in_=esink[:], func=AF.Exp)
        P.dma("sp", lam[:], T["lam"], [], [b_lam], "ld_lam")
        P.dma("sp", linit[:], T["linit"], [], [b_lam], "ld_linit")
        P.dma("sp", subln[:], T["subln"], [], [b_lam], "ld_subln")
        P.I("dve", "tensor_tensor", [b_lam], [b_lam], out=lt[:, 0, :], in0=lam[:, 0, :], in1=lam[:, 1, :], op=ALU.mult)
        P.I("dve", "tensor_tensor", [b_lam], [b_lam], out=lt[:, 1, :], in0=lam[:, 2, :], in1=lam[:, 3, :], op=ALU.mult)
        P.I("dve", "reduce_sum", [b_lam], [b_lam], out=ls[:], in_=lt[:], axis=AX.X)
        P.I("act", "activation", [b_lam], [b_lam], out=ls[:], in_=ls[:], func=AF.Exp)
        P.I("dve", "tensor_tensor", [b_lam], [b_lam], out=nlam[:], in0=ls[:, 1:2], in1=ls[:, 0:1], op=ALU.subtract)
        P.I("dve", "tensor_tensor", [b_lam], [b_lam], out=nlam[:], in0=nlam[:], in1=linit[:], op=ALU.subtract)
        P.I("dve", "tensor_scalar", [b_lam], [b_lam], out=sl[:], in0=linit[:], scalar1=-1.0, scalar2=1.0, op0=ALU.mult, op1=ALU.add)
        P.I("dve", "tensor_tensor", [b_lam], [b_lam], out=sl[:], in0=sl[:], in1=subln[:], op=ALU.mult)
        P.dma("sp", bm[:], T["bmask"].rearrange("m p t -> p m t"), [], [b_bm], "ld_bm")
        P.dma("sp", hm[:], T["hmask"].rearrange("m p t -> p m t"), [], [b_bm], "ld_hm")
        P.dma("sp", kh[:], T["khalo"].rearrange("g s p n -> p g s n"), [], [b_halo], "ld_kh")
        P.dma("sp", vh[:], T["vhalo"].rearrange("g s p n -> p g s n"), [], [b_halo], "ld_vh")

        def load_kv(kc, vc):
            Kt, b_K = Kr.next(); ki = (Kr.i - 1) % 2
            Vt, b_V = Vr.next(); vi = (Vr.i - 1) % 2
            for r in range(NR):
                P.dma("sp", Kt[:, r, :], T["kT_all"][r, kc * 128:(kc + 1) * 128, :], [k.dbuf("kT")], [b_K], f"K{ki}")
                P.dma("sp", Vt[:, r, :, :], T["v_all"][r, vc], [k.dbuf("vimg")], [b_V], f"V{vi}")
            return Kt, b_K, Vt, b_V

        def load_q(qc, off, w):
            q_, b_q = qr.next(); qi = (qr.i - 1) % 3
            P.dma("sp", q_[:, :w], T["qT"][qc * 128:(qc + 1) * 128, off:off + w], [k.dbuf("qT")], [b_q], f"q{qi}")
            return q_, b_q

        def store_o(oc, off, w, o_, b_o):
            oi = (obr.i - 1) % 2
            P.dma("sp", T["attnT"][oc * 128:(oc + 1) * 128, off:off + w], o_[:, :w], [b_o], [k.dbuf("attnT")], f"ao{oi}")

        def all_keys(is_ctx):
            ks = []
            if not is_ctx:
                for r in range(NR):
                    for t in range(NLT):
                        ks.append((r, t, 128))
            for r in range(NR):
                for ct in range(NCT):
                    ks.append((r, NLT + ct, min(128, CTXL - ct * 128)))
            return ks

        def softmax_loop(keys, w, scale, O, b_O, L, b_L):
            n = len(keys)
            las = [lacc.next(), lacc.next()]
            for la, b_la in las:
                P.I("pool", "memset", [], [b_la], la[:, :w], 0.0)
            LA = 3

            def issue_s(i):
                kd = keys[i]
                s_, b_s = S.next()
                P.I("pe", "matmul", kd["kbufs"] + [kd["q"][1]], [b_s], s_[:kd["nk"], :w], lhsT=kd["kT"], rhs=kd["q"][0], start=True, stop=True)
                return s_, b_s
            sq_ = [issue_s(i) for i in range(min(LA, n))]
            for i, kd in enumerate(keys):
                nk = kd["nk"]
                s_, b_s = sq_[i]
                p_, b_p = pr.next()
                P.I("act", "activation", [b_s], [b_p], out=p_[:nk, :w], in_=s_[:nk, :w], func=AF.Exp, scale=scale)
                if i + LA < n:
                    sq_.append(issue_s(i + LA))
                if kd.get("mask") is not None:
                    m_, b_m = kd["mask"]
                    p2, b_p2 = pm.next()
                    P.I("dve", "tensor_tensor", [b_p, b_m], [b_p2], out=p2[:nk, :w], in0=p_[:nk, :w], in1=m_[:nk, :w], op=ALU.mult)
                    p_, b_p = p2, b_p2
                P.I("pe", "matmul", kd["vbufs"] + [b_p], [b_O], O[:, :w], lhsT=kd["v"], rhs=p_[:nk, :w], start=(i == 0), stop=(i == n - 1))
                la, b_la = las[i % 2]
                P.I("dve", "tensor_tensor", [b_la, b_p], [b_la], out=la[:nk, :w], in0=la[:nk, :w], in1=p_[:nk, :w], op=ALU.add)
            for j, (la, b_la) in enumerate(las):
                P.I("pe", "matmul", [k.b_const, b_la], [b_L], L[:, :w], lhsT=k.ones32[:], rhs=la[:, :w], start=(j == 0), stop=(j == 1))

        for g in range(2):
            Kt, b_K, Vt, b_V = load_kv(g, g)
            for hh in range(4):
                hd = 4 * g + hh
                for (off, w, is_ctx) in tiles_list():
                    q_, b_q = load_q(QBASE["a"] + hd, off, w)
                    keys = []
                    for (r, t, nk) in all_keys(is_ctx):
                        keys.append(dict(kT=Kt[:, r, t * 128:t * 128 + nk], kbufs=[b_K], v=Vt[:nk, r, t, :], vbufs=[b_V], nk=nk,
                                         q=(q_[:, :w], b_q)))
                    O, b_O = ACC.next(); L, b_L = ACC.next()
                    softmax_loop(keys, w, scale_ab, O, b_O, L, b_L)
                    r_, b_r = f1.next()
                    P.I("dve", "reciprocal", [b_L], [b_r], out=r_[:, :w], in_=L[:, :w])
                    o_, b_o = obr.next()
                    P.I("dve", "tensor_tensor", [b_O, b_r], [b_o], out=o_[:, :w], in0=O[:, :w], in1=r_[:, :w], op=ALU.mult)
                    store_o(hd, off, w, o_, b_o)

        Kb = k.sb([128, TL], BF16, "Kb"); Vb = k.sb([128, NLT, 128], BF16, "Vb")
        Kbc = k.sb([128, NR, CTXL], BF16, "Kbc"); Vbc = k.sb([128, NR, NCT, 128], BF16, "Vbc")
        b_Kb = Buf(); b_Vb = Buf()
        for g in range(2):
            P.dma("sp", Kb[:], T["kT"][(2 + g) * 128:(3 + g) * 128, 0:TL], [k.dbuf("kT")], [b_Kb], "Kb")
            P.dma("sp", Vb[:], T["vimg"][2 + g, :, 0:NLT, :], [k.dbuf("vimg")], [b_Vb], "Vb")
            for r in range(NR):
                P.dma("sp", Kbc[:, r, :], T["kT_all"][r, (2 + g) * 128:(3 + g) * 128, TL:TLC], [k.dbuf("kT")], [b_Kb], "Kb")
                P.dma("sp", Vbc[:, r, :, :], T["v_all"][r, 2 + g, :, NLT:NLT + NCT, :], [k.dbuf("vimg")], [b_Vb], "Vb")
            for hh in range(4):
                hd = 4 * g + hh
                for ti, (off, w, is_ctx) in enumerate(tiles_list()):
                    q_, b_q = load_q(QBASE["b"] + hd, off, w)
                    qq = (q_[:, :w], b_q)
                    keys = []
                    if not is_ctx:
                        t0 = off // 128
                        for rel in range(-1, 5):
                            kt = t0 + rel
                            if kt < 0 or kt >= NLT:
                                continue
                            keys.append(dict(kT=Kb[:, kt * 128:(kt + 1) * 128], kbufs=[b_Kb], v=Vb[:, kt, :], vbufs=[b_Vb], nk=128,
                                             mask=(bm[:, rel + 1, :], b_bm), q=qq))
                        for side, cond in ((0, off == 0), (1, off + w == TL)):
                            if cond and NR > 1:
                                keys.append(dict(kT=kh[:, g, side, :], kbufs=[b_halo], v=vh[:, g, side, :], vbufs=[b_halo], nk=128,
                                                 mask=(hm[:, side, :], b_bm), q=qq))
                    for r in range(NR):
                        for ct in range(NCT):
                            nk = min(128, CTXL - ct * 128)
                            keys.append(dict(kT=Kbc[:, r, ct * 128:ct * 128 + nk], kbufs=[b_Kb], v=Vbc[:nk, r, ct, :], vbufs=[b_Vb], nk=nk, q=qq))
                    O, b_O = ACC.next(); L, b_L = ACC.next()
                    softmax_loop(keys, w, scale_ab, O, b_O, L, b_L)
                    r_, b_r = f1.next()
                    P.I("dve", "tensor_scalar", [b_L, b_sink], [b_r], out=r_[:, :w], in0=L[:, :w], scalar1=esink[:, hd:hd + 1], scalar2=None,
                        op0=ALU.add)
                    r2, b_r2 = f2.next()
                    P.I("dve", "reciprocal", [b_r], [b_r2], out=r2[:, :w], in_=r_[:, :w])
                    o_, b_o = obr.next()
                    P.I("dve", "tensor_tensor", [b_O, b_r2], [b_o], out=o_[:, :w], in0=O[:, :w], in1=r2[:, :w], op=ALU.mult)
                    store_o(8 + hd, off, w, o_, b_o)

        for hd in range(8):
            Kt, b_K, Vt, b_V = load_kv(4 + hd, 4 + hd)
            for (off, w, is_ctx) in tiles_list():
                q_, b_q = load_q(QBASE["c"] + hd, off, w)
                accs = [ACC.next() for _ in range(4)]
                for half in range(2):
                    lo, hi = half * 64, half * 64 + 64
                    keys = []
                    for (r, t, nk) in all_keys(is_ctx):
                        keys.append(dict(kT=Kt[lo:hi, r, t * 128:t * 128 + nk], kbufs=[b_K], v=Vt[:nk, r, t, :], vbufs=[b_V], nk=nk,
                                         q=(q_[lo:hi, :w], b_q)))
                    (O, b_O), (L, b_L) = accs[2 * half], accs[2 * half + 1]
                    softmax_loop(keys, w, scale_c, O, b_O, L, b_L)
                (O1, b_O1), (L1, b_L1), (O2, b_O2), (L2, b_L2) = accs
                r1, b_r1 = f1.next()
                P.I("dve", "reciprocal", [b_L1], [b_r1], out=r1[:, :w], in_=L1[:, :w])
                a1, b_a1 = f2.next()
                P.I("dve", "tensor_tensor", [b_O1, b_r1], [b_a1], out=a1[:, :w], in0=O1[:, :w], in1=r1[:, :w], op=ALU.mult)
                r2, b_r2 = f1.next()
                P.I("dve", "reciprocal", [b_L2], [b_r2], out=r2[:, :w], in_=L2[:, :w])
                a2, b_a2 = f2.next()
                P.I("dve", "tensor_tensor", [b_O2, b_r2], [b_a2], out=a2[:, :w], in0=O2[:, :w], in1=r2[:, :w], op=ALU.mult)
                o3, b_o3 = f3.next()
                P.I("dve", "scalar_tensor_tensor", [b_a2, b_a1, b_lam], [b_o3], out=o3[:, :w], in0=a2[:, :w], scalar=nlam[:, 0:1], in1=a1[:, :w],
                    op0=ALU.mult, op1=ALU.add)
                s_, b_s = sqb.next()
                P.I("act", "activation", [b_o3], [b_s], out=s_[:, :w], in_=o3[:, :w], func=AF.Square)
                st, b_st = S.next()
                P.I("pe", "matmul", [b_s, k.b_const], [b_st], st[:, :w], lhsT=k.ones[:], rhs=s_[:, :w], start=True, stop=True)
                emit_rstd(k, st, b_st, rstd, b_rstd, rtmp, b_rtmp, w, HD)
                o_, b_o = obr.next()
                P.I("dve", "scalar_tensor_tensor", [b_o3, b_rstd, b_lam], [b_o], out=o_[:, :w], in0=o3[:, :w], scalar=sl[:, 0:1], in1=rstd[:, :w],
                    op0=ALU.mult, op1=ALU.mult)
                store_o(16 + hd, off, w, o_, b_o)


def merge_stage(k, xin_d, xin_key, xout_d, xout_key, wbr_d, wout_d, vec, T):
    P = k.P
    with k.phase():
        at = k.sb([128, 24, W], BF16, "at"); b_at = Buf()
        gt = k.sb([128, 48, W], BF16, "gt"); b_gt = Buf()
        ym = k.sb([128, KC, W], BF16, "ym"); b_ym = [Buf() for _ in range(KC)]
        xy = k.sb([128, KC, W], F32, "xy"); b_xy = [Buf() for _ in range(KC)]
        wbr = k.ring(2, [128, 3, 8, 256], BF16, "wbr")
        wor = k.ring(2, [128, KC, 256], BF16, "wor")
        sq = k.ring(2, [128, W], BF16, "sq")
        tmpf = k.ring(2, [128, W], F32, "tmpf")
        m1 = k.ring(2, [128, W], F32, "m1")
        m2 = k.ring(2, [128, W], F32, "m2")
        rstd = k.sb([128, W], F32, "rstd"); b_rstd = Buf()
        rtmp = k.sb([128, W], F32, "rtmp"); b_rtmp = Buf()
        xst = k.ring(3, [128, W], F32, "xst")
        xo = k.ring(2, [128, W], F32, "xo")
        bp = k.ring(6, [128, W], F32, "bp", psum=True)
        stat = (k.ps([128, W]), Buf())
        wbv = wbr_d.rearrange("r (kc p) n -> p r kc n", p=128)
        wov = wout_d.rearrange("(kc p) n -> p kc n", p=128)
        k.wscr("wbr", KC // 2, [3, 8, 256]); k.wscr("wor", KC // 2, [KC, 256])
        for ti, (off, w, is_ctx) in enumerate(tiles_list()):
            first = ti == 0
            V = vec[(1, 1 if is_ctx else 0)]
            P.dma("sp", at[:, :, :w], T["attnT"][:, off:off + w].rearrange("(c p) t -> p c t", p=128), [k.dbuf("attnT")], [b_at], "ld_at")
            P.dma("sp", gt[:, :, :w], T["gT"][:, off:off + w].rearrange("(c p) t -> p c t", p=128), [k.dbuf("gT")], [b_gt], "ld_gt")
            for npair in range(KC // 2):
                wt, b_w = wbr.next(); wi = (wbr.i - 1) % 2
                k.wload("wbr", npair, wt, b_w, wi, first, [(wt[:, r, :, :], wbv[:, r, :, npair * 256:(npair + 1) * 256]) for r in range(3)])
                for nn in range(2):
                    n = npair * 2 + nn
                    pss = [bp.next() for _ in range(3)]
                    for r in range(3):
                        p_, b_p = pss[r]
                        for kk in range(8):
                            P.I("pe", "matmul", [b_w, b_at], [b_p], p_[:, :w], lhsT=wt[:, r, kk, nn * 128:(nn + 1) * 128], rhs=at[:, r * 8 + kk, :w],
                                start=(kk == 0), stop=(kk == 7))
                    a_, b_a = m1.next()
                    P.I("dve", "tensor_tensor", [pss[0][1], b_gt], [b_a], out=a_[:, :w], in0=pss[0][0][:, :w], in1=gt[:, n, :w], op=ALU.mult)
                    c_, b_c = m2.next()
                    P.I("dve", "tensor_tensor", [pss[1][1], b_gt], [b_c], out=c_[:, :w], in0=pss[1][0][:, :w], in1=gt[:, 16 + n, :w], op=ALU.mult)
                    P.I("dve", "tensor_tensor", [b_a, b_c], [b_a], out=a_[:, :w], in0=a_[:, :w], in1=c_[:, :w], op=ALU.add)
                    c2, b_c2 = m2.next()
                    P.I("dve", "tensor_tensor", [pss[2][1], b_gt], [b_c2], out=c2[:, :w], in0=pss[2][0][:, :w], in1=gt[:, 32 + n, :w], op=ALU.mult)
                    P.I("dve", "tensor_tensor", [b_a, b_c2], [b_ym[n]], out=ym[:, n, :w], in0=a_[:, :w], in1=c2[:, :w], op=ALU.add)
            k.wflush("wbr")
            st, b_st = stat
            for npair in range(KC // 2):
                wt, b_w = wor.next(); wi = (wor.i - 1) % 2
                k.wload("wor", npair, wt, b_w, wi, first, [(wt[:], wov[:, :, npair * 256:(npair + 1) * 256])])
                for nn in range(2):
                    n = npair * 2 + nn
                    y_, b_y = bp.next()
                    for kk in range(KC):
                        P.I("pe", "matmul", [b_w, b_ym[kk]], [b_y], y_[:, :w], lhsT=wt[:, kk, nn * 128:(nn + 1) * 128], rhs=ym[:, kk, :w],
                            start=(kk == 0), stop=(kk == KC - 1))
                    P.I("dve", "tensor_copy", [b_y], [b_xy[n]], out=xy[:, n, :w], in_=y_[:, :w])
                    q_, b_q = sq.next()
                    P.I("act", "activation", [b_xy[n]], [b_q], out=q_[:, :w], in_=xy[:, n, :w], func=AF.Square)
                    P.I("pe", "matmul", [b_q, k.b_const], [b_st], st[:, :w], lhsT=k.ones[:], rhs=q_[:, :w], start=(n == 0), stop=(n == KC - 1))
            k.wflush("wor")
            emit_rstd(k, st, b_st, rstd, b_rstd, rtmp, b_rtmp, w, D)
            residual_update(k, xin_d, xin_key, xout_d, xout_key, off, w, V, xy, b_xy, rstd, b_rstd, tmpf, xst, xo)


def new_k():
    nc = bass.Bass("TRN2", target_bir_lowering=False)
    es = contextlib.ExitStack()
    k = K(nc, es)
    k.es_cur = es
    return k


def mod_stage(k, cT_d, wada_d, bada_d, mo, b_mo):
    P = k.P
    NCH = 9 * KC
    with k.phase():
        cs = k.sb([128, KC, 2], F32, "cs"); b_cs = Buf()
        bs = k.sb([128, DEPTH, NCH], F32, "bs"); b_bs = Buf()
        wr = k.ring(4, [128, KC, 256], F32, "wr")
        pp = k.ring(4, [128, 2], F32, "pp", psum=True)
        P.dma("sp", cs[:], cT_d, [], [b_cs], "ld_c")
        P.dma("sp", bs[:], bada_d, [], [b_bs], "ld_b")
        P.I("act", "activation", [b_cs], [b_cs], out=cs[:], in_=cs[:], func=AF.Silu)
        for l in range(DEPTH):
            wv = wada_d[l].rearrange("(kc p) n -> p kc n", p=128)
            for u in range(NCH // 2):
                wt, b_w = wr.next(); wi = (wr.i - 1) % 4
                P.dma("sp", wt[:], wv[:, :, u * 256:(u + 1) * 256], [], [b_w], f"wa{wi}")
                for cc in range(2):
                    c = 2 * u + cc
                    p_, b_p = pp.next()
                    for kk in range(KC):
                        P.I("pe", "matmul", [b_w, b_cs], [b_p], p_[:, 0:2], lhsT=wt[:, kk, cc * 128:(cc + 1) * 128], rhs=cs[:, kk, :],
                            start=(kk == 0), stop=(kk == KC - 1))
                    P.I("dve", "tensor_scalar", [b_p, b_bs], [b_mo], out=mo[:, l, c, :], in0=p_[:, 0:2], scalar1=bs[:, l, c:c + 1], scalar2=None,
                        op0=ALU.add)


def layer_vecs(k, l, mo, b_mo, npre, npost, b_np, gs, hg):
    P = k.P
    vec = {}
    for v in range(2):
        for s in range(3):
            def m(i):
                return mo[:, l, i * KC:(i + 1) * KC, v]
            P.I("dve", "scalar_tensor_tensor", [b_mo, b_np], [k.b_vec], out=gs[:, v, s, :], in0=m(3 * s + 1), scalar=1.0,
                in1=npre[:, l, s, :], op0=ALU.add, op1=ALU.mult)
            P.I("dve", "scalar_tensor_tensor", [b_mo, b_np], [k.b_vec], out=hg[:, v, s, :], in0=m(3 * s + 2),
                scalar=(1.0 if s == 1 else 0.5), in1=npost[:, l, s, :], op0=ALU.mult, op1=ALU.mult)
            vec[(s, v)] = dict(gs=gs[:, v, s, :], sh=m(3 * s), hg=hg[:, v, s, :])
    return vec


def build_fused():
    k = new_k(); nc = k.nc; P = k.P
    xin = k.dram_in("xT", [D, TLC], F32)
    cT = k.dram_in("cT", [128, KC, 2], F32)
    wada = k.dram_in("wada", [DEPTH, D, 9 * D], F32)
    bada = k.dram_in("bada", [128, DEPTH, 9 * KC], F32)
    npre_d = k.dram_in("npre", [128, DEPTH, 3, KC], F32)
    npost_d = k.dram_in("npost", [128, DEPTH, 3, KC], F32)
    wffin = k.dram_in("wffin", [DEPTH, 2, D, 2 * DFF], F32)
    wffout = k.dram_in("wffout", [DEPTH, 2, DFF, D], F32)
    win = k.dram_in("win", [DEPTH, D, INW], F32)
    wbr = k.dram_in("wbr", [DEPTH, 3, 1024, D], F32)
    wout = k.dram_in("wout", [DEPTH, D, D], F32)
    qkn = k.dram_in("qkn", [DEPTH, 128, 2], F32)
    sink = k.dram_in("sink", [DEPTH, 128, 8], F32)
    lam = k.dram_in("lam", [DEPTH, 128, 4, 64], F32)
    linit = k.dram_in("linit", [DEPTH, 128, 1], F32)
    subln = k.dram_in("subln", [DEPTH, 128, 1], F32)
    rope = k.dram_in("rope", [4, 128, TL], F32)
    rmat = k.dram_in("rmat", [2, 128, 128], F32)
    bmask = k.dram_in("bmask", [6, 128, W], BF16)
    hmask = k.dram_in("hmask", [2, 128, W], BF16)
    khalo = k.dram_in("khalo", [2, 2, 128, 128], BF16)
    vhalo = k.dram_in("vhalo", [2, 2, 128, 128], BF16)
    xout = k.dram_out("xout", [D, TLC], F32)

    def scratch(name, shape, dt):
        return nc.dram_tensor(name, list(shape), dt, kind="Internal").ap()
    xs = [scratch("xs0", [D, TLC], F32), scratch("xs1", [D, TLC], F32)]
    kT = scratch("kT", [NR, NKC * 128, TLC], BF16)
    vimg = scratch("vimg", [NR, NVC, 128, NTI, 128], BF16)
    Tb = dict(rope=rope, rmat=rmat, bmask=bmask, hmask=hmask, khalo=khalo, vhalo=vhalo,
              qT=scratch("qT", [NQC * 128, TLC], BF16), gT=scratch("gT", [48 * 128, TLC], BF16),
              attnT=scratch("attnT", [24 * 128, TLC], BF16), kT=kT[0], vimg=vimg[0], kT_all=kT, v_all=vimg)

    setup_consts(k)
    mo = k.sb([128, DEPTH, 9 * KC, 2], F32, "mo"); b_mo = Buf("mo")
    npre = k.sb([128, DEPTH, 3, KC], F32, "npre"); npost = k.sb([128, DEPTH, 3, KC], F32, "npost"); b_np = Buf("np")
    gs = k.sb([128, 2, 3, KC], F32, "gs"); hg = k.sb([128, 2, 3, KC], F32, "hg")
    k.b_vec = Buf("vec"); k.b_vec_in = b_mo
    P.dma("sp", npre[:], npre_d, [], [b_np], "ld_npre")
    P.dma("sp", npost[:], npost_d, [], [b_np], "ld_npost")
    mod_stage(k, cT, wada, bada, mo, b_mo)

    seq = []
    nsub = 3 * DEPTH
    for i in range(nsub + 1):
        if i == 0:
            seq.append((xin, "xin"))
        elif i == nsub:
            seq.append((xout, "xout"))
        else:
            seq.append((xs[(i - 1) % 2], f"xs{(i - 1) % 2}"))
    si = 0
    for l in range(DEPTH):
        vec = layer_vecs(k, l, mo, b_mo, npre, npost, b_np, gs, hg)
        T = dict(Tb)
        T.update(qkn=qkn[l], sink=sink[l], lam=lam[l], linit=linit[l], subln=subln[l])
        (a, ak), (b, bk) = seq[si], seq[si + 1]
        ffn_sublayer(k, a, ak, b, bk, wffin[l, 0], wffout[l, 0], vec, 0)
        si += 1
        inproj_stage(k, b, bk, win[l], vec, T)
        attn_stage(k, T)
        (a, ak), (b, bk) = seq[si], seq[si + 1]
        merge_stage(k, a, ak, b, bk, wbr[l], wout[l], vec, T)
        si += 1
        (a, ak), (b, bk) = seq[si], seq[si + 1]
        ffn_sublayer(k, a, ak, b, bk, wffin[l, 1], wffout[l, 1], vec, 2)
        si += 1
    st = k.P.finish()
    k.es.close()
    k.stats = st
    return nc


def fm(v):
    v = np.asarray(v)
    lead = v.shape[:-1]
    return np.ascontiguousarray(np.moveaxis(v.reshape(*lead, KC, 128), -1, 0))


def rope_consts(tok0):
    def tables(rot_dim):
        pos = tok0 + np.arange(TL)
        row = (pos // GRID_W).astype(np.float32)
        col = (pos % GRID_W).astype(np.float32)
        d_ax = rot_dim // 2
        inv = (10000.0 ** (-np.arange(0, d_ax, 2, dtype=np.float32) / d_ax)).astype(np.float32)
        ang_r = row[:, None] * inv[None, :]
        ang_c = col[:, None] * inv[None, :]
        ang = np.concatenate([ang_r, ang_r, ang_c, ang_c], axis=-1).astype(np.float32)
        return np.cos(ang).T.astype(np.float32), np.sin(ang).T.astype(np.float32)
    cab, sab = tables(128)
    cc, sc = tables(64)
    return np.ascontiguousarray(np.stack([cab, sab, np.concatenate([cc, cc], 0), np.concatenate([sc, sc], 0)], 0))


def rot_mats():
    def rt(dim, n):
        q = dim // 4
        m = np.zeros((n, n), np.float32)
        for base in range(0, n, dim):
            for i in range(dim):
                quarter = i // q
                if quarter == 0: src, sgn = i + q, -1.0
                elif quarter == 1: src, sgn = i - q, 1.0
                elif quarter == 2: src, sgn = i + q, -1.0
                else: src, sgn = i - q, 1.0
                m[base + src, base + i] = sgn
        return m
    return np.stack([rt(128, 128), rt(64, 128)], 0)


def band_masks():
    j = np.arange(128)[:, None]
    i = np.arange(W)[None, :]
    ms = []
    for rel in range(-1, 5):
        ms.append((np.abs(rel * 128 + j - i) <= 128))
    return np.stack(ms, 0)


_CACHE = {}


def get_prog(name, builder):
    key = (name, SEQ, DEPTH)
    if key not in _CACHE:
        _CACHE[key] = builder()
    return _CACHE[key]


def kernel(x, c, ctx, c_ctx, w_ada, b_ada, norm_pre, norm_post, w_ff_in, w_ff_out, w_in,
           qk_norm_a, sink_b, lam_c, subln_c, w_branch, w_out):
    f32 = np.float32
    A = lambda v: np.ascontiguousarray(np.asarray(v, f32))
    x = A(x); ctx = A(ctx); c = A(c); c_ctx = A(c_ctx)
    depth = np.asarray(w_ada).shape[0]
    assert depth == DEPTH and x.shape[1] == SEQ
    nb = x.shape[0]
    cores = list(range(nb))
    nc = get_prog("fused", build_fused)
    bc = lambda v, shape: np.ascontiguousarray(np.broadcast_to(v, shape))
    linit = np.array([0.8 - 0.6 * float(np.exp(-0.3 * l)) for l in range(depth)], f32)
    shared = dict(
        wada=A(w_ada), bada=np.ascontiguousarray(A(b_ada).reshape(depth, 9 * KC, 128).transpose(2, 0, 1)),
        npre=fm(A(norm_pre)), npost=fm(A(norm_post)),
        wffin=A(w_ff_in), wffout=A(w_ff_out), win=A(w_in), wbr=A(w_branch), wout=A(w_out),
        qkn=np.ascontiguousarray(A(qk_norm_a).transpose(0, 2, 1)),
        sink=bc(A(sink_b)[:, None, :], (depth, 128, 8)), lam=bc(A(lam_c)[:, None], (depth, 128, 4, 64)),
        linit=bc(linit[:, None, None], (depth, 128, 1)), subln=np.ascontiguousarray(A(subln_c).reshape(depth, 128, 1)),
        rope=rope_consts(0), rmat=rot_mats(), bmask=band_masks().astype(NPBF),
        hmask=np.zeros((2, 128, W), NPBF), khalo=np.zeros((2, 2, 128, 128), NPBF), vhalo=np.zeros((2, 2, 128, 128), NPBF))
    in_maps = []
    for b in cores:
        m = dict(shared)
        m["xT"] = np.ascontiguousarray(np.concatenate([x[b], ctx[b]], 0).T)
        m["cT"] = np.ascontiguousarray(np.stack([c[b], c_ctx], 0).reshape(2, KC, 128).transpose(2, 1, 0))
        in_maps.append(m)
    res = run_bass_kernel_spmd(nc, in_maps, core_ids=cores).results
    out = np.zeros((nb, SEQ, D), f32)
    for b in cores:
        out[b] = np.asarray(res[b]["xout"])[:, :TL].T
    return out
```

```python
import contextlib
import numpy as np
import ml_dtypes
import concourse.bass as bass
import concourse.mybir as mybir
from concourse.bass_utils import run_bass_kernel_spmd

F32 = mybir.dt.float32
BF16 = mybir.dt.bfloat16
AF = mybir.ActivationFunctionType
ALU = mybir.AluOpType
AX = mybir.AxisListType
NPBF = ml_dtypes.bfloat16

D = 2048
KC = 16
DFF = 5632
JC = 44
EPS = 1e-6
DEPTH = 4
NCORE = 2
NR = 1
SEQ = 8192
TL = SEQ // NR
CTX = 256
CTXL = CTX // NR
NCT = (CTXL + 127) // 128
TLC = TL + CTXL
GRID_W = 64
HD = 128
INW = 12288
W = 512
NQC = 24
NKC = 12
NVC = 12
NTI = TL // 128 + NCT
DEBUG = False


def set_scale(seq):
    global SEQ, TL, TLC, NTI
    SEQ = seq
    TL = SEQ // NR
    TLC = TL + CTXL
    NTI = TL // 128 + NCT


class Buf:
    __slots__ = ("name", "last_w", "readers")

    def __init__(self, name=""):
        self.name = name
        self.last_w = None
        self.readers = {}


class Prog:
    ENGINES = ("pe", "act", "dve", "pool", "sp")

    def __init__(self, nc, es):
        self.nc = nc
        self.es = es
        self.ops = []
        self.emap = {"pe": nc.tensor, "act": nc.scalar, "dve": nc.vector, "pool": nc.gpsimd, "sp": nc.sync}
        self.base = 0
        self.chan = []
        self.val = []
        self.waited = {}
        self.last_on_chan = {}
        self.count = {}
        self.sems = {}
        self.n_ops = 0

    def I(self, engine, method, reads, writes, *args, **kw):
        f = getattr(self.emap[engine], method)
        self.ops.append((engine, (lambda: f(*args, **kw)), tuple(reads), tuple(writes), None))

    def dma(self, engine, out, in_, reads, writes, stream):
        f = self.emap[engine].dma_start
        self.ops.append((engine, (lambda: f(out=out, in_=in_)), tuple(reads), tuple(writes), stream))

    def barrier(self):
        self.ops.append(("barrier", None, (), (), None))

    def flush(self):
        nc = self.nc
        ops = self.ops
        n = len(ops)
        base = self.base
        chan = self.chan
        val = self.val
        waited = self.waited
        last_on_chan = self.last_on_chan
        for o in ops:
            chan.append(o[4] if o[4] is not None else o[0])
            val.append(0)
        waits = [None] * n
        signal = [False] * n
        for li, (eng, emit, r, w, stream) in enumerate(ops):
            i = base + li
            if eng == "barrier":
                wl = {}
                for c, j in last_on_chan.items():
                    need = False
                    for e in self.ENGINES:
                        if waited.get((e, c), -1) < j:
                            waited[(e, c)] = j
                            need = True
                    if need:
                        wl[c] = j
                        if j >= base:
                            signal[j - base] = True
                waits[li] = wl
                continue
            deps = {}
            for b in r:
                j = b.last_w
                if j is not None:
                    c = chan[j]
                    if deps.get(c, -1) < j:
                        deps[c] = j
            for b in w:
                j = b.last_w
                if j is not None:
                    c = chan[j]
                    if deps.get(c, -1) < j:
                        deps[c] = j
                for c, j in b.readers.items():
                    if deps.get(c, -1) < j:
                        deps[c] = j
            my = chan[i]
            wl = None
            for c, j in deps.items():
                if c == my and (c == "pe" or stream is not None):
                    continue
                if waited.get((eng, c), -1) >= j:
                    continue
                waited[(eng, c)] = j
                assert j >= base, "dependency on an un-signalled op from a previous phase"
                signal[j - base] = True
                if wl is None:
                    wl = {}
                wl[c] = j
            waits[li] = wl
            for b in r:
                b.readers[my] = i
            for b in w:
                b.last_w = i
                b.readers = {}
            last_on_chan[my] = i
        count = self.count
        for li, (eng, emit, r, w, stream) in enumerate(ops):
            if eng == "barrier":
                continue
            i = base + li
            c = chan[i]
            if stream is not None:
                count[c] = count.get(c, 0) + 16
                val[i] = count[c]
            elif signal[li]:
                count[c] = count.get(c, 0) + 1
                val[i] = count[c]
        sems = self.sems
        for c in count:
            if c not in sems:
                sems[c] = self.es.enter_context(nc.semaphore("s_" + c))
        for li, (eng, emit, r, w, stream) in enumerate(ops):
            i = base + li
            if eng == "barrier":
                for e in self.ENGINES:
                    ee = self.emap[e]
                    for c, j in waits[li].items():
                        ee.wait_ge(sems[c], val[j])
                continue
            e = self.emap[eng]
            if waits[li]:
                for c, j in waits[li].items():
                    e.wait_ge(sems[c], val[j])
            ins = emit()
            if stream is not None:
                ins.then_inc(sems[chan[i]], 16)
            elif signal[li]:
                ins.then_inc(sems[chan[i]], 1)
        self.base += n
        self.n_ops += n
        self.ops = []

    def finish(self):
        self.barrier()
        self.flush()
        for c, v in self.count.items():
            self.nc.sync.wait_ge(self.sems[c], v)
        return dict(n_ops=self.n_ops, n_sems=len(self.sems))


class Ring:
    def __init__(self, items):
        self.items = items
        self.i = 0

    def next(self):
        it = self.items[self.i % len(self.items)]
        self.i += 1
        return it


class K:
    def __init__(self, nc, es):
        self.nc = nc
        self.es = es
        self.P = Prog(nc, es)
        self.dbufs = {}
        self.uid = 0

    def sb(self, shape, dt, name=None):
        self.uid += 1
        return self.es_cur.enter_context(self.nc.sbuf_tensor(f"{name or 't'}_{self.uid}", list(shape), dt))

    def ps(self, shape, dt=F32, name=None):
        self.uid += 1
        return self.es_cur.enter_context(self.nc.psum_tensor(f"{name or 'p'}_{self.uid}", list(shape), dt))

    def ring(self, n, shape, dt, name=None, psum=False):
        f = self.ps if psum else self.sb
        return Ring([(f(shape, dt, name), Buf(name or "")) for _ in range(n)])

    def dbuf(self, key):
        b = self.dbufs.get(key)
        if b is None:
            b = self.dbufs[key] = Buf(str(key))
        return b

    def wscr(self, name, n_units, unit_shape):
        d = self.__dict__.setdefault("_wscr", {})
        if name not in d:
            d[name] = self.nc.dram_tensor("wscr_" + name, [n_units, 128] + list(unit_shape), BF16, kind="Internal").ap()
        return d[name]

    def wload(self, name, u, slot, b_slot, si, first, cast_parts):
        P = self.P
        scr = self._wscr[name]
        wb = self.dbuf(("wscr", name, u))
        if first:
            for dst, src in cast_parts:
                P.dma("pool", dst, src, [], [b_slot], f"{name}_{si}")
            pend = self.__dict__.setdefault("_wpend", {})
            prev = pend.get(name)
            if prev is not None:
                prev()
            pend[name] = lambda: P.dma("pool", scr[u], slot[:], [b_slot], [wb], f"{name}s_{si}")
        else:
            self.wflush(name)
            P.dma("pool", slot[:], scr[u], [wb], [b_slot], f"{name}_{si}")

    def wflush(self, name):
        pend = self.__dict__.setdefault("_wpend", {})
        prev = pend.pop(name, None)
        if prev is not None:
            prev()

    def dram_in(self, name, shape, dt):
        return self.nc.dram_tensor(name, list(shape), dt, kind="ExternalInput").ap()

    def dram_out(self, name, shape, dt):
        return self.nc.dram_tensor(name, list(shape), dt, kind="ExternalOutput").ap()

    @contextlib.contextmanager
    def phase(self):
        with contextlib.ExitStack() as pes:
            old = getattr(self, "es_cur", None)
            self.es_cur = pes
            yield
            self.P.barrier()
            self.P.flush()
            self.es_cur = old


def tiles_list():
    return [(t * W, W, False) for t in range(TL // W)] + [(TL, CTXL, True)]


def emit_rstd(k, stat_ps, b_stat, rstd, b_rstd, tmp, b_tmp, w, nfeat):
    P = k.P
    P.I("dve", "tensor_scalar", [b_stat], [b_tmp], out=tmp[:, :w], in0=stat_ps[:, :w], scalar1=1.0 / nfeat, scalar2=EPS,
        op0=ALU.mult, op1=ALU.add)
    P.I("act", "sqrt", [b_tmp], [b_tmp], out=tmp[:, :w], in_=tmp[:, :w])
    P.I("dve", "reciprocal", [b_tmp], [b_rstd], out=rstd[:, :w], in_=tmp[:, :w])


def setup_consts(k):
    P = k.P
    k.ones = k.sb([128, 128], BF16, "ones")
    k.b_const = Buf("const")
    P.I("dve", "memset", [], [k.b_const], k.ones[:], 1.0)
    k.ones32 = k.sb([128, 128], F32, "ones32")
    P.I("dve", "memset", [], [k.b_const], k.ones32[:], 1.0)


def setup_vecs(k, modv_d, npre_d, npost_d):
    P = k.P
    modv = k.sb([128, 2, 9, KC], F32, "modv")
    npre = k.sb([128, 3, KC], F32, "npre")
    npost = k.sb([128, 3, KC], F32, "npost")
    gs = k.sb([128, 2, 3, KC], F32, "gs")
    hg = k.sb([128, 2, 3, KC], F32, "hg")
    b_in = Buf("vec_in")
    k.b_vec = Buf("vec")
    P.dma("sp", modv[:], modv_d, [], [b_in], "ld_modv")
    P.dma("sp", npre[:], npre_d, [], [b_in], "ld_npre")
    P.dma("sp", npost[:], npost_d, [], [b_in], "ld_npost")
    vec = {}
    for v in range(2):
        for s in range(3):
            P.I("dve", "scalar_tensor_tensor", [b_in], [k.b_vec], out=gs[:, v, s, :], in0=modv[:, v, 3 * s + 1, :], scalar=1.0,
                in1=npre[:, s, :], op0=ALU.add, op1=ALU.mult)
            P.I("dve", "scalar_tensor_tensor", [b_in], [k.b_vec], out=hg[:, v, s, :], in0=modv[:, v, 3 * s + 2, :],
                scalar=(1.0 if s == 1 else 0.5), in1=npost[:, s, :], op0=ALU.mult, op1=ALU.mult)
            vec[(s, v)] = dict(gs=gs[:, v, s, :], sh=modv[:, v, 3 * s, :], hg=hg[:, v, s, :])
    k.b_vec_in = b_in
    return vec


def norm_tile(k, xT_d, xkey, off, w, V, xy, b_xy, h, b_h, sq, tmpf, stat, rstd, b_rstd, rtmp, b_rtmp):
    P = k.P
    P.dma("sp", xy[:, :, :w], xT_d[:, off:off + w].rearrange("(kc p) t -> p kc t", p=128), [k.dbuf((xkey, off))], b_xy, "ld_xy")
    st, b_st = stat
    for kk in range(KC):
        s_, b_s = sq.next()
        P.I("act", "activation", [b_xy[kk]], [b_s], out=s_[:, :w], in_=xy[:, kk, :w], func=AF.Square)
        P.I("pe", "matmul", [b_s, k.b_const], [b_st], st[:, :w], lhsT=k.ones[:], rhs=s_[:, :w], start=(kk == 0), stop=(kk == KC - 1))
    emit_rstd(k, st, b_st, rstd, b_rstd, rtmp, b_rtmp, w, D)
    for kk in range(KC):
        t_, b_t = tmpf.next()
        P.I("dve", "tensor_tensor", [b_xy[kk], b_rstd], [b_t], out=t_[:, :w], in0=xy[:, kk, :w], in1=rstd[:, :w], op=ALU.mult)
        P.I("act", "activation", [b_t, k.b_vec, k.b_vec_in], [b_h[kk]], out=h[:, kk, :w], in_=t_[:, :w], func=AF.Identity,
            bias=V["sh"][:, kk:kk + 1], scale=V["gs"][:, kk:kk + 1])


def residual_update(k, xin_d, xin_key, xout_d, xout_key, off, w, V, xy, b_xy, rstd, b_rstd, tmpf, xst, xo):
    P = k.P
    lds = []

    def ld(n):
        x_, b_x = xst.next()
        P.dma("sp", x_[:, :w], xin_d[n * 128:(n + 1) * 128, off:off + w], [k.dbuf((xin_key, off))], [b_x], f"xst{(xst.i - 1) % len(xst.items)}")
        lds.append((x_, b_x))
    ld(0)
    ld(1)
    for n in range(KC):
        x_, b_x = lds[n]
        t_, b_t = tmpf.next()
        P.I("dve", "tensor_tensor", [b_xy[n], b_rstd], [b_t], out=t_[:, :w], in0=xy[:, n, :w], in1=rstd[:, :w], op=ALU.mult)
        o_, b_o = xo.next()
        oi = (xo.i - 1) % len(xo.items)
        P.I("dve", "scalar_tensor_tensor", [b_t, b_x, k.b_vec], [b_o], out=o_[:, :w], in0=t_[:, :w], scalar=V["hg"][:, n:n + 1],
            in1=x_[:, :w], op0=ALU.mult, op1=ALU.add)
        if n + 2 < KC:
            ld(n + 2)
        P.dma("sp", xout_d[n * 128:(n + 1) * 128, off:off + w], o_[:, :w], [b_o], [k.dbuf((xout_key, off))], f"xo{oi}")


def ffn_sublayer(k, xin_d, xin_key, xout_d, xout_key, w1_d, w2_d, vec, s):
    P = k.P
    with k.phase():
        xy = k.sb([128, KC, W], F32, "xy"); b_xy = [Buf() for _ in range(KC)]
        h = k.sb([128, KC, W], BF16, "h"); b_h = [Buf() for _ in range(KC)]
        G = k.sb([128, JC, W], BF16, "G"); b_G = [Buf() for _ in range(JC)]
        w1r = k.ring(2, [128, KC, 2, 256], BF16, "w1r")
        w2r = k.ring(2, [128, JC, 256], BF16, "w2r")
        sq = k.ring(2, [128, W], BF16, "sq")
        tmpf = k.ring(2, [128, W], F32, "tmpf")
        sg = k.ring(2, [128, W], F32, "sg")
        rstd = k.sb([128, W], F32, "rstd"); b_rstd = Buf()
        rtmp = k.sb([128, W], F32, "rtmp"); b_rtmp = Buf()
        xst = k.ring(3, [128, W], F32, "xst")
        xo = k.ring(2, [128, W], F32, "xo")
        gp = Ring([(k.ps([128, W]), Buf(), k.ps([128, W]), Buf()) for _ in range(2)])
        yp = k.ring(3, [128, W], F32, "yp", psum=True)
        stat = (k.ps([128, W]), Buf())
        w1v = w1_d.rearrange("(kc p) n -> p kc n", p=128)
        w2v = w2_d.rearrange("(jc p) n -> p jc n", p=128)
        k.wscr("w1", JC // 2, [KC, 2, 256]); k.wscr("w2", KC // 2, [JC, 256])
        for ti, (off, w, is_ctx) in enumerate(tiles_list()):
            first = ti == 0
            V = vec[(s, 1 if is_ctx else 0)]
            norm_tile(k, xin_d, xin_key, off, w, V, xy, b_xy, h, b_h, sq, tmpf, stat, rstd, b_rstd, rtmp, b_rtmp)
            for jp in range(JC // 2):
                wt, b_w = w1r.next()
                si = (w1r.i - 1) % 2
                k.wload("w1", jp, wt, b_w, si, first, [(wt[:, :, 0, :], w1v[:, :, jp * 256:(jp + 1) * 256]),
                                                        (wt[:, :, 1, :], w1v[:, :, DFF + jp * 256:DFF + (jp + 1) * 256])])
                for jj in range(2):
                    j = jp * 2 + jj
                    pg, b_pg, pu, b_pu = gp.next()
                    for kk in range(KC):
                        P.I("pe", "matmul", [b_w, b_h[kk]], [b_pg], pg[:, :w], lhsT=wt[:, kk, 0, jj * 128:(jj + 1) * 128],
                            rhs=h[:, kk, :w], start=(kk == 0), stop=(kk == KC - 1))
                    for kk in range(KC):
                        P.I("pe", "matmul", [b_w, b_h[kk]], [b_pu], pu[:, :w], lhsT=wt[:, kk, 1, jj * 128:(jj + 1) * 128],
                            rhs=h[:, kk, :w], start=(kk == 0), stop=(kk == KC - 1))
                    s_, b_s = sg.next()
                    P.I("act", "activation", [b_pg], [b_s], out=s_[:, :w], in_=pg[:, :w], func=AF.Silu)
                    P.I("dve", "tensor_tensor", [b_pu, b_s], [b_G[j]], out=G[:, j, :w], in0=pu[:, :w], in1=s_[:, :w], op=ALU.mult)
            st, b_st = stat
            pend_st = None
            k.wflush("w1")
            for npair in range(KC // 2):
                wt, b_w = w2r.next()
                si = (w2r.i - 1) % 2
                k.wload("w2", npair, wt, b_w, si, first, [(wt[:], w2v[:, :, npair * 256:(npair + 1) * 256])])
                for nn in range(2):
                    n = npair * 2 + nn
                    y_, b_y = yp.next()
                    for j in range(JC):
                        P.I("pe", "matmul", [b_w, b_G[j]], [b_y], y_[:, :w], lhsT=wt[:, j, nn * 128:(nn + 1) * 128],
                            rhs=G[:, j, :w], start=(j == 0), stop=(j == JC - 1))
                    if pend_st is not None:
                        pend_st()
                    P.I("dve", "tensor_copy", [b_y], [b_xy[n]], out=xy[:, n, :w], in_=y_[:, :w])
                    q_, b_q = sq.next()
                    P.I("act", "activation", [b_xy[n]], [b_q], out=q_[:, :w], in_=xy[:, n, :w], func=AF.Square)

                    def _st(n=n, q_=q_, b_q=b_q, w=w):
                        P.I("pe", "matmul", [b_q, k.b_const], [b_st], st[:, :w], lhsT=k.ones[:], rhs=q_[:, :w], start=(n == 0), stop=(n == KC - 1))
                    pend_st = _st
            pend_st()
            pend_st = None
            k.wflush("w2")
            emit_rstd(k, st, b_st, rstd, b_rstd, rtmp, b_rtmp, w, D)
            residual_update(k, xin_d, xin_key, xout_d, xout_key, off, w, V, xy, b_xy, rstd, b_rstd, tmpf, xst, xo)


def chunk_role(c):
    if c < 8: return ("q", "a", c)
    if c < 10: return ("k", "a", c - 8)
    if c < 12: return ("v", "a", c - 10)
    if c < 20: return ("q", "b", c - 12)
    if c < 22: return ("k", "b", c - 20)
    if c < 24: return ("v", "b", c - 22)
    if c < 32: return ("q", "c", c - 24)
    if c < 40: return ("k", "c", c - 32)
    if c < 48: return ("v", "c", c - 40)
    return ("g", None, c - 48)


QBASE = {"a": 0, "b": 8, "c": 16}
KBASE = {"a": 0, "b": 2, "c": 4}


def inproj_stage(k, x_d, xkey, win_d, vec, T):
    P = k.P
    with k.phase():
        xy = k.sb([128, KC, W], F32, "xy"); b_xy = [Buf() for _ in range(KC)]
        h = k.sb([128, KC, W], BF16, "h"); b_h = [Buf() for _ in range(KC)]
        wr = k.ring(3, [128, KC, 256], BF16, "wr")
        sq = k.ring(2, [128, W], BF16, "sq")
        tmpf = k.ring(2, [128, W], F32, "tmpf")
        rstd = k.sb([128, W], F32, "rstd"); b_rstd = Buf()
        rtmp = k.sb([128, W], F32, "rtmp"); b_rtmp = Buf()
        hrs = k.sb([128, W], F32, "hrs"); b_hrs = Buf()
        hrt = k.sb([128, W], F32, "hrt"); b_hrt = Buf()
        rope = k.sb([128, 4, W], F32, "rope"); b_rope = Buf()
        rmat = k.sb([128, 2, 128], F32, "rmat"); b_rmat = Buf()
        qkn = k.sb([128, 2], F32, "qkn"); b_qkn = Buf()
        qn = k.ring(2, [128, W], F32, "qn")
        t1 = k.ring(2, [128, W], F32, "t1")
        t2 = k.ring(2, [128, W], F32, "t2")
        ob = k.ring(3, [128, W], BF16, "ob")
        vt = k.ring(2, [128, 256], BF16, "vt")
        acc = k.ring(3, [128, W], F32, "acc", psum=True)
        stat = (k.ps([128, W]), Buf())
        hst = stat
        rot = k.ring(2, [128, W], F32, "rot", psum=True)
        vps = k.ring(2, [128, 256], F32, "vps", psum=True)
        P.dma("sp", rmat[:], T["rmat"].rearrange("a p m -> p a m"), [], [b_rmat], "ld_rmat")
        P.dma("sp", qkn[:], T["qkn"], [], [b_qkn], "ld_qkn")
        wv = win_d.rearrange("(kc p) n -> p kc n", p=128)
        due = []

        def step():
            cur = list(due)
            del due[:]
            for f in cur:
                f()

        def make_stage1(kind, br, idx, a_, b_a, off, w, is_ctx):
            def stage1():
                if kind == "g":
                    o_, b_o = ob.next()
                    oi = (ob.i - 1) % 3
                    P.I("act", "activation", [b_a], [b_o], out=o_[:, :w], in_=a_[:, :w], func=AF.Sigmoid)
                    P.dma("sp", T["gT"][idx * 128:(idx + 1) * 128, off:off + w], o_[:, :w], [b_o], [k.dbuf("gT")], f"ob{oi}")
                    return
                q_, b_q = qn.next()
                if br == "a":
                    s_, b_s = sq.next()
                    P.I("act", "activation", [b_a], [b_s], out=s_[:, :w], in_=a_[:, :w], func=AF.Square)
                    P.I("pe", "matmul", [b_s, k.b_const], [hst[1]], hst[0][:, :w], lhsT=k.ones[:], rhs=s_[:, :w], start=True, stop=True)
                    emit_rstd(k, hst[0], hst[1], hrs, b_hrs, hrt, b_hrt, w, HD)
                    col = 0 if kind == "q" else 1
                    P.I("dve", "scalar_tensor_tensor", [b_a, b_hrs, b_qkn], [b_q], out=q_[:, :w], in0=a_[:, :w], scalar=qkn[:, col:col + 1],
                        in1=hrs[:, :w], op0=ALU.mult, op1=ALU.mult)
                else:
                    P.I("act", "copy", [b_a], [b_q], out=q_[:, :w], in_=a_[:, :w])
                due.append(make_stage2(kind, br, idx, q_, b_q, off, w, is_ctx))
            return stage1

        def make_stage2(kind, br, idx, q_, b_q, off, w, is_ctx):
            def stage2():
                o_, b_o = ob.next()
                oi = (ob.i - 1) % 3
                if is_ctx:
                    P.I("dve", "tensor_copy", [b_q], [b_o], out=o_[:, :w], in_=q_[:, :w])
                else:
                    ri = 1 if br == "c" else 0
                    r_, b_r = rot.next()
                    P.I("pe", "matmul", [b_q, b_rmat], [b_r], r_[:, :w], lhsT=rmat[:, ri, :], rhs=q_[:, :w], start=True, stop=True)
                    a1, b_1 = t1.next()
                    a2, b_2 = t2.next()
                    P.I("dve", "tensor_tensor", [b_q, b_rope], [b_1], out=a1[:, :w], in0=q_[:, :w], in1=rope[:, 2 * ri, :w], op=ALU.mult)
                    P.I("dve", "tensor_tensor", [b_r, b_rope], [b_2], out=a2[:, :w], in0=r_[:, :w], in1=rope[:, 2 * ri + 1, :w], op=ALU.mult)
                    P.I("dve", "tensor_tensor", [b_1, b_2], [b_o], out=o_[:, :w], in0=a1[:, :w], in1=a2[:, :w], op=ALU.add)
                if kind == "q":
                    row = (QBASE[br] + idx) * 128
                    P.dma("sp", T["qT"][row:row + 128, off:off + w], o_[:, :w], [b_o], [k.dbuf("qT")], f"ob{oi}")
                else:
                    row = (KBASE[br] + idx) * 128
                    P.dma("sp", T["kT"][row:row + 128, off:off + w], o_[:, :w], [b_o], [k.dbuf("kT")], f"ob{oi}")
            return stage2

        k.wscr("win", 48, [KC, 256])
        for ti, (off, w, is_ctx) in enumerate(tiles_list()):
            first = ti == 0
            k.wflush("win")
            V = vec[(1, 1 if is_ctx else 0)]
            norm_tile(k, x_d, xkey, off, w, V, xy, b_xy, h, b_h, sq, tmpf, stat, rstd, b_rstd, rtmp, b_rtmp)
            if not is_ctx:
                P.dma("sp", rope[:, :, :w], T["rope"][:, :, off:off + w].rearrange("a p t -> p a t"), [], [b_rope], "ld_rope")
            for u in range(48):
                wt, b_w = wr.next()
                wi = (wr.i - 1) % 3
                k.wload("win", u, wt, b_w, wi, first, [(wt[:], wv[:, :, u * 256:(u + 1) * 256])])
                role0 = chunk_role(2 * u)
                if role0[0] == "v":
                    vbase = {"a": 0, "b": 2, "c": 4}[role0[1]] + role0[2]
                    for tb in range((w + 127) // 128):
                        m = min(128, w - tb * 128)
                        vp, b_vp = vps.next()
                        for kk in range(KC):
                            P.I("pe", "matmul", [b_w, b_h[kk]], [b_vp], vp[:m, :], lhsT=h[:, kk, tb * 128:tb * 128 + m], rhs=wt[:, kk, :],
                                start=(kk == 0), stop=(kk == KC - 1))
                        v_, b_v = vt.next()
                        vi = (vt.i - 1) % 2
                        P.I("act", "copy", [b_vp], [b_v], out=v_[:m, :], in_=vp[:m, :])
                        ti = (off // 128 + tb)
                        P.dma("sp", T["vimg"][vbase:vbase + 2, :m, ti, :].rearrange("c p d -> p c d"),
                              v_[:m, :].rearrange("p (c d) -> p c d", c=2), [b_v], [k.dbuf("vimg")], f"vt{vi}")
                        step()
                    continue
                for cc in range(2):
                    c = 2 * u + cc
                    kind, br, idx = chunk_role(c)
                    a_, b_a = acc.next()
                    for kk in range(KC):
                        P.I("pe", "matmul", [b_w, b_h[kk]], [b_a], a_[:, :w], lhsT=wt[:, kk, cc * 128:(cc + 1) * 128], rhs=h[:, kk, :w],
                            start=(kk == 0), stop=(kk == KC - 1))
                    step()
                    due.append(make_stage1(kind, br, idx, a_, b_a, off, w, is_ctx))
            step()
            step()
            step()


def attn_stage(k, T):
    P = k.P
    NLT = TL // 128
    with k.phase():
        Kr = k.ring(2, [128, NR, TLC], BF16, "K")
        Vr = k.ring(2, [128, NR, NTI, 128], BF16, "V")
        qr = k.ring(3, [128, W], BF16, "q")
        pr = k.ring(6, [128, W], BF16, "p")
        pm = k.ring(2, [128, W], BF16, "pm")
        bm = k.sb([128, 6, W], BF16, "bm"); hm = k.sb([128, 2, W], BF16, "hm"); b_bm = Buf()
        obr = k.ring(2, [128, W], BF16, "ob")
        f1 = k.ring(2, [128, W], F32, "f1")
        f2 = k.ring(2, [128, W], F32, "f2")
        f3 = k.ring(2, [128, W], F32, "f3")
        sqb = k.ring(2, [128, W], BF16, "sqb")
        lacc = k.ring(4, [128, W], F32, "lacc")
        rstd = k.sb([128, W], F32, "rstd"); b_rstd = Buf()
        rtmp = k.sb([128, W], F32, "rtmp"); b_rtmp = Buf()
        kh = k.sb([128, 2, 2, 128], BF16, "kh"); vh = k.sb([128, 2, 2, 128], BF16, "vh"); b_halo = Buf()
        esink = k.sb([128, 8], F32, "esink"); b_sink = Buf()
        lam = k.sb([128, 4, 64], F32, "lam"); b_lam = Buf()
        lt = k.sb([128, 2, 64], F32, "lt"); ls = k.sb([128, 2], F32, "ls"); nlam = k.sb([128, 1], F32, "nlam")
        linit = k.sb([128, 1], F32, "linit"); subln = k.sb([128, 1], F32, "subln"); sl = k.sb([128, 1], F32, "sl")
        S = k.ring(4, [128, W], F32, "S", psum=True)
        ACC = k.ring(4, [128, W], F32, "ACC", psum=True)
        scale_ab = float(HD) ** -0.5
        scale_c = float(HD // 2) ** -0.5

        P.dma("sp", esink[:], T["sink"], [], [b_sink], "ld_sink")
        P.I("act", "activation", [b_sink], [b_sink], out=esink[:], in_=esink[:], func=AF.Exp)
        P.dma("sp", lam[:], T["lam"], [], [b_lam], "ld_lam")
        P.dma("sp", linit[:], T["linit"], [], [b_lam], "ld_linit")
        P.dma("sp", subln[:], T["subln"], [], [b_lam], "ld_subln")
        P.I("dve", "tensor_tensor", [b_lam], [b_lam], out=lt[:, 0, :], in0=lam[:, 0, :], in1=lam[:, 1, :], op=ALU.mult)
        P.I("dve", "tensor_tensor", [b_lam], [b_lam], out=lt[:, 1, :], in0=lam[:, 2, :], in1=lam[:, 3, :], op=ALU.mult)
        P.I("dve", "reduce_sum", [b_lam], [b_lam], out=ls[:], in_=lt[:], axis=AX.X)
        P.I("act", "activation", [b_lam], [b_lam], out=ls[:], in_=ls[:], func=AF.Exp)
        P.I("dve", "tensor_tensor", [b_lam], [b_lam], out=nlam[:], in0=ls[:, 1:2], in1=ls[:, 0:1], op=ALU.subtract)
        P.I("dve", "tensor_tensor", [b_lam], [b_lam], out=nlam[:], in0=nlam[:], in1=linit[:], op=ALU.subtract)
        P.I("dve", "tensor_scalar", [b_lam], [b_lam], out=sl[:], in0=linit[:], scalar1=-1.0, scalar2=1.0, op0=ALU.mult, op1=ALU.add)
        P.I("dve", "tensor_tensor", [b_lam], [b_lam], out=sl[:], in0=sl[:], in1=subln[:], op=ALU.mult)
        P.dma("sp", bm[:], T["bmask"].rearrange("m p t -> p m t"), [], [b_bm], "ld_bm")
        P.dma("sp", hm[:], T["hmask"].rearrange("m p t -> p m t"), [], [b_bm], "ld_hm")
        P.dma("sp", kh[:], T["khalo"].rearrange("g s p n -> p g s n"), [], [b_halo], "ld_kh")
        P.dma("sp", vh[:], T["vhalo"].rearrange("g s p n -> p g s n"), [], [b_halo], "ld_vh")

        def load_kv(kc, vc):
            Kt, b_K = Kr.next(); ki = (Kr.i - 1) % 2
            Vt, b_V = Vr.next(); vi = (Vr.i - 1) % 2
            for r in range(NR):
                P.dma("sp", Kt[:, r, :], T["kT_all"][r, kc * 128:(kc + 1) * 128, :], [k.dbuf("kT")], [b_K], f"K{ki}")
                P.dma("sp", Vt[:, r, :, :], T["v_all"][r, vc], [k.dbuf("vimg")], [b_V], f"V{vi}")
            return Kt, b_K, Vt, b_V

        def load_q(qc, off, w):
            q_, b_q = qr.next(); qi = (qr.i - 1) % 3
            P.dma("sp", q_[:, :w], T["qT"][qc * 128:(qc + 1) * 128, off:off + w], [k.dbuf("qT")], [b_q], f"q{qi}")
            return q_, b_q

        def store_o(oc, off, w, o_, b_o):
            oi = (obr.i - 1) % 2
            P.dma("sp", T["attnT"][oc * 128:(oc + 1) * 128, off:off + w], o_[:, :w], [b_o], [k.dbuf("attnT")], f"ao{oi}")

        def all_keys(is_ctx):
            ks = []
            if not is_ctx:
                for r in range(NR):
                    for t in range(NLT):
                        ks.append((r, t, 128))
            for r in range(NR):
                for ct in range(NCT):
                    ks.append((r, NLT + ct, min(128, CTXL - ct * 128)))
            return ks

        def softmax_loop(keys, w, scale, O, b_O, L, b_L):
            n = len(keys)
            la, b_la = lacc.next()
            P.I("pool", "memset", [], [b_la], la[:, :w], 0.0)
            LA = 3

            def issue_s(i):
                kd = keys[i]
                s_, b_s = S.next()
                P.I("pe", "matmul", kd["kbufs"] + [kd["q"][1]], [b_s], s_[:kd["nk"], :w], lhsT=kd["kT"], rhs=kd["q"][0], start=True, stop=True)
                return s_, b_s
            sq_ = [issue_s(i) for i in range(min(LA, n))]
            for i, kd in enumerate(keys):
                nk = kd["nk"]
                s_, b_s = sq_[i]
                p_, b_p = pr.next()
                P.I("act", "activation", [b_s], [b_p], out=p_[:nk, :w], in_=s_[:nk, :w], func=AF.Exp, scale=scale)
                if i + LA < n:
                    sq_.append(issue_s(i + LA))
                if kd.get("mask") is not None:
                    m_, b_m = kd["mask"]
                    p2, b_p2 = pm.next()
                    P.I("dve", "tensor_tensor", [b_p, b_m], [b_p2], out=p2[:nk, :w], in0=p_[:nk, :w], in1=m_[:nk, :w], op=ALU.mult)
                    p_, b_p = p2, b_p2
                P.I("pe", "matmul", kd["vbufs"] + [b_p], [b_O], O[:, :w], lhsT=kd["v"], rhs=p_[:nk, :w], start=(i == 0), stop=(i == n - 1))
                if i % 2 == 0:
                    P.I("pe", "matmul", [k.b_const, b_p], [b_L], L[:, :w], lhsT=k.ones[:nk, :], rhs=p_[:nk, :w], start=(i == 0), stop=False)
                else:
                    P.I("dve", "tensor_tensor", [b_la, b_p], [b_la], out=la[:nk, :w], in0=la[:nk, :w], in1=p_[:nk, :w], op=ALU.add)
            P.I("pe", "matmul", [k.b_const, b_la], [b_L], L[:, :w], lhsT=k.ones32[:], rhs=la[:, :w], start=False, stop=True)

        for g in range(2):
            Kt, b_K, Vt, b_V = load_kv(g, g)
            for hh in range(4):
                hd = 4 * g + hh
                for (off, w, is_ctx) in tiles_list():
                    q_, b_q = load_q(QBASE["a"] + hd, off, w)
                    keys = []
                    for (r, t, nk) in all_keys(is_ctx):
                        keys.append(dict(kT=Kt[:, r, t * 128:t * 128 + nk], kbufs=[b_K], v=Vt[:nk, r, t, :], vbufs=[b_V], nk=nk,
                                         q=(q_[:, :w], b_q)))
                    O, b_O = ACC.next(); L, b_L = ACC.next()
                    softmax_loop(keys, w, scale_ab, O, b_O, L, b_L)
                    r_, b_r = f1.next()
                    P.I("dve", "reciprocal", [b_L], [b_r], out=r_[:, :w], in_=L[:, :w])
                    o_, b_o = obr.next()
                    P.I("dve", "tensor_tensor", [b_O, b_r], [b_o], out=o_[:, :w], in0=O[:, :w], in1=r_[:, :w], op=ALU.mult)
                    store_o(hd, off, w, o_, b_o)

        Kb = k.sb([128, TL], BF16, "Kb"); Vb = k.sb([128, NLT, 128], BF16, "Vb")
        Kbc = k.sb([128, NR, CTXL], BF16, "Kbc"); Vbc = k.sb([128, NR, NCT, 128], BF16, "Vbc")
        b_Kb = Buf(); b_Vb = Buf()
        for g in range(2):
            P.dma("sp", Kb[:], T["kT"][(2 + g) * 128:(3 + g) * 128, 0:TL], [k.dbuf("kT")], [b_Kb], "Kb")
            P.dma("sp", Vb[:], T["vimg"][2 + g, :, 0:NLT, :], [k.dbuf("vimg")], [b_Vb], "Vb")
            for r in range(NR):
                P.dma("sp", Kbc[:, r, :], T["kT_all"][r, (2 + g) * 128:(3 + g) * 128, TL:TLC], [k.dbuf("kT")], [b_Kb], "Kb")
                P.dma("sp", Vbc[:, r, :, :], T["v_all"][r, 2 + g, :, NLT:NLT + NCT, :], [k.dbuf("vimg")], [b_Vb], "Vb")
            for hh in range(4):
                hd = 4 * g + hh
                for ti, (off, w, is_ctx) in enumerate(tiles_list()):
                    q_, b_q = load_q(QBASE["b"] + hd, off, w)
                    qq = (q_[:, :w], b_q)
                    keys = []
                    if not is_ctx:
                        t0 = off // 128
                        for rel in range(-1, 5):
                            kt = t0 + rel
                            if kt < 0 or kt >= NLT:
                                continue
                            keys.append(dict(kT=Kb[:, kt * 128:(kt + 1) * 128], kbufs=[b_Kb], v=Vb[:, kt, :], vbufs=[b_Vb], nk=128,
                                             mask=(bm[:, rel + 1, :], b_bm), q=qq))
                        for side, cond in ((0, off == 0), (1, off + w == TL)):
                            if cond and NR > 1:
                                keys.append(dict(kT=kh[:, g, side, :], kbufs=[b_halo], v=vh[:, g, side, :], vbufs=[b_halo], nk=128,
                                                 mask=(hm[:, side, :], b_bm), q=qq))
                    for r in range(NR):
                        for ct in range(NCT):
                            nk = min(128, CTXL - ct * 128)
                            keys.append(dict(kT=Kbc[:, r, ct * 128:ct * 128 + nk], kbufs=[b_Kb], v=Vbc[:nk, r, ct, :], vbufs=[b_Vb], nk=nk, q=qq))
                    O, b_O = ACC.next(); L, b_L = ACC.next()
                    softmax_loop(keys, w, scale_ab, O, b_O, L, b_L)
                    r_, b_r = f1.next()
                    P.I("dve", "tensor_scalar", [b_L, b_sink], [b_r], out=r_[:, :w], in0=L[:, :w], scalar1=esink[:, hd:hd + 1], scalar2=None,
                        op0=ALU.add)
                    r2, b_r2 = f2.next()
                    P.I("dve", "reciprocal", [b_r], [b_r2], out=r2[:, :w], in_=r_[:, :w])
                    o_, b_o = obr.next()
                    P.I("dve", "tensor_tensor", [b_O, b_r2], [b_o], out=o_[:, :w], in0=O[:, :w], in1=r2[:, :w], op=ALU.mult)
                    store_o(8 + hd, off, w, o_, b_o)

        for hd in range(8):
            Kt, b_K, Vt, b_V = load_kv(4 + hd, 4 + hd)
            for (off, w, is_ctx) in tiles_list():
                q_, b_q = load_q(QBASE["c"] + hd, off, w)
                accs = [ACC.next() for _ in range(4)]
                for half in range(2):
                    lo, hi = half * 64, half * 64 + 64
                    keys = []
                    for (r, t, nk) in all_keys(is_ctx):
                        keys.append(dict(kT=Kt[lo:hi, r, t * 128:t * 128 + nk], kbufs=[b_K], v=Vt[:nk, r, t, :], vbufs=[b_V], nk=nk,
                                         q=(q_[lo:hi, :w], b_q)))
                    (O, b_O), (L, b_L) = accs[2 * half], accs[2 * half + 1]
                    softmax_loop(keys, w, scale_c, O, b_O, L, b_L)
                (O1, b_O1), (L1, b_L1), (O2, b_O2), (L2, b_L2) = accs
                r1, b_r1 = f1.next()
                P.I("dve", "reciprocal", [b_L1], [b_r1], out=r1[:, :w], in_=L1[:, :w])
                a1, b_a1 = f2.next()
                P.I("dve", "tensor_tensor", [b_O1, b_r1], [b_a1], out=a1[:, :w], in0=O1[:, :w], in1=r1[:, :w], op=ALU.mult)
                r2, b_r2 = f1.next()
                P.I("dve", "reciprocal", [b_L2], [b_r2], out=r2[:, :w], in_=L2[:, :w])
                a2, b_a2 = f2.next()
                P.I("dve", "tensor_tensor", [b_O2, b_r2], [b_a2], out=a2[:, :w], in0=O2[:, :w], in1=r2[:, :w], op=ALU.mult)
                o3, b_o3 = f3.next()
                P.I("dve", "scalar_tensor_tensor", [b_a2, b_a1, b_lam], [b_o3], out=o3[:, :w], in0=a2[:, :w], scalar=nlam[:, 0:1], in1=a1[:, :w],
                    op0=ALU.mult, op1=ALU.add)
                s_, b_s = sqb.next()
                P.I("act", "activation", [b_o3], [b_s], out=s_[:, :w], in_=o3[:, :w], func=AF.Square)
                st, b_st = S.next()
                P.I("pe", "matmul", [b_s, k.b_const], [b_st], st[:, :w], lhsT=k.ones[:], rhs=s_[:, :w], start=True, stop=True)
                emit_rstd(k, st, b_st, rstd, b_rstd, rtmp, b_rtmp, w, HD)
                o_, b_o = obr.next()
                P.I("dve", "scalar_tensor_tensor", [b_o3, b_rstd, b_lam], [b_o], out=o_[:, :w], in0=o3[:, :w], scalar=sl[:, 0:1], in1=rstd[:, :w],
                    op0=ALU.mult, op1=ALU.mult)
                store_o(16 + hd, off, w, o_, b_o)


def merge_stage(k, xin_d, xin_key, xout_d, xout_key, wbr_d, wout_d, vec, T):
    P = k.P
    with k.phase():
        at = k.sb([128, 24, W], BF16, "at"); b_at = Buf()
        gt = k.sb([128, 48, W], BF16, "gt"); b_gt = Buf()
        ym = k.sb([128, KC, W], BF16, "ym"); b_ym = [Buf() for _ in range(KC)]
        xy = k.sb([128, KC, W], F32, "xy"); b_xy = [Buf() for _ in range(KC)]
        wbr = k.ring(2, [128, 3, 8, 256], BF16, "wbr")
        wor = k.ring(2, [128, KC, 256], BF16, "wor")
        sq = k.ring(2, [128, W], BF16, "sq")
        tmpf = k.ring(2, [128, W], F32, "tmpf")
        m1 = k.ring(2, [128, W], F32, "m1")
        m2 = k.ring(2, [128, W], F32, "m2")
        rstd = k.sb([128, W], F32, "rstd"); b_rstd = Buf()
        rtmp = k.sb([128, W], F32, "rtmp"); b_rtmp = Buf()
        xst = k.ring(3, [128, W], F32, "xst")
        xo = k.ring(2, [128, W], F32, "xo")
        bp = k.ring(6, [128, W], F32, "bp", psum=True)
        stat = (k.ps([128, W]), Buf())
        wbv = wbr_d.rearrange("r (kc p) n -> p r kc n", p=128)
        wov = wout_d.rearrange("(kc p) n -> p kc n", p=128)
        k.wscr("wbr", KC // 2, [3, 8, 256]); k.wscr("wor", KC // 2, [KC, 256])
        for ti, (off, w, is_ctx) in enumerate(tiles_list()):
            first = ti == 0
            V = vec[(1, 1 if is_ctx else 0)]
            P.dma("sp", at[:, :, :w], T["attnT"][:, off:off + w].rearrange("(c p) t -> p c t", p=128), [k.dbuf("attnT")], [b_at], "ld_at")
            P.dma("sp", gt[:, :, :w], T["gT"][:, off:off + w].rearrange("(c p) t -> p c t", p=128), [k.dbuf("gT")], [b_gt], "ld_gt")
            for npair in range(KC // 2):
                wt, b_w = wbr.next(); wi = (wbr.i - 1) % 2
                k.wload("wbr", npair, wt, b_w, wi, first, [(wt[:, r, :, :], wbv[:, r, :, npair * 256:(npair + 1) * 256]) for r in range(3)])
                for nn in range(2):
                    n = npair * 2 + nn
                    pss = [bp.next() for _ in range(3)]
                    for r in range(3):
                        p_, b_p = pss[r]
                        for kk in range(8):
                            P.I("pe", "matmul", [b_w, b_at], [b_p], p_[:, :w], lhsT=wt[:, r, kk, nn * 128:(nn + 1) * 128], rhs=at[:, r * 8 + kk, :w],
                                start=(kk == 0), stop=(kk == 7))
                    a_, b_a = m1.next()
                    P.I("dve", "tensor_tensor", [pss[0][1], b_gt], [b_a], out=a_[:, :w], in0=pss[0][0][:, :w], in1=gt[:, n, :w], op=ALU.mult)
                    c_, b_c = m2.next()
                    P.I("dve", "tensor_tensor", [pss[1][1], b_gt], [b_c], out=c_[:, :w], in0=pss[1][0][:, :w], in1=gt[:, 16 + n, :w], op=ALU.mult)
                    P.I("dve", "tensor_tensor", [b_a, b_c], [b_a], out=a_[:, :w], in0=a_[:, :w], in1=c_[:, :w], op=ALU.add)
                    c2, b_c2 = m2.next()
                    P.I("dve", "tensor_tensor", [pss[2][1], b_gt], [b_c2], out=c2[:, :w], in0=pss[2][0][:, :w], in1=gt[:, 32 + n, :w], op=ALU.mult)
                    P.I("dve", "tensor_tensor", [b_a, b_c2], [b_ym[n]], out=ym[:, n, :w], in0=a_[:, :w], in1=c2[:, :w], op=ALU.add)
            k.wflush("wbr")
            st, b_st = stat
            for npair in range(KC // 2):
                wt, b_w = wor.next(); wi = (wor.i - 1) % 2
                k.wload("wor", npair, wt, b_w, wi, first, [(wt[:], wov[:, :, npair * 256:(npair + 1) * 256])])
                for nn in range(2):
                    n = npair * 2 + nn
                    y_, b_y = bp.next()
                    for kk in range(KC):
                        P.I("pe", "matmul", [b_w, b_ym[kk]], [b_y], y_[:, :w], lhsT=wt[:, kk, nn * 128:(nn + 1) * 128], rhs=ym[:, kk, :w],
                            start=(kk == 0), stop=(kk == KC - 1))
                    P.I("dve", "tensor_copy", [b_y], [b_xy[n]], out=xy[:, n, :w], in_=y_[:, :w])
                    q_, b_q = sq.next()
                    P.I("act", "activation", [b_xy[n]], [b_q], out=q_[:, :w], in_=xy[:, n, :w], func=AF.Square)
                    P.I("pe", "matmul", [b_q, k.b_const], [b_st], st[:, :w], lhsT=k.ones[:], rhs=q_[:, :w], start=(n == 0), stop=(n == KC - 1))
            k.wflush("wor")
            emit_rstd(k, st, b_st, rstd, b_rstd, rtmp, b_rtmp, w, D)
            residual_update(k, xin_d, xin_key, xout_d, xout_key, off, w, V, xy, b_xy, rstd, b_rstd, tmpf, xst, xo)


def new_k():
    nc = bass.Bass("TRN2", target_bir_lowering=False)
    es = contextlib.ExitStack()
    k = K(nc, es)
    k.es_cur = es
    return k


def mod_stage(k, cT_d, wada_d, bada_d, mo, b_mo):
    P = k.P
    NCH = 9 * KC
    with k.phase():
        cs = k.sb([128, KC, 2], F32, "cs"); b_cs = Buf()
        bs = k.sb([128, DEPTH, NCH], F32, "bs"); b_bs = Buf()
        wr = k.ring(4, [128, KC, 256], F32, "wr")
        pp = k.ring(4, [128, 2], F32, "pp", psum=True)
        P.dma("sp", cs[:], cT_d, [], [b_cs], "ld_c")
        P.dma("sp", bs[:], bada_d, [], [b_bs], "ld_b")
        P.I("act", "activation", [b_cs], [b_cs], out=cs[:], in_=cs[:], func=AF.Silu)
        for l in range(DEPTH):
            wv = wada_d[l].rearrange("(kc p) n -> p kc n", p=128)
            for u in range(NCH // 2):
                wt, b_w = wr.next(); wi = (wr.i - 1) % 4
                P.dma("sp", wt[:], wv[:, :, u * 256:(u + 1) * 256], [], [b_w], f"wa{wi}")
                for cc in range(2):
                    c = 2 * u + cc
                    p_, b_p = pp.next()
                    for kk in range(KC):
                        P.I("pe", "matmul", [b_w, b_cs], [b_p], p_[:, 0:2], lhsT=wt[:, kk, cc * 128:(cc + 1) * 128], rhs=cs[:, kk, :],
                            start=(kk == 0), stop=(kk == KC - 1))
                    P.I("dve", "tensor_scalar", [b_p, b_bs], [b_mo], out=mo[:, l, c, :], in0=p_[:, 0:2], scalar1=bs[:, l, c:c + 1], scalar2=None,
                        op0=ALU.add)


def layer_vecs(k, l, mo, b_mo, npre, npost, b_np, gs, hg):
    P = k.P
    vec = {}
    for v in range(2):
        for s in range(3):
            def m(i):
                return mo[:, l, i * KC:(i + 1) * KC, v]
            P.I("dve", "scalar_tensor_tensor", [b_mo, b_np], [k.b_vec], out=gs[:, v, s, :], in0=m(3 * s + 1), scalar=1.0,
                in1=npre[:, l, s, :], op0=ALU.add, op1=ALU.mult)
            P.I("dve", "scalar_tensor_tensor", [b_mo, b_np], [k.b_vec], out=hg[:, v, s, :], in0=m(3 * s + 2),
                scalar=(1.0 if s == 1 else 0.5), in1=npost[:, l, s, :], op0=ALU.mult, op1=ALU.mult)
            vec[(s, v)] = dict(gs=gs[:, v, s, :], sh=m(3 * s), hg=hg[:, v, s, :])
    return vec


def build_fused():
    k = new_k(); nc = k.nc; P = k.P
    xin = k.dram_in("xT", [D, TLC], F32)
    cT = k.dram_in("cT", [128, KC, 2], F32)
    wada = k.dram_in("wada", [DEPTH, D, 9 * D], F32)
    bada = k.dram_in("bada", [128, DEPTH, 9 * KC], F32)
    npre_d = k.dram_in("npre", [128, DEPTH, 3, KC], F32)
    npost_d = k.dram_in("npost", [128, DEPTH, 3, KC], F32)
    wffin = k.dram_in("wffin", [DEPTH, 2, D, 2 * DFF], F32)
    wffout = k.dram_in("wffout", [DEPTH, 2, DFF, D], F32)
    win = k.dram_in("win", [DEPTH, D, INW], F32)
    wbr = k.dram_in("wbr", [DEPTH, 3, 1024, D], F32)
    wout = k.dram_in("wout", [DEPTH, D, D], F32)
    qkn = k.dram_in("qkn", [DEPTH, 128, 2], F32)
    sink = k.dram_in("sink", [DEPTH, 128, 8], F32)
    lam = k.dram_in("lam", [DEPTH, 128, 4, 64], F32)
    linit = k.dram_in("linit", [DEPTH, 128, 1], F32)
    subln = k.dram_in("subln", [DEPTH, 128, 1], F32)
    rope = k.dram_in("rope", [4, 128, TL], F32)
    rmat = k.dram_in("rmat", [2, 128, 128], F32)
    bmask = k.dram_in("bmask", [6, 128, W], BF16)
    hmask = k.dram_in("hmask", [2, 128, W], BF16)
    khalo = k.dram_in("khalo", [2, 2, 128, 128], BF16)
    vhalo = k.dram_in("vhalo", [2, 2, 128, 128], BF16)
    xout = k.dram_out("xout", [D, TLC], F32)

    def scratch(name, shape, dt):
        return nc.dram_tensor(name, list(shape), dt, kind="Internal").ap()
    xs = [scratch("xs0", [D, TLC], F32), scratch("xs1", [D, TLC], F32)]
    kT = scratch("kT", [NR, NKC * 128, TLC], BF16)
    vimg = scratch("vimg", [NR, NVC, 128, NTI, 128], BF16)
    Tb = dict(rope=rope, rmat=rmat, bmask=bmask, hmask=hmask, khalo=khalo, vhalo=vhalo,
              qT=scratch("qT", [NQC * 128, TLC], BF16), gT=scratch("gT", [48 * 128, TLC], BF16),
              attnT=scratch("attnT", [24 * 128, TLC], BF16), kT=kT[0], vimg=vimg[0], kT_all=kT, v_all=vimg)

    setup_consts(k)
    mo = k.sb([128, DEPTH, 9 * KC, 2], F32, "mo"); b_mo = Buf("mo")
    npre = k.sb([128, DEPTH, 3, KC], F32, "npre"); npost = k.sb([128, DEPTH, 3, KC], F32, "npost"); b_np = Buf("np")
    gs = k.sb([128, 2, 3, KC], F32, "gs"); hg = k.sb([128, 2, 3, KC], F32, "hg")
    k.b_vec = Buf("vec"); k.b_vec_in = b_mo
    P.dma("sp", npre[:], npre_d, [], [b_np], "ld_npre")
    P.dma("sp", npost[:], npost_d, [], [b_np], "ld_npost")
    mod_stage(k, cT, wada, bada, mo, b_mo)

    seq = []
    nsub = 3 * DEPTH
    for i in range(nsub + 1):
        if i == 0:
            seq.append((xin, "xin"))
        elif i == nsub:
            seq.append((xout, "xout"))
        else:
            seq.append((xs[(i - 1) % 2], f"xs{(i - 1) % 2}"))
    si = 0
    for l in range(DEPTH):
        vec = layer_vecs(k, l, mo, b_mo, npre, npost, b_np, gs, hg)
        T = dict(Tb)
        T.update(qkn=qkn[l], sink=sink[l], lam=lam[l], linit=linit[l], subln=subln[l])
        (a, ak), (b, bk) = seq[si], seq[si + 1]
        ffn_sublayer(k, a, ak, b, bk, wffin[l, 0], wffout[l, 0], vec, 0)
        si += 1
        inproj_stage(k, b, bk, win[l], vec, T)
        attn_stage(k, T)
        (a, ak), (b, bk) = seq[si], seq[si + 1]
        merge_stage(k, a, ak, b, bk, wbr[l], wout[l], vec, T)
        si += 1
        (a, ak), (b, bk) = seq[si], seq[si + 1]
        ffn_sublayer(k, a, ak, b, bk, wffin[l, 1], wffout[l, 1], vec, 2)
        si += 1
    st = k.P.finish()
    k.es.close()
    k.stats = st
    return nc


def fm(v):
    v = np.asarray(v)
    lead = v.shape[:-1]
    return np.ascontiguousarray(np.moveaxis(v.reshape(*lead, KC, 128), -1, 0))


def rope_consts(tok0):
    def tables(rot_dim):
        pos = tok0 + np.arange(TL)
        row = (pos // GRID_W).astype(np.float32)
        col = (pos % GRID_W).astype(np.float32)
        d_ax = rot_dim // 2
        inv = (10000.0 ** (-np.arange(0, d_ax, 2, dtype=np.float32) / d_ax)).astype(np.float32)
        ang_r = row[:, None] * inv[None, :]
        ang_c = col[:, None] * inv[None, :]
        ang = np.concatenate([ang_r, ang_r, ang_c, ang_c], axis=-1).astype(np.float32)
        return np.cos(ang).T.astype(np.float32), np.sin(ang).T.astype(np.float32)
    cab, sab = tables(128)
    cc, sc = tables(64)
    return np.ascontiguousarray(np.stack([cab, sab, np.concatenate([cc, cc], 0), np.concatenate([sc, sc], 0)], 0))


def rot_mats():
    def rt(dim, n):
        q = dim // 4
        m = np.zeros((n, n), np.float32)
        for base in range(0, n, dim):
            for i in range(dim):
                quarter = i // q
                if quarter == 0: src, sgn = i + q, -1.0
                elif quarter == 1: src, sgn = i - q, 1.0
                elif quarter == 2: src, sgn = i + q, -1.0
                else: src, sgn = i - q, 1.0
                m[base + src, base + i] = sgn
        return m
    return np.stack([rt(128, 128), rt(64, 128)], 0)


def band_masks():
    j = np.arange(128)[:, None]
    i = np.arange(W)[None, :]
    ms = []
    for rel in range(-1, 5):
        ms.append((np.abs(rel * 128 + j - i) <= 128))
    return np.stack(ms, 0)


_CACHE = {}


def get_prog(name, builder):
    key = (name, SEQ, DEPTH)
    if key not in _CACHE:
        _CACHE[key] = builder()
    return _CACHE[key]


def kernel(x, c, ctx, c_ctx, w_ada, b_ada, norm_pre, norm_post, w_ff_in, w_ff_out, w_in,
           qk_norm_a, sink_b, lam_c, subln_c, w_branch, w_out):
    f32 = np.float32
    A = lambda v: np.ascontiguousarray(np.asarray(v, f32))
    x = A(x); ctx = A(ctx); c = A(c); c_ctx = A(c_ctx)
    depth = np.asarray(w_ada).shape[0]
    assert depth == DEPTH and x.shape[1] == SEQ
    nb = x.shape[0]
    cores = list(range(nb))
    nc = get_prog("fused", build_fused)
    bc = lambda v, shape: np.ascontiguousarray(np.broadcast_to(v, shape))
    linit = np.array([0.8 - 0.6 * float(np.exp(-0.3 * l)) for l in range(depth)], f32)
    shared = dict(
        wada=A(w_ada), bada=np.ascontiguousarray(A(b_ada).reshape(depth, 9 * KC, 128).transpose(2, 0, 1)),
        npre=fm(A(norm_pre)), npost=fm(A(norm_post)),
        wffin=A(w_ff_in), wffout=A(w_ff_out), win=A(w_in), wbr=A(w_branch), wout=A(w_out),
        qkn=np.ascontiguousarray(A(qk_norm_a).transpose(0, 2, 1)),
        sink=bc(A(sink_b)[:, None, :], (depth, 128, 8)), lam=bc(A(lam_c)[:, None], (depth, 128, 4, 64)),
        linit=bc(linit[:, None, None], (depth, 128, 1)), subln=np.ascontiguousarray(A(subln_c).reshape(depth, 128, 1)),
        rope=rope_consts(0), rmat=rot_mats(), bmask=band_masks().astype(NPBF),
        hmask=np.zeros((2, 128, W), NPBF), khalo=np.zeros((2, 2, 128, 128), NPBF), vhalo=np.zeros((2, 2, 128, 128), NPBF))
    in_maps = []
    for b in cores:
        m = dict(shared)
        m["xT"] = np.ascontiguousarray(np.concatenate([x[b], ctx[b]], 0).T)
        m["cT"] = np.ascontiguousarray(np.stack([c[b], c_ctx], 0).reshape(2, KC, 128).transpose(2, 1, 0))
        in_maps.append(m)
    res = run_bass_kernel_spmd(nc, in_maps, core_ids=cores).results
    out = np.zeros((nb, SEQ, D), f32)
    for b in cores:
        out[b] = np.asarray(res[b]["xout"])[:, :TL].T
    return out
```

```python
import contextlib
import numpy as np
import ml_dtypes
import concourse.bass as bass
import concourse.mybir as mybir
from concourse.bass_utils import run_bass_kernel_spmd

F32 = mybir.dt.float32
BF16 = mybir.dt.bfloat16
AF = mybir.ActivationFunctionType
ALU = mybir.AluOpType
AX = mybir.AxisListType
NPBF = ml_dtypes.bfloat16

D = 2048
KC = 16
DFF = 5632
JC = 44
EPS = 1e-6
DEPTH = 4
NCORE = 2
NR = 1
SEQ = 8192
TL = SEQ // NR
CTX = 256
CTXL = CTX // NR
NCT = (CTXL + 127) // 128
TLC = TL + CTXL
GRID_W = 64
HD = 128
INW = 12288
W = 512
NQC = 24
NKC = 12
NVC = 12
NTI = TL // 128 + NCT
DEBUG = False


def set_scale(seq):
    global SEQ, TL, TLC, NTI
    SEQ = seq
    TL = SEQ // NR
    TLC = TL + CTXL
    NTI = TL // 128 + NCT


class Buf:
    __slots__ = ("name", "last_w", "readers")

    def __init__(self, name=""):
        self.name = name
        self.last_w = None
        self.readers = {}


class Prog:
    ENGINES = ("pe", "act", "dve", "pool", "sp")

    def __init__(self, nc, es):
        self.nc = nc
        self.es = es
        self.ops = []
        self.emap = {"pe": nc.tensor, "act": nc.scalar, "dve": nc.vector, "pool": nc.gpsimd, "sp": nc.sync}
        self.base = 0
        self.chan = []
        self.val = []
        self.waited = {}
        self.last_on_chan = {}
        self.count = {}
        self.sems = {}
        self.n_ops = 0

    def I(self, engine, method, reads, writes, *args, **kw):
        f = getattr(self.emap[engine], method)
        self.ops.append((engine, (lambda: f(*args, **kw)), tuple(reads), tuple(writes), None))

    def dma(self, engine, out, in_, reads, writes, stream):
        f = self.emap[engine].dma_start
        self.ops.append((engine, (lambda: f(out=out, in_=in_)), tuple(reads), tuple(writes), stream))

    def barrier(self):
        self.ops.append(("barrier", None, (), (), None))

    def flush(self):
        nc = self.nc
        ops = self.ops
        n = len(ops)
        base = self.base
        chan = self.chan
        val = self.val
        waited = self.waited
        last_on_chan = self.last_on_chan
        for o in ops:
            chan.append(o[4] if o[4] is not None else o[0])
            val.append(0)
        waits = [None] * n
        signal = [False] * n
        for li, (eng, emit, r, w, stream) in enumerate(ops):
            i = base + li
            if eng == "barrier":
                wl = {}
                for c, j in last_on_chan.items():
                    need = False
                    for e in self.ENGINES:
                        if waited.get((e, c), -1) < j:
                            waited[(e, c)] = j
                            need = True
                    if need:
                        wl[c] = j
                        if j >= base:
                            signal[j - base] = True
                waits[li] = wl
                continue
            deps = {}
            for b in r:
                j = b.last_w
                if j is not None:
                    c = chan[j]
                    if deps.get(c, -1) < j:
                        deps[c] = j
            for b in w:
                j = b.last_w
                if j is not None:
                    c = chan[j]
                    if deps.get(c, -1) < j:
                        deps[c] = j
                for c, j in b.readers.items():
                    if deps.get(c, -1) < j:
                        deps[c] = j
            my = chan[i]
            wl = None
            for c, j in deps.items():
                if c == my and (c == "pe" or stream is not None):
                    continue
                if waited.get((eng, c), -1) >= j:
                    continue
                waited[(eng, c)] = j
                assert j >= base, "dependency on an un-signalled op from a previous phase"
                signal[j - base] = True
                if wl is None:
                    wl = {}
                wl[c] = j
            waits[li] = wl
            for b in r:
                b.readers[my] = i
            for b in w:
                b.last_w = i
                b.readers = {}
            last_on_chan[my] = i
        count = self.count
        for li, (eng, emit, r, w, stream) in enumerate(ops):
            if eng == "barrier":
                continue
            i = base + li
            c = chan[i]
            if stream is not None:
                count[c] = count.get(c, 0) + 16
                val[i] = count[c]
            elif signal[li]:
                count[c] = count.get(c, 0) + 1
                val[i] = count[c]
        sems = self.sems
        for c in count:
            if c not in sems:
                sems[c] = self.es.enter_context(nc.semaphore("s_" + c))
        for li, (eng, emit, r, w, stream) in enumerate(ops):
            i = base + li
            if eng == "barrier":
                for e in self.ENGINES:
                    ee = self.emap[e]
                    for c, j in waits[li].items():
                        ee.wait_ge(sems[c], val[j])
                continue
            e = self.emap[eng]
            if waits[li]:
                for c, j in waits[li].items():
                    e.wait_ge(sems[c], val[j])
            ins = emit()
            if stream is not None:
                ins.then_inc(sems[chan[i]], 16)
            elif signal[li]:
                ins.then_inc(sems[chan[i]], 1)
        self.base += n
        self.n_ops += n
        self.ops = []

    def finish(self):
        self.barrier()
        self.flush()
        for c, v in self.count.items():
            self.nc.sync.wait_ge(self.sems[c], v)
        return dict(n_ops=self.n_ops, n_sems=len(self.sems))


class Ring:
    def __init__(self, items):
        self.items = items
        self.i = 0

    def next(self):
        it = self.items[self.i % len(self.items)]
        self.i += 1
        return it


class K:
    def __init__(self, nc, es):
        self.nc = nc
        self.es = es
        self.P = Prog(nc, es)
        self.dbufs = {}
        self.uid = 0

    def sb(self, shape, dt, name=None):
        self.uid += 1
        return self.es_cur.enter_context(self.nc.sbuf_tensor(f"{name or 't'}_{self.uid}", list(shape), dt))

    def ps(self, shape, dt=F32, name=None):
        self.uid += 1
        return self.es_cur.enter_context(self.nc.psum_tensor(f"{name or 'p'}_{self.uid}", list(shape), dt))

    def ring(self, n, shape, dt, name=None, psum=False):
        f = self.ps if psum else self.sb
        return Ring([(f(shape, dt, name), Buf(name or "")) for _ in range(n)])

    def dbuf(self, key):
        b = self.dbufs.get(key)
        if b is None:
            b = self.dbufs[key] = Buf(str(key))
        return b

    def wscr(self, name, n_units, unit_shape):
        d = self.__dict__.setdefault("_wscr", {})
        if name not in d:
            d[name] = self.nc.dram_tensor("wscr_" + name, [n_units, 128] + list(unit_shape), BF16, kind="Internal").ap()
        return d[name]

    def wload(self, name, u, slot, b_slot, si, first, cast_parts):
        P = self.P
        scr = self._wscr[name]
        wb = self.dbuf(("wscr", name, u))
        if first:
            for dst, src in cast_parts:
                P.dma("pool", dst, src, [], [b_slot], f"{name}_{si}")
            pend = self.__dict__.setdefault("_wpend", {})
            prev = pend.get(name)
            if prev is not None:
                prev()
            pend[name] = lambda: P.dma("pool", scr[u], slot[:], [b_slot], [wb], f"{name}s_{si}")
        else:
            self.wflush(name)
            P.dma("pool", slot[:], scr[u], [wb], [b_slot], f"{name}_{si}")

    def wflush(self, name):
        pend = self.__dict__.setdefault("_wpend", {})
        prev = pend.pop(name, None)
        if prev is not None:
            prev()

    def dram_in(self, name, shape, dt):
        return self.nc.dram_tensor(name, list(shape), dt, kind="ExternalInput").ap()

    def dram_out(self, name, shape, dt):
        return self.nc.dram_tensor(name, list(shape), dt, kind="ExternalOutput").ap()

    @contextlib.contextmanager
    def phase(self):
        with contextlib.ExitStack() as pes:
            old = getattr(self, "es_cur", None)
            self.es_cur = pes
            yield
            self.P.barrier()
            self.P.flush()
            self.es_cur = old


def tiles_list():
    return [(t * W, W, False) for t in range(TL // W)] + [(TL, CTXL, True)]


def emit_rstd(k, stat_ps, b_stat, rstd, b_rstd, tmp, b_tmp, w, nfeat):
    P = k.P
    P.I("dve", "tensor_scalar", [b_stat], [b_tmp], out=tmp[:, :w], in0=stat_ps[:, :w], scalar1=1.0 / nfeat, scalar2=EPS,
        op0=ALU.mult, op1=ALU.add)
    P.I("act", "sqrt", [b_tmp], [b_tmp], out=tmp[:, :w], in_=tmp[:, :w])
    P.I("dve", "reciprocal", [b_tmp], [b_rstd], out=rstd[:, :w], in_=tmp[:, :w])


def setup_consts(k):
    P = k.P
    k.ones = k.sb([128, 128], BF16, "ones")
    k.b_const = Buf("const")
    P.I("dve", "memset", [], [k.b_const], k.ones[:], 1.0)
    k.ones32 = k.sb([128, 128], F32, "ones32")
    P.I("dve", "memset", [], [k.b_const], k.ones32[:], 1.0)


def setup_vecs(k, modv_d, npre_d, npost_d):
    P = k.P
    modv = k.sb([128, 2, 9, KC], F32, "modv")
    npre = k.sb([128, 3, KC], F32, "npre")
    npost = k.sb([128, 3, KC], F32, "npost")
    gs = k.sb([128, 2, 3, KC], F32, "gs")
    hg = k.sb([128, 2, 3, KC], F32, "hg")
    b_in = Buf("vec_in")
    k.b_vec = Buf("vec")
    P.dma("sp", modv[:], modv_d, [], [b_in], "ld_modv")
    P.dma("sp", npre[:], npre_d, [], [b_in], "ld_npre")
    P.dma("sp", npost[:], npost_d, [], [b_in], "ld_npost")
    vec = {}
    for v in range(2):
        for s in range(3):
            P.I("dve", "scalar_tensor_tensor", [b_in], [k.b_vec], out=gs[:, v, s, :], in0=modv[:, v, 3 * s + 1, :], scalar=1.0,
                in1=npre[:, s, :], op0=ALU.add, op1=ALU.mult)
            P.I("dve", "scalar_tensor_tensor", [b_in], [k.b_vec], out=hg[:, v, s, :], in0=modv[:, v, 3 * s + 2, :],
                scalar=(1.0 if s == 1 else 0.5), in1=npost[:, s, :], op0=ALU.mult, op1=ALU.mult)
            vec[(s, v)] = dict(gs=gs[:, v, s, :], sh=modv[:, v, 3 * s, :], hg=hg[:, v, s, :])
    k.b_vec_in = b_in
    return vec


def norm_tile(k, xT_d, xkey, off, w, V, xy, b_xy, h, b_h, sq, tmpf, stat, rstd, b_rstd, rtmp, b_rtmp):
    P = k.P
    xv = xT_d[:, off:off + w].rearrange("(kc p) t -> p kc t", p=128)
    for qd in range(4):
        P.dma("sp", xy[:, 4 * qd:4 * qd + 4, :w], xv[:, 4 * qd:4 * qd + 4, :], [k.dbuf((xkey, off))], b_xy[4 * qd:4 * qd + 4], f"ld_xy{qd}")
    st, b_st = stat
    for kk in range(KC):
        s_, b_s = sq.next()
        P.I("act", "activation", [b_xy[kk]], [b_s], out=s_[:, :w], in_=xy[:, kk, :w], func=AF.Square)
        P.I("pe", "matmul", [b_s, k.b_const], [b_st], st[:, :w], lhsT=k.ones[:], rhs=s_[:, :w], start=(kk == 0), stop=(kk == KC - 1))
    emit_rstd(k, st, b_st, rstd, b_rstd, rtmp, b_rtmp, w, D)
    for kk in range(KC):
        t_, b_t = tmpf.next()
        P.I("dve", "tensor_tensor", [b_xy[kk], b_rstd], [b_t], out=t_[:, :w], in0=xy[:, kk, :w], in1=rstd[:, :w], op=ALU.mult)
        P.I("act", "activation", [b_t, k.b_vec, k.b_vec_in], [b_h[kk]], out=h[:, kk, :w], in_=t_[:, :w], func=AF.Identity,
            bias=V["sh"][:, kk:kk + 1], scale=V["gs"][:, kk:kk + 1])


def residual_update(k, xin_d, xin_key, xout_d, xout_key, off, w, V, xy, b_xy, rstd, b_rstd, tmpf, xst, xo):
    P = k.P
    lds = []

    def ld(n):
        x_, b_x = xst.next()
        P.dma("sp", x_[:, :w], xin_d[n * 128:(n + 1) * 128, off:off + w], [k.dbuf((xin_key, off))], [b_x], f"xst{(xst.i - 1) % len(xst.items)}")
        lds.append((x_, b_x))
    ld(0)
    ld(1)
    for n in range(KC):
        x_, b_x = lds[n]
        t_, b_t = tmpf.next()
        P.I("dve", "tensor_tensor", [b_xy[n], b_rstd], [b_t], out=t_[:, :w], in0=xy[:, n, :w], in1=rstd[:, :w], op=ALU.mult)
        o_, b_o = xo.next()
        oi = (xo.i - 1) % len(xo.items)
        P.I("dve", "scalar_tensor_tensor", [b_t, b_x, k.b_vec], [b_o], out=o_[:, :w], in0=t_[:, :w], scalar=V["hg"][:, n:n + 1],
            in1=x_[:, :w], op0=ALU.mult, op1=ALU.add)
        if n + 2 < KC:
            ld(n + 2)
        P.dma("sp", xout_d[n * 128:(n + 1) * 128, off:off + w], o_[:, :w], [b_o], [k.dbuf((xout_key, off))], f"xo{oi}")


def ffn_sublayer(k, xin_d, xin_key, xout_d, xout_key, w1_d, w2_d, vec, s):
    P = k.P
    with k.phase():
        xy = k.sb([128, KC, W], F32, "xy"); b_xy = [Buf() for _ in range(KC)]
        h = k.sb([128, KC, W], BF16, "h"); b_h = [Buf() for _ in range(KC)]
        G = k.sb([128, JC, W], BF16, "G"); b_G = [Buf() for _ in range(JC)]
        w1r = k.ring(2, [128, KC, 2, 256], BF16, "w1r")
        w2r = k.ring(2, [128, JC, 256], BF16, "w2r")
        sq = k.ring(2, [128, W], BF16, "sq")
        tmpf = k.ring(2, [128, W], F32, "tmpf")
        sg = k.ring(2, [128, W], F32, "sg")
        rstd = k.sb([128, W], F32, "rstd"); b_rstd = Buf()
        rtmp = k.sb([128, W], F32, "rtmp"); b_rtmp = Buf()
        xst = k.ring(3, [128, W], F32, "xst")
        xo = k.ring(2, [128, W], F32, "xo")
        gp = Ring([(k.ps([128, W]), Buf(), k.ps([128, W]), Buf()) for _ in range(2)])
        yp = k.ring(3, [128, W], F32, "yp", psum=True)
        stat = (k.ps([128, W]), Buf())
        w1v = w1_d.rearrange("(kc p) n -> p kc n", p=128)
        w2v = w2_d.rearrange("(jc p) n -> p jc n", p=128)
        k.wscr("w1", JC // 2, [KC, 2, 256]); k.wscr("w2", KC // 2, [JC, 256])
        for ti, (off, w, is_ctx) in enumerate(tiles_list()):
            first = ti == 0
            V = vec[(s, 1 if is_ctx else 0)]
            norm_tile(k, xin_d, xin_key, off, w, V, xy, b_xy, h, b_h, sq, tmpf, stat, rstd, b_rstd, rtmp, b_rtmp)
            for jp in range(JC // 2):
                wt, b_w = w1r.next()
                si = (w1r.i - 1) % 2
                k.wload("w1", jp, wt, b_w, si, first, [(wt[:, :, 0, :], w1v[:, :, jp * 256:(jp + 1) * 256]),
                                                        (wt[:, :, 1, :], w1v[:, :, DFF + jp * 256:DFF + (jp + 1) * 256])])
                for jj in range(2):
                    j = jp * 2 + jj
                    pg, b_pg, pu, b_pu = gp.next()
                    for kk in range(KC):
                        P.I("pe", "matmul", [b_w, b_h[kk]], [b_pg], pg[:, :w], lhsT=wt[:, kk, 0, jj * 128:(jj + 1) * 128],
                            rhs=h[:, kk, :w], start=(kk == 0), stop=(kk == KC - 1))
                    for kk in range(KC):
                        P.I("pe", "matmul", [b_w, b_h[kk]], [b_pu], pu[:, :w], lhsT=wt[:, kk, 1, jj * 128:(jj + 1) * 128],
                            rhs=h[:, kk, :w], start=(kk == 0), stop=(kk == KC - 1))
                    s_, b_s = sg.next()
                    P.I("act", "activation", [b_pg], [b_s], out=s_[:, :w], in_=pg[:, :w], func=AF.Silu)
                    P.I("dve", "tensor_tensor", [b_pu, b_s], [b_G[j]], out=G[:, j, :w], in0=pu[:, :w], in1=s_[:, :w], op=ALU.mult)
            st, b_st = stat
            pend_st = None
            k.wflush("w1")
            for npair in range(KC // 2):
                wt, b_w = w2r.next()
                si = (w2r.i - 1) % 2
                k.wload("w2", npair, wt, b_w, si, first, [(wt[:], w2v[:, :, npair * 256:(npair + 1) * 256])])
                for nn in range(2):
                    n = npair * 2 + nn
                    y_, b_y = yp.next()
                    for j in range(JC):
                        P.I("pe", "matmul", [b_w, b_G[j]], [b_y], y_[:, :w], lhsT=wt[:, j, nn * 128:(nn + 1) * 128],
                            rhs=G[:, j, :w], start=(j == 0), stop=(j == JC - 1))
                    if pend_st is not None:
                        pend_st()
                    P.I("dve", "tensor_copy", [b_y], [b_xy[n]], out=xy[:, n, :w], in_=y_[:, :w])
                    q_, b_q = sq.next()
                    P.I("act", "activation", [b_xy[n]], [b_q], out=q_[:, :w], in_=xy[:, n, :w], func=AF.Square)

                    def _st(n=n, q_=q_, b_q=b_q, w=w):
                        P.I("pe", "matmul", [b_q, k.b_const], [b_st], st[:, :w], lhsT=k.ones[:], rhs=q_[:, :w], start=(n == 0), stop=(n == KC - 1))
                    pend_st = _st
            pend_st()
            pend_st = None
            k.wflush("w2")
            emit_rstd(k, st, b_st, rstd, b_rstd, rtmp, b_rtmp, w, D)
            residual_update(k, xin_d, xin_key, xout_d, xout_key, off, w, V, xy, b_xy, rstd, b_rstd, tmpf, xst, xo)


def chunk_role(c):
    if c < 8: return ("q", "a", c)
    if c < 10: return ("k", "a", c - 8)
    if c < 12: return ("v", "a", c - 10)
    if c < 20: return ("q", "b", c - 12)
    if c < 22: return ("k", "b", c - 20)
    if c < 24: return ("v", "b", c - 22)
    if c < 32: return ("q", "c", c - 24)
    if c < 40: return ("k", "c", c - 32)
    if c < 48: return ("v", "c", c - 40)
    return ("g", None, c - 48)


QBASE = {"a": 0, "b": 8, "c": 16}
KBASE = {"a": 0, "b": 2, "c": 4}


def inproj_stage(k, x_d, xkey, win_d, vec, T):
    P = k.P
    with k.phase():
        xy = k.sb([128, KC, W], F32, "xy"); b_xy = [Buf() for _ in range(KC)]
        h = k.sb([128, KC, W], BF16, "h"); b_h = [Buf() for _ in range(KC)]
        wr = k.ring(3, [128, KC, 256], BF16, "wr")
        sq = k.ring(2, [128, W], BF16, "sq")
        tmpf = k.ring(2, [128, W], F32, "tmpf")
        rstd = k.sb([128, W], F32, "rstd"); b_rstd = Buf()
        rtmp = k.sb([128, W], F32, "rtmp"); b_rtmp = Buf()
        hrs = k.sb([128, W], F32, "hrs"); b_hrs = Buf()
        hrt = k.sb([128, W], F32, "hrt"); b_hrt = Buf()
        rope = k.sb([128, 4, W], F32, "rope"); b_rope = Buf()
        rmat = k.sb([128, 2, 128], F32, "rmat"); b_rmat = Buf()
        qkn = k.sb([128, 2], F32, "qkn"); b_qkn = Buf()
        qn = k.ring(2, [128, W], F32, "qn")
        t1 = k.ring(2, [128, W], F32, "t1")
        t2 = k.ring(2, [128, W], F32, "t2")
        ob = k.ring(3, [128, W], BF16, "ob")
        vt = k.ring(2, [128, 256], BF16, "vt")
        acc = k.ring(3, [128, W], F32, "acc", psum=True)
        stat = (k.ps([128, W]), Buf())
        hst = stat
        rot = k.ring(2, [128, W], F32, "rot", psum=True)
        vps = k.ring(2, [128, 256], F32, "vps", psum=True)
        P.dma("sp", rmat[:], T["rmat"].rearrange("a p m -> p a m"), [], [b_rmat], "ld_rmat")
        P.dma("sp", qkn[:], T["qkn"], [], [b_qkn], "ld_qkn")
        wv = win_d.rearrange("(kc p) n -> p kc n", p=128)
        due = []

        def step():
            cur = list(due)
            del due[:]
            for f in cur:
                f()

        def make_stage1(kind, br, idx, a_, b_a, off, w, is_ctx):
            def stage1():
                if kind == "g":
                    o_, b_o = ob.next()
                    oi = (ob.i - 1) % 3
                    P.I("act", "activation", [b_a], [b_o], out=o_[:, :w], in_=a_[:, :w], func=AF.Sigmoid)
                    P.dma("sp", T["gT"][idx * 128:(idx + 1) * 128, off:off + w], o_[:, :w], [b_o], [k.dbuf("gT")], f"ob{oi}")
                    return
                q_, b_q = qn.next()
                if br == "a":
                    s_, b_s = sq.next()
                    P.I("act", "activation", [b_a], [b_s], out=s_[:, :w], in_=a_[:, :w], func=AF.Square)
                    P.I("pe", "matmul", [b_s, k.b_const], [hst[1]], hst[0][:, :w], lhsT=k.ones[:], rhs=s_[:, :w], start=True, stop=True)
                    emit_rstd(k, hst[0], hst[1], hrs, b_hrs, hrt, b_hrt, w, HD)
                    col = 0 if kind == "q" else 1
                    P.I("dve", "scalar_tensor_tensor", [b_a, b_hrs, b_qkn], [b_q], out=q_[:, :w], in0=a_[:, :w], scalar=qkn[:, col:col + 1],
                        in1=hrs[:, :w], op0=ALU.mult, op1=ALU.mult)
                else:
                    P.I("act", "copy", [b_a], [b_q], out=q_[:, :w], in_=a_[:, :w])
                due.append(make_stage2(kind, br, idx, q_, b_q, off, w, is_ctx))
            return stage1

        def make_stage2(kind, br, idx, q_, b_q, off, w, is_ctx):
            def stage2():
                o_, b_o = ob.next()
                oi = (ob.i - 1) % 3
                if is_ctx:
                    P.I("dve", "tensor_copy", [b_q], [b_o], out=o_[:, :w], in_=q_[:, :w])
                else:
                    ri = 1 if br == "c" else 0
                    r_, b_r = rot.next()
                    P.I("pe", "matmul", [b_q, b_rmat], [b_r], r_[:, :w], lhsT=rmat[:, ri, :], rhs=q_[:, :w], start=True, stop=True)
                    a1, b_1 = t1.next()
                    a2, b_2 = t2.next()
                    P.I("dve", "tensor_tensor", [b_q, b_rope], [b_1], out=a1[:, :w], in0=q_[:, :w], in1=rope[:, 2 * ri, :w], op=ALU.mult)
                    P.I("dve", "tensor_tensor", [b_r, b_rope], [b_2], out=a2[:, :w], in0=r_[:, :w], in1=rope[:, 2 * ri + 1, :w], op=ALU.mult)
                    P.I("dve", "tensor_tensor", [b_1, b_2], [b_o], out=o_[:, :w], in0=a1[:, :w], in1=a2[:, :w], op=ALU.add)
                if kind == "q":
                    row = (QBASE[br] + idx) * 128
                    P.dma("sp", T["qT"][row:row + 128, off:off + w], o_[:, :w], [b_o], [k.dbuf("qT")], f"ob{oi}")
                else:
                    row = (KBASE[br] + idx) * 128
                    P.dma("sp", T["kT"][row:row + 128, off:off + w], o_[:, :w], [b_o], [k.dbuf("kT")], f"ob{oi}")
            return stage2

        k.wscr("win", 48, [KC, 256])
        for ti, (off, w, is_ctx) in enumerate(tiles_list()):
            first = ti == 0
            k.wflush("win")
            V = vec[(1, 1 if is_ctx else 0)]
            norm_tile(k, x_d, xkey, off, w, V, xy, b_xy, h, b_h, sq, tmpf, stat, rstd, b_rstd, rtmp, b_rtmp)
            if not is_ctx:
                P.dma("sp", rope[:, :, :w], T["rope"][:, :, off:off + w].rearrange("a p t -> p a t"), [], [b_rope], "ld_rope")
            for u in range(48):
                wt, b_w = wr.next()
                wi = (wr.i - 1) % 3
                k.wload("win", u, wt, b_w, wi, first, [(wt[:], wv[:, :, u * 256:(u + 1) * 256])])
                role0 = chunk_role(2 * u)
                if role0[0] == "v":
                    vbase = {"a": 0, "b": 2, "c": 4}[role0[1]] + role0[2]
                    for tb in range((w + 127) // 128):
                        m = min(128, w - tb * 128)
                        vp, b_vp = vps.next()
                        for kk in range(KC):
                            P.I("pe", "matmul", [b_w, b_h[kk]], [b_vp], vp[:m, :], lhsT=h[:, kk, tb * 128:tb * 128 + m], rhs=wt[:, kk, :],
                                start=(kk == 0), stop=(kk == KC - 1))
                        v_, b_v = vt.next()
                        vi = (vt.i - 1) % 2
                        P.I("act", "copy", [b_vp], [b_v], out=v_[:m, :], in_=vp[:m, :])
                        ti = (off // 128 + tb)
                        P.dma("sp", T["vimg"][vbase:vbase + 2, :m, ti, :].rearrange("c p d -> p c d"),
                              v_[:m, :].rearrange("p (c d) -> p c d", c=2), [b_v], [k.dbuf("vimg")], f"vt{vi}")
                        step()
                    continue
                for cc in range(2):
                    c = 2 * u + cc
                    kind, br, idx = chunk_role(c)
                    a_, b_a = acc.next()
                    for kk in range(KC):
                        P.I("pe", "matmul", [b_w, b_h[kk]], [b_a], a_[:, :w], lhsT=wt[:, kk, cc * 128:(cc + 1) * 128], rhs=h[:, kk, :w],
                            start=(kk == 0), stop=(kk == KC - 1))
                    step()
                    due.append(make_stage1(kind, br, idx, a_, b_a, off, w, is_ctx))
            step()
            step()
            step()


def attn_stage(k, T):
    P = k.P
    NLT = TL // 128
    with k.phase():
        Kr = k.ring(2, [128, NR, TLC], BF16, "K")
        Vr = k.ring(2, [128, NR, NTI, 128], BF16, "V")
        qr = k.ring(3, [128, W], BF16, "q")
        pr = k.ring(6, [128, W], BF16, "p")
        pm = k.ring(2, [128, W], BF16, "pm")
        bm = k.sb([128, 6, W], BF16, "bm"); hm = k.sb([128, 2, W], BF16, "hm"); b_bm = Buf()
        obr = k.ring(2, [128, W], BF16, "ob")
        f1 = k.ring(2, [128, W], F32, "f1")
        f2 = k.ring(2, [128, W], F32, "f2")
        f3 = k.ring(2, [128, W], F32, "f3")
        sqb = k.ring(2, [128, W], BF16, "sqb")
        lacc = k.ring(4, [128, W], F32, "lacc")
        rstd = k.sb([128, W], F32, "rstd"); b_rstd = Buf()
        rtmp = k.sb([128, W], F32, "rtmp"); b_rtmp = Buf()
        kh = k.sb([128, 2, 2, 128], BF16, "kh"); vh = k.sb([128, 2, 2, 128], BF16, "vh"); b_halo = Buf()
        esink = k.sb([128, 8], F32, "esink"); b_sink = Buf()
        lam = k.sb([128, 4, 64], F32, "lam"); b_lam = Buf()
        lt = k.sb([128, 2, 64], F32, "lt"); ls = k.sb([128, 2], F32, "ls"); nlam = k.sb([128, 1], F32, "nlam")
        linit = k.sb([128, 1], F32, "linit"); subln = k.sb([128, 1], F32, "subln"); sl = k.sb([128, 1], F32, "sl")
        S = k.ring(4, [128, W], F32, "S", psum=True)
        ACC = k.ring(4, [128, W], F32, "ACC", psum=True)
        scale_ab = float(HD) ** -0.5
        scale_c = float(HD // 2) ** -0.5

        P.dma("sp", esink[:], T["sink"], [], [b_sink], "ld_sink")
        P.I("act", "activation", [b_sink], [b_sink], out=esink[:], in_=esink[:], func=AF.Exp)
        P.dma("sp", lam[:], T["lam"], [], [b_lam], "ld_lam")
        P.dma("sp", linit[:], T["linit"], [], [b_lam], "ld_linit")
        P.dma("sp", subln[:], T["subln"], [], [b_lam], "ld_subln")
        P.I("dve", "tensor_tensor", [b_lam], [b_lam], out=lt[:, 0, :], in0=lam[:, 0, :], in1=lam[:, 1, :], op=ALU.mult)
        P.I("dve", "tensor_tensor", [b_lam], [b_lam], out=lt[:, 1, :], in0=lam[:, 2, :], in1=lam[:, 3, :], op=ALU.mult)
        P.I("dve", "reduce_sum", [b_lam], [b_lam], out=ls[:], in_=lt[:], axis=AX.X)
        P.I("act", "activation", [b_lam], [b_lam], out=ls[:], in_=ls[:], func=AF.Exp)
        P.I("dve", "tensor_tensor", [b_lam], [b_lam], out=nlam[:], in0=ls[:, 1:2], in1=ls[:, 0:1], op=ALU.subtract)
        P.I("dve", "tensor_tensor", [b_lam], [b_lam], out=nlam[:], in0=nlam[:], in1=linit[:], op=ALU.subtract)
        P.I("dve", "tensor_scalar", [b_lam], [b_lam], out=sl[:], in0=linit[:], scalar1=-1.0, scalar2=1.0, op0=ALU.mult, op1=ALU.add)
        P.I("dve", "tensor_tensor", [b_lam], [b_lam], out=sl[:], in0=sl[:], in1=subln[:], op=ALU.mult)
        P.dma("sp", bm[:], T["bmask"].rearrange("m p t -> p m t"), [], [b_bm], "ld_bm")
        P.dma("sp", hm[:], T["hmask"].rearrange("m p t -> p m t"), [], [b_bm], "ld_hm")
        P.dma("sp", kh[:], T["khalo"].rearrange("g s p n -> p g s n"), [], [b_halo], "ld_kh")
        P.dma("sp", vh[:], T["vhalo"].rearrange("g s p n -> p g s n"), [], [b_halo], "ld_vh")

        def load_kv(kc, vc):
            Kt, b_K = Kr.next(); ki = (Kr.i - 1) % 2
            Vt, b_V = Vr.next(); vi = (Vr.i - 1) % 2
            for r in range(NR):
                P.dma("sp", Kt[:, r, :], T["kT_all"][r, kc * 128:(kc + 1) * 128, :], [k.dbuf("kT")], [b_K], f"K{ki}")
                P.dma("sp", Vt[:, r, :, :], T["v_all"][r, vc], [k.dbuf("vimg")], [b_V], f"V{vi}")
            return Kt, b_K, Vt, b_V

        def load_q(qc, off, w):
            q_, b_q = qr.next(); qi = (qr.i - 1) % 3
            P.dma("sp", q_[:, :w], T["qT"][qc * 128:(qc + 1) * 128, off:off + w], [k.dbuf("qT")], [b_q], f"q{qi}")
            return q_, b_q

        def store_o(oc, off, w, o_, b_o):
            oi = (obr.i - 1) % 2
            P.dma("sp", T["attnT"][oc * 128:(oc + 1) * 128, off:off + w], o_[:, :w], [b_o], [k.dbuf("attnT")], f"ao{oi}")

        def all_keys(is_ctx):
            ks = []
            if not is_ctx:
                for r in range(NR):
                    for t in range(NLT):
                        ks.append((r, t, 128))
            for r in range(NR):
                for ct in range(NCT):
                    ks.append((r, NLT + ct, min(128, CTXL - ct * 128)))
            return ks

        def softmax_loop(keys, w, scale, O, b_O, L, b_L):
            n = len(keys)
            la, b_la = lacc.next()
            P.I("pool", "memset", [], [b_la], la[:, :w], 0.0)
            LA = 3

            def issue_s(i):
                kd = keys[i]
                s_, b_s = S.next()
                P.I("pe", "matmul", kd["kbufs"] + [kd["q"][1]], [b_s], s_[:kd["nk"], :w], lhsT=kd["kT"], rhs=kd["q"][0], start=True, stop=True)
                return s_, b_s
            sq_ = [issue_s(i) for i in range(min(LA, n))]
            for i, kd in enumerate(keys):
                nk = kd["nk"]
                s_, b_s = sq_[i]
                p_, b_p = pr.next()
                P.I("act", "activation", [b_s], [b_p], out=p_[:nk, :w], in_=s_[:nk, :w], func=AF.Exp, scale=scale)
                if i + LA < n:
                    sq_.append(issue_s(i + LA))
                if kd.get("mask") is not None:
                    m_, b_m = kd["mask"]
                    p2, b_p2 = pm.next()
                    P.I("dve", "tensor_tensor", [b_p, b_m], [b_p2], out=p2[:nk, :w], in0=p_[:nk, :w], in1=m_[:nk, :w], op=ALU.mult)
                    p_, b_p = p2, b_p2
                P.I("pe", "matmul", kd["vbufs"] + [b_p], [b_O], O[:, :w], lhsT=kd["v"], rhs=p_[:nk, :w], start=(i == 0), stop=(i == n - 1))
                if i % 2 == 0:
                    P.I("pe", "matmul", [k.b_const, b_p], [b_L], L[:, :w], lhsT=k.ones[:nk, :], rhs=p_[:nk, :w], start=(i == 0), stop=False)
                else:
                    P.I("dve", "tensor_tensor", [b_la, b_p], [b_la], out=la[:nk, :w], in0=la[:nk, :w], in1=p_[:nk, :w], op=ALU.add)
            P.I("pe", "matmul", [k.b_const, b_la], [b_L], L[:, :w], lhsT=k.ones32[:], rhs=la[:, :w], start=False, stop=True)

        for g in range(2):
            Kt, b_K, Vt, b_V = load_kv(g, g)
            for hh in range(4):
                hd = 4 * g + hh
                for (off, w, is_ctx) in tiles_list():
                    q_, b_q = load_q(QBASE["a"] + hd, off, w)
                    keys = []
                    for (r, t, nk) in all_keys(is_ctx):
                        keys.append(dict(kT=Kt[:, r, t * 128:t * 128 + nk], kbufs=[b_K], v=Vt[:nk, r, t, :], vbufs=[b_V], nk=nk,
                                         q=(q_[:, :w], b_q)))
                    O, b_O = ACC.next(); L, b_L = ACC.next()
                    softmax_loop(keys, w, scale_ab, O, b_O, L, b_L)
                    r_, b_r = f1.next()
                    P.I("dve", "reciprocal", [b_L], [b_r], out=r_[:, :w], in_=L[:, :w])
                    o_, b_o = obr.next()
                    P.I("dve", "tensor_tensor", [b_O, b_r], [b_o], out=o_[:, :w], in0=O[:, :w], in1=r_[:, :w], op=ALU.mult)
                    store_o(hd, off, w, o_, b_o)

        Kb = k.sb([128, TL], BF16, "Kb"); Vb = k.sb([128, NLT, 128], BF16, "Vb")
        Kbc = k.sb([128, NR, CTXL], BF16, "Kbc"); Vbc = k.sb([128, NR, NCT, 128], BF16, "Vbc")
        b_Kb = Buf(); b_Vb = Buf()
        for g in range(2):
            P.dma("sp", Kb[:], T["kT"][(2 + g) * 128:(3 + g) * 128, 0:TL], [k.dbuf("kT")], [b_Kb], "Kb")
            P.dma("sp", Vb[:], T["vimg"][2 + g, :, 0:NLT, :], [k.dbuf("vimg")], [b_Vb], "Vb")
            for r in range(NR):
                P.dma("sp", Kbc[:, r, :], T["kT_all"][r, (2 + g) * 128:(3 + g) * 128, TL:TLC], [k.dbuf("kT")], [b_Kb], "Kb")
                P.dma("sp", Vbc[:, r, :, :], T["v_all"][r, 2 + g, :, NLT:NLT + NCT, :], [k.dbuf("vimg")], [b_Vb], "Vb")
            for hh in range(4):
                hd = 4 * g + hh
                for ti, (off, w, is_ctx) in enumerate(tiles_list()):
                    q_, b_q = load_q(QBASE["b"] + hd, off, w)
                    qq = (q_[:, :w], b_q)
                    keys = []
                    if not is_ctx:
                        t0 = off // 128
                        for rel in range(-1, 5):
                            kt = t0 + rel
                            if kt < 0 or kt >= NLT:
                                continue
                            keys.append(dict(kT=Kb[:, kt * 128:(kt + 1) * 128], kbufs=[b_Kb], v=Vb[:, kt, :], vbufs=[b_Vb], nk=128,
                                             mask=(bm[:, rel + 1, :], b_bm), q=qq))
                        for side, cond in ((0, off == 0), (1, off + w == TL)):
                            if cond and NR > 1:
                                keys.append(dict(kT=kh[:, g, side, :], kbufs=[b_halo], v=vh[:, g, side, :], vbufs=[b_halo], nk=128,
                                                 mask=(hm[:, side, :], b_bm), q=qq))
                    for r in range(NR):
                        for ct in range(NCT):
                            nk = min(128, CTXL - ct * 128)
                            keys.append(dict(kT=Kbc[:, r, ct * 128:ct * 128 + nk], kbufs=[b_Kb], v=Vbc[:nk, r, ct, :], vbufs=[b_Vb], nk=nk, q=qq))
                    O, b_O = ACC.next(); L, b_L = ACC.next()
                    softmax_loop(keys, w, scale_ab, O, b_O, L, b_L)
                    r_, b_r = f1.next()
                    P.I("dve", "tensor_scalar", [b_L, b_sink], [b_r], out=r_[:, :w], in0=L[:, :w], scalar1=esink[:, hd:hd + 1], scalar2=None,
                        op0=ALU.add)
                    r2, b_r2 = f2.next()
                    P.I("dve", "reciprocal", [b_r], [b_r2], out=r2[:, :w], in_=r_[:, :w])
                    o_, b_o = obr.next()
                    P.I("dve", "tensor_tensor", [b_O, b_r2], [b_o], out=o_[:, :w], in0=O[:, :w], in1=r2[:, :w], op=ALU.mult)
                    store_o(8 + hd, off, w, o_, b_o)

        for hd in range(8):
            Kt, b_K, Vt, b_V = load_kv(4 + hd, 4 + hd)
            for (off, w, is_ctx) in tiles_list():
                q_, b_q = load_q(QBASE["c"] + hd, off, w)
                accs = [ACC.next() for _ in range(4)]
                for half in range(2):
                    lo, hi = half * 64, half * 64 + 64
                    keys = []
                    for (r, t, nk) in all_keys(is_ctx):
                        keys.append(dict(kT=Kt[lo:hi, r, t * 128:t * 128 + nk], kbufs=[b_K], v=Vt[:nk, r, t, :], vbufs=[b_V], nk=nk,
                                         q=(q_[lo:hi, :w], b_q)))
                    (O, b_O), (L, b_L) = accs[2 * half], accs[2 * half + 1]
                    softmax_loop(keys, w, scale_c, O, b_O, L, b_L)
                (O1, b_O1), (L1, b_L1), (O2, b_O2), (L2, b_L2) = accs
                r1, b_r1 = f1.next()
                P.I("dve", "reciprocal", [b_L1], [b_r1], out=r1[:, :w], in_=L1[:, :w])
                a1, b_a1 = f2.next()
                P.I("dve", "tensor_tensor", [b_O1, b_r1], [b_a1], out=a1[:, :w], in0=O1[:, :w], in1=r1[:, :w], op=ALU.mult)
                r2, b_r2 = f1.next()
                P.I("dve", "reciprocal", [b_L2], [b_r2], out=r2[:, :w], in_=L2[:, :w])
                a2, b_a2 = f2.next()
                P.I("dve", "tensor_tensor", [b_O2, b_r2], [b_a2], out=a2[:, :w], in0=O2[:, :w], in1=r2[:, :w], op=ALU.mult)
                o3, b_o3 = f3.next()
                P.I("dve", "scalar_tensor_tensor", [b_a2, b_a1, b_lam], [b_o3], out=o3[:, :w], in0=a2[:, :w], scalar=nlam[:, 0:1], in1=a1[:, :w],
                    op0=ALU.mult, op1=ALU.add)
                s_, b_s = sqb.next()
                P.I("act", "activation", [b_o3], [b_s], out=s_[:, :w], in_=o3[:, :w], func=AF.Square)
                st, b_st = S.next()
                P.I("pe", "matmul", [b_s, k.b_const], [b_st], st[:, :w], lhsT=k.ones[:], rhs=s_[:, :w], start=True, stop=True)
                emit_rstd(k, st, b_st, rstd, b_rstd, rtmp, b_rtmp, w, HD)
                o_, b_o = obr.next()
                P.I("dve", "scalar_tensor_tensor", [b_o3, b_rstd, b_lam], [b_o], out=o_[:, :w], in0=o3[:, :w], scalar=sl[:, 0:1], in1=rstd[:, :w],
                    op0=ALU.mult, op1=ALU.mult)
                store_o(16 + hd, off, w, o_, b_o)


def merge_stage(k, xin_d, xin_key, xout_d, xout_key, wbr_d, wout_d, vec, T):
    P = k.P
    with k.phase():
        at = k.sb([128, 24, W], BF16, "at"); b_at = Buf()
        gt = k.sb([128, 48, W], BF16, "gt"); b_gt = Buf()
        ym = k.sb([128, KC, W], BF16, "ym"); b_ym = [Buf() for _ in range(KC)]
        xy = k.sb([128, KC, W], F32, "xy"); b_xy = [Buf() for _ in range(KC)]
        wbr = k.ring(2, [128, 3, 8, 256], BF16, "wbr")
        wor = k.ring(2, [128, KC, 256], BF16, "wor")
        sq = k.ring(2, [128, W], BF16, "sq")
        tmpf = k.ring(2, [128, W], F32, "tmpf")
        m1 = k.ring(2, [128, W], F32, "m1")
        m2 = k.ring(2, [128, W], F32, "m2")
        rstd = k.sb([128, W], F32, "rstd"); b_rstd = Buf()
        rtmp = k.sb([128, W], F32, "rtmp"); b_rtmp = Buf()
        xst = k.ring(3, [128, W], F32, "xst")
        xo = k.ring(2, [128, W], F32, "xo")
        bp = k.ring(6, [128, W], F32, "bp", psum=True)
        stat = (k.ps([128, W]), Buf())
        wbv = wbr_d.rearrange("r (kc p) n -> p r kc n", p=128)
        wov = wout_d.rearrange("(kc p) n -> p kc n", p=128)
        k.wscr("wbr", KC // 2, [3, 8, 256]); k.wscr("wor", KC // 2, [KC, 256])
        for ti, (off, w, is_ctx) in enumerate(tiles_list()):
            first = ti == 0
            V = vec[(1, 1 if is_ctx else 0)]
            P.dma("sp", at[:, :, :w], T["attnT"][:, off:off + w].rearrange("(c p) t -> p c t", p=128), [k.dbuf("attnT")], [b_at], "ld_at")
            P.dma("sp", gt[:, :, :w], T["gT"][:, off:off + w].rearrange("(c p) t -> p c t", p=128), [k.dbuf("gT")], [b_gt], "ld_gt")
            for npair in range(KC // 2):
                wt, b_w = wbr.next(); wi = (wbr.i - 1) % 2
                k.wload("wbr", npair, wt, b_w, wi, first, [(wt[:, r, :, :], wbv[:, r, :, npair * 256:(npair + 1) * 256]) for r in range(3)])
                for nn in range(2):
                    n = npair * 2 + nn
                    pss = [bp.next() for _ in range(3)]
                    for r in range(3):
                        p_, b_p = pss[r]
                        for kk in range(8):
                            P.I("pe", "matmul", [b_w, b_at], [b_p], p_[:, :w], lhsT=wt[:, r, kk, nn * 128:(nn + 1) * 128], rhs=at[:, r * 8 + kk, :w],
                                start=(kk == 0), stop=(kk == 7))
                    a_, b_a = m1.next()
                    P.I("dve", "tensor_tensor", [pss[0][1], b_gt], [b_a], out=a_[:, :w], in0=pss[0][0][:, :w], in1=gt[:, n, :w], op=ALU.mult)
                    c_, b_c = m2.next()
                    P.I("dve", "tensor_tensor", [pss[1][1], b_gt], [b_c], out=c_[:, :w], in0=pss[1][0][:, :w], in1=gt[:, 16 + n, :w], op=ALU.mult)
                    P.I("dve", "tensor_tensor", [b_a, b_c], [b_a], out=a_[:, :w], in0=a_[:, :w], in1=c_[:, :w], op=ALU.add)
                    c2, b_c2 = m2.next()
                    P.I("dve", "tensor_tensor", [pss[2][1], b_gt], [b_c2], out=c2[:, :w], in0=pss[2][0][:, :w], in1=gt[:, 32 + n, :w], op=ALU.mult)
                    P.I("dve", "tensor_tensor", [b_a, b_c2], [b_ym[n]], out=ym[:, n, :w], in0=a_[:, :w], in1=c2[:, :w], op=ALU.add)
            k.wflush("wbr")
            st, b_st = stat
            for npair in range(KC // 2):
                wt, b_w = wor.next(); wi = (wor.i - 1) % 2
                k.wload("wor", npair, wt, b_w, wi, first, [(wt[:], wov[:, :, npair * 256:(npair + 1) * 256])])
                for nn in range(2):
                    n = npair * 2 + nn
                    y_, b_y = bp.next()
                    for kk in range(KC):
                        P.I("pe", "matmul", [b_w, b_ym[kk]], [b_y], y_[:, :w], lhsT=wt[:, kk, nn * 128:(nn + 1) * 128], rhs=ym[:, kk, :w],
                            start=(kk == 0), stop=(kk == KC - 1))
                    P.I("dve", "tensor_copy", [b_y], [b_xy[n]], out=xy[:, n, :w], in_=y_[:, :w])
                    q_, b_q = sq.next()
                    P.I("act", "activation", [b_xy[n]], [b_q], out=q_[:, :w], in_=xy[:, n, :w], func=AF.Square)
                    P.I("pe", "matmul", [b_q, k.b_const], [b_st], st[:, :w], lhsT=k.ones[:], rhs=q_[:, :w], start=(n == 0), stop=(n == KC - 1))
            k.wflush("wor")
            emit_rstd(k, st, b_st, rstd, b_rstd, rtmp, b_rtmp, w, D)
            residual_update(k, xin_d, xin_key, xout_d, xout_key, off, w, V, xy, b_xy, rstd, b_rstd, tmpf, xst, xo)


def new_k():
    nc = bass.Bass("TRN2", target_bir_lowering=False)
    es = contextlib.ExitStack()
    k = K(nc, es)
    k.es_cur = es
    return k


def mod_stage(k, cT_d, wada_d, bada_d, mo, b_mo):
    P = k.P
    NCH = 9 * KC
    with k.phase():
        cs = k.sb([128, KC, 2], F32, "cs"); b_cs = Buf()
        bs = k.sb([128, DEPTH, NCH], F32, "bs"); b_bs = Buf()
        wr = k.ring(4, [128, KC, 256], F32, "wr")
        pp = k.ring(4, [128, 2], F32, "pp", psum=True)
        P.dma("sp", cs[:], cT_d, [], [b_cs], "ld_c")
        P.dma("sp", bs[:], bada_d, [], [b_bs], "ld_b")
        P.I("act", "activation", [b_cs], [b_cs], out=cs[:], in_=cs[:], func=AF.Silu)
        for l in range(DEPTH):
            wv = wada_d[l].rearrange("(kc p) n -> p kc n", p=128)
            for u in range(NCH // 2):
                wt, b_w = wr.next(); wi = (wr.i - 1) % 4
                P.dma("sp", wt[:], wv[:, :, u * 256:(u + 1) * 256], [], [b_w], f"wa{wi}")
                for cc in range(2):
                    c = 2 * u + cc
                    p_, b_p = pp.next()
                    for kk in range(KC):
                        P.I("pe", "matmul", [b_w, b_cs], [b_p], p_[:, 0:2], lhsT=wt[:, kk, cc * 128:(cc + 1) * 128], rhs=cs[:, kk, :],
                            start=(kk == 0), stop=(kk == KC - 1))
                    P.I("dve", "tensor_scalar", [b_p, b_bs], [b_mo], out=mo[:, l, c, :], in0=p_[:, 0:2], scalar1=bs[:, l, c:c + 1], scalar2=None,
                        op0=ALU.add)


def layer_vecs(k, l, mo, b_mo, npre, npost, b_np, gs, hg):
    P = k.P
    vec = {}
    for v in range(2):
        for s in range(3):
            def m(i):
                return mo[:, l, i * KC:(i + 1) * KC, v]
            P.I("dve", "scalar_tensor_tensor", [b_mo, b_np], [k.b_vec], out=gs[:, v, s, :], in0=m(3 * s + 1), scalar=1.0,
                in1=npre[:, l, s, :], op0=ALU.add, op1=ALU.mult)
            P.I("dve", "scalar_tensor_tensor", [b_mo, b_np], [k.b_vec], out=hg[:, v, s, :], in0=m(3 * s + 2),
                scalar=(1.0 if s == 1 else 0.5), in1=npost[:, l, s, :], op0=ALU.mult, op1=ALU.mult)
            vec[(s, v)] = dict(gs=gs[:, v, s, :], sh=m(3 * s), hg=hg[:, v, s, :])
    return vec


def build_fused():
    k = new_k(); nc = k.nc; P = k.P
    xin = k.dram_in("xT", [D, TLC], F32)
    cT = k.dram_in("cT", [128, KC, 2], F32)
    wada = k.dram_in("wada", [DEPTH, D, 9 * D], F32)
    bada = k.dram_in("bada", [128, DEPTH, 9 * KC], F32)
    npre_d = k.dram_in("npre", [128, DEPTH, 3, KC], F32)
    npost_d = k.dram_in("npost", [128, DEPTH, 3, KC], F32)
    wffin = k.dram_in("wffin", [DEPTH, 2, D, 2 * DFF], F32)
    wffout = k.dram_in("wffout", [DEPTH, 2, DFF, D], F32)
    win = k.dram_in("win", [DEPTH, D, INW], F32)
    wbr = k.dram_in("wbr", [DEPTH, 3, 1024, D], F32)
    wout = k.dram_in("wout", [DEPTH, D, D], F32)
    qkn = k.dram_in("qkn", [DEPTH, 128, 2], F32)
    sink = k.dram_in("sink", [DEPTH, 128, 8], F32)
    lam = k.dram_in("lam", [DEPTH, 128, 4, 64], F32)
    linit = k.dram_in("linit", [DEPTH, 128, 1], F32)
    subln = k.dram_in("subln", [DEPTH, 128, 1], F32)
    rope = k.dram_in("rope", [4, 128, TL], F32)
    rmat = k.dram_in("rmat", [2, 128, 128], F32)
    bmask = k.dram_in("bmask", [6, 128, W], BF16)
    hmask = k.dram_in("hmask", [2, 128, W], BF16)
    khalo = k.dram_in("khalo", [2, 2, 128, 128], BF16)
    vhalo = k.dram_in("vhalo", [2, 2, 128, 128], BF16)
    xout = k.dram_out("xout", [D, TLC], F32)

    def scratch(name, shape, dt):
        return nc.dram_tensor(name, list(shape), dt, kind="Internal").ap()
    xs = [scratch("xs0", [D, TLC], F32), scratch("xs1", [D, TLC], F32)]
    kT = scratch("kT", [NR, NKC * 128, TLC], BF16)
    vimg = scratch("vimg", [NR, NVC, 128, NTI, 128], BF16)
    Tb = dict(rope=rope, rmat=rmat, bmask=bmask, hmask=hmask, khalo=khalo, vhalo=vhalo,
              qT=scratch("qT", [NQC * 128, TLC], BF16), gT=scratch("gT", [48 * 128, TLC], BF16),
              attnT=scratch("attnT", [24 * 128, TLC], BF16), kT=kT[0], vimg=vimg[0], kT_all=kT, v_all=vimg)

    setup_consts(k)
    mo = k.sb([128, DEPTH, 9 * KC, 2], F32, "mo"); b_mo = Buf("mo")
    npre = k.sb([128, DEPTH, 3, KC], F32, "npre"); npost = k.sb([128, DEPTH, 3, KC], F32, "npost"); b_np = Buf("np")
    gs = k.sb([128, 2, 3, KC], F32, "gs"); hg = k.sb([128, 2, 3, KC], F32, "hg")
    k.b_vec = Buf("vec"); k.b_vec_in = b_mo
    P.dma("sp", npre[:], npre_d, [], [b_np], "ld_npre")
    P.dma("sp", npost[:], npost_d, [], [b_np], "ld_npost")
    mod_stage(k, cT, wada, bada, mo, b_mo)

    seq = []
    nsub = 3 * DEPTH
    for i in range(nsub + 1):
        if i == 0:
            seq.append((xin, "xin"))
        elif i == nsub:
            seq.append((xout, "xout"))
        else:
            seq.append((xs[(i - 1) % 2], f"xs{(i - 1) % 2}"))
    si = 0
    for l in range(DEPTH):
        vec = layer_vecs(k, l, mo, b_mo, npre, npost, b_np, gs, hg)
        T = dict(Tb)
        T.update(qkn=qkn[l], sink=sink[l], lam=lam[l], linit=linit[l], subln=subln[l])
        (a, ak), (b, bk) = seq[si], seq[si + 1]
        ffn_sublayer(k, a, ak, b, bk, wffin[l, 0], wffout[l, 0], vec, 0)
        si += 1
        inproj_stage(k, b, bk, win[l], vec, T)
        attn_stage(k, T)
        (a, ak), (b, bk) = seq[si], seq[si + 1]
        merge_stage(k, a, ak, b, bk, wbr[l], wout[l], vec, T)
        si += 1
        (a, ak), (b, bk) = seq[si], seq[si + 1]
        ffn_sublayer(k, a, ak, b, bk, wffin[l, 1], wffout[l, 1], vec, 2)
        si += 1
    st = k.P.finish()
    k.es.close()
    k.stats = st
    return nc


def fm(v):
    v = np.asarray(v)
    lead = v.shape[:-1]
    return np.ascontiguousarray(np.moveaxis(v.reshape(*lead, KC, 128), -1, 0))


def rope_consts(tok0):
    def tables(rot_dim):
        pos = tok0 + np.arange(TL)
        row = (pos // GRID_W).astype(np.float32)
        col = (pos % GRID_W).astype(np.float32)
        d_ax = rot_dim // 2
        inv = (10000.0 ** (-np.arange(0, d_ax, 2, dtype=np.float32) / d_ax)).astype(np.float32)
        ang_r = row[:, None] * inv[None, :]
        ang_c = col[:, None] * inv[None, :]
        ang = np.concatenate([ang_r, ang_r, ang_c, ang_c], axis=-1).astype(np.float32)
        return np.cos(ang).T.astype(np.float32), np.sin(ang).T.astype(np.float32)
    cab, sab = tables(128)
    cc, sc = tables(64)
    return np.ascontiguousarray(np.stack([cab, sab, np.concatenate([cc, cc], 0), np.concatenate([sc, sc], 0)], 0))


def rot_mats():
    def rt(dim, n):
        q = dim // 4
        m = np.zeros((n, n), np.float32)
        for base in range(0, n, dim):
            for i in range(dim):
                quarter = i // q
                if quarter == 0: src, sgn = i + q, -1.0
                elif quarter == 1: src, sgn = i - q, 1.0
                elif quarter == 2: src, sgn = i + q, -1.0
                else: src, sgn = i - q, 1.0
                m[base + src, base + i] = sgn
        return m
    return np.stack([rt(128, 128), rt(64, 128)], 0)


def band_masks():
    j = np.arange(128)[:, None]
    i = np.arange(W)[None, :]
    ms = []
    for rel in range(-1, 5):
        ms.append((np.abs(rel * 128 + j - i) <= 128))
    return np.stack(ms, 0)


_CACHE = {}


def get_prog(name, builder):
    key = (name, SEQ, DEPTH)
    if key not in _CACHE:
        _CACHE[key] = builder()
    return _CACHE[key]


def kernel(x, c, ctx, c_ctx, w_ada, b_ada, norm_pre, norm_post, w_ff_in, w_ff_out, w_in,
           qk_norm_a, sink_b, lam_c, subln_c, w_branch, w_out):
    f32 = np.float32
    A = lambda v: np.ascontiguousarray(np.asarray(v, f32))
    x = A(x); ctx = A(ctx); c = A(c); c_ctx = A(c_ctx)
    depth = np.asarray(w_ada).shape[0]
    assert depth == DEPTH and x.shape[1] == SEQ
    nb = x.shape[0]
    cores = list(range(nb))
    nc = get_prog("fused", build_fused)
    bc = lambda v, shape: np.ascontiguousarray(np.broadcast_to(v, shape))
    linit = np.array([0.8 - 0.6 * float(np.exp(-0.3 * l)) for l in range(depth)], f32)
    shared = dict(
        wada=A(w_ada), bada=np.ascontiguousarray(A(b_ada).reshape(depth, 9 * KC, 128).transpose(2, 0, 1)),
        npre=fm(A(norm_pre)), npost=fm(A(norm_post)),
        wffin=A(w_ff_in), wffout=A(w_ff_out), win=A(w_in), wbr=A(w_branch), wout=A(w_out),
        qkn=np.ascontiguousarray(A(qk_norm_a).transpose(0, 2, 1)),
        sink=bc(A(sink_b)[:, None, :], (depth, 128, 8)), lam=bc(A(lam_c)[:, None], (depth, 128, 4, 64)),
        linit=bc(linit[:, None, None], (depth, 128, 1)), subln=np.ascontiguousarray(A(subln_c).reshape(depth, 128, 1)),
        rope=rope_consts(0), rmat=rot_mats(), bmask=band_masks().astype(NPBF),
        hmask=np.zeros((2, 128, W), NPBF), khalo=np.zeros((2, 2, 128, 128), NPBF), vhalo=np.zeros((2, 2, 128, 128), NPBF))
    in_maps = []
    for b in cores:
        m = dict(shared)
        m["xT"] = np.ascontiguousarray(np.concatenate([x[b], ctx[b]], 0).T)
        m["cT"] = np.ascontiguousarray(np.stack([c[b], c_ctx], 0).reshape(2, KC, 128).transpose(2, 1, 0))
        in_maps.append(m)
    res = run_bass_kernel_spmd(nc, in_maps, core_ids=cores).results
    out = np.zeros((nb, SEQ, D), f32)
    for b in cores:
        out[b] = np.asarray(res[b]["xout"])[:, :TL].T
    return out
```

```python
import contextlib
import numpy as np
import ml_dtypes
import concourse.bass as bass
import concourse.mybir as mybir
from concourse.bass_utils import run_bass_kernel_spmd

F32 = mybir.dt.float32
BF16 = mybir.dt.bfloat16
AF = mybir.ActivationFunctionType
ALU = mybir.AluOpType
AX = mybir.AxisListType
NPBF = ml_dtypes.bfloat16

D = 2048
KC = 16
DFF = 5632
JC = 44
EPS = 1e-6
DEPTH = 4
NCORE = 2
NR = 1
SEQ = 8192
TL = SEQ // NR
CTX = 256
CTXL = CTX // NR
NCT = (CTXL + 127) // 128
TLC = TL + CTXL
GRID_W = 64
HD = 128
INW = 12288
W = 512
NQC = 24
NKC = 12
NVC = 12
NTI = TL // 128 + NCT
DEBUG = False


def set_scale(seq):
    global SEQ, TL, TLC, NTI
    SEQ = seq
    TL = SEQ // NR
    TLC = TL + CTXL
    NTI = TL // 128 + NCT


class Buf:
    __slots__ = ("name", "last_w", "readers")

    def __init__(self, name=""):
        self.name = name
        self.last_w = None
        self.readers = {}


class Prog:
    ENGINES = ("pe", "act", "dve", "pool", "sp")

    def __init__(self, nc, es):
        self.nc = nc
        self.es = es
        self.ops = []
        self.emap = {"pe": nc.tensor, "act": nc.scalar, "dve": nc.vector, "pool": nc.gpsimd, "sp": nc.sync}
        self.base = 0
        self.chan = []
        self.val = []
        self.waited = {}
        self.last_on_chan = {}
        self.count = {}
        self.sems = {}
        self.n_ops = 0

    def I(self, engine, method, reads, writes, *args, **kw):
        f = getattr(self.emap[engine], method)
        self.ops.append((engine, (lambda: f(*args, **kw)), tuple(reads), tuple(writes), None))

    def dma(self, engine, out, in_, reads, writes, stream):
        f = self.emap[engine].dma_start
        self.ops.append((engine, (lambda: f(out=out, in_=in_)), tuple(reads), tuple(writes), stream))

    def barrier(self):
        self.ops.append(("barrier", None, (), (), None))

    def flush(self):
        nc = self.nc
        ops = self.ops
        n = len(ops)
        base = self.base
        chan = self.chan
        val = self.val
        waited = self.waited
        last_on_chan = self.last_on_chan
        for o in ops:
            chan.append(o[4] if o[4] is not None else o[0])
            val.append(0)
        waits = [None] * n
        signal = [False] * n
        for li, (eng, emit, r, w, stream) in enumerate(ops):
            i = base + li
            if eng == "barrier":
                wl = {}
                for c, j in last_on_chan.items():
                    need = False
                    for e in self.ENGINES:
                        if waited.get((e, c), -1) < j:
                            waited[(e, c)] = j
                            need = True
                    if need:
                        wl[c] = j
                        if j >= base:
                            signal[j - base] = True
                waits[li] = wl
                continue
            deps = {}
            for b in r:
                j = b.last_w
                if j is not None:
                    c = chan[j]
                    if deps.get(c, -1) < j:
                        deps[c] = j
            for b in w:
                j = b.last_w
                if j is not None:
                    c = chan[j]
                    if deps.get(c, -1) < j:
                        deps[c] = j
                for c, j in b.readers.items():
                    if deps.get(c, -1) < j:
                        deps[c] = j
            my = chan[i]
            wl = None
            for c, j in deps.items():
                if c == my and (c == "pe" or stream is not None):
                    continue
                if waited.get((eng, c), -1) >= j:
                    continue
                waited[(eng, c)] = j
                assert j >= base, "dependency on an un-signalled op from a previous phase"
                signal[j - base] = True
                if wl is None:
                    wl = {}
                wl[c] = j
            waits[li] = wl
            for b in r:
                b.readers[my] = i
            for b in w:
                b.last_w = i
                b.readers = {}
            last_on_chan[my] = i
        count = self.count
        for li, (eng, emit, r, w, stream) in enumerate(ops):
            if eng == "barrier":
                continue
            i = base + li
            c = chan[i]
            if stream is not None:
                count[c] = count.get(c, 0) + 16
                val[i] = count[c]
            elif signal[li]:
                count[c] = count.get(c, 0) + 1
                val[i] = count[c]
        sems = self.sems
        for c in count:
            if c not in sems:
                sems[c] = self.es.enter_context(nc.semaphore("s_" + c))
        for li, (eng, emit, r, w, stream) in enumerate(ops):
            i = base + li
            if eng == "barrier":
                for e in self.ENGINES:
                    ee = self.emap[e]
                    for c, j in waits[li].items():
                        ee.wait_ge(sems[c], val[j])
                continue
            e = self.emap[eng]
            if waits[li]:
                for c, j in waits[li].items():
                    e.wait_ge(sems[c], val[j])
            ins = emit()
            if stream is not None:
                ins.then_inc(sems[chan[i]], 16)
            elif signal[li]:
                ins.then_inc(sems[chan[i]], 1)
        self.base += n
        self.n_ops += n
        self.ops = []

    def finish(self):
        self.barrier()
        self.flush()
        for c, v in self.count.items():
            self.nc.sync.wait_ge(self.sems[c], v)
        return dict(n_ops=self.n_ops, n_sems=len(self.sems))


class Ring:
    def __init__(self, items):
        self.items = items
        self.i = 0

    def next(self):
        it = self.items[self.i % len(self.items)]
        self.i += 1
        return it


class K:
    def __init__(self, nc, es):
        self.nc = nc
        self.es = es
        self.P = Prog(nc, es)
        self.dbufs = {}
        self.uid = 0

    def sb(self, shape, dt, name=None):
        self.uid += 1
        return self.es_cur.enter_context(self.nc.sbuf_tensor(f"{name or 't'}_{self.uid}", list(shape), dt))

    def ps(self, shape, dt=F32, name=None):
        self.uid += 1
        return self.es_cur.enter_context(self.nc.psum_tensor(f"{name or 'p'}_{self.uid}", list(shape), dt))

    def ring(self, n, shape, dt, name=None, psum=False):
        f = self.ps if psum else self.sb
        return Ring([(f(shape, dt, name), Buf(name or "")) for _ in range(n)])

    def dbuf(self, key):
        b = self.dbufs.get(key)
        if b is None:
            b = self.dbufs[key] = Buf(str(key))
        return b

    def wscr(self, name, n_units, unit_shape):
        d = self.__dict__.setdefault("_wscr", {})
        if name not in d:
            d[name] = self.nc.dram_tensor("wscr_" + name, [n_units, 128] + list(unit_shape), BF16, kind="Internal").ap()
        return d[name]

    def wload(self, name, u, slot, b_slot, si, first, cast_parts):
        P = self.P
        scr = self._wscr[name]
        wb = self.dbuf(("wscr", name, u))
        if first:
            for dst, src in cast_parts:
                P.dma("pool", dst, src, [], [b_slot], f"{name}_{si}")
            pend = self.__dict__.setdefault("_wpend", {})
            prev = pend.get(name)
            if prev is not None:
                prev()
            pend[name] = lambda: P.dma("pool", scr[u], slot[:], [b_slot], [wb], f"{name}s_{si}")
        else:
            self.wflush(name)
            P.dma("pool", slot[:], scr[u], [wb], [b_slot], f"{name}_{si}")

    def wflush(self, name):
        pend = self.__dict__.setdefault("_wpend", {})
        prev = pend.pop(name, None)
        if prev is not None:
            prev()

    def dram_in(self, name, shape, dt):
        return self.nc.dram_tensor(name, list(shape), dt, kind="ExternalInput").ap()

    def dram_out(self, name, shape, dt):
        return self.nc.dram_tensor(name, list(shape), dt, kind="ExternalOutput").ap()

    @contextlib.contextmanager
    def phase(self):
        with contextlib.ExitStack() as pes:
            old = getattr(self, "es_cur", None)
            self.es_cur = pes
            yield
            self.P.barrier()
            self.P.flush()
            self.es_cur = old


def tiles_list():
    return [(t * W, W, False) for t in range(TL // W)] + [(TL, CTXL, True)]


def emit_rstd(k, stat_ps, b_stat, rstd, b_rstd, tmp, b_tmp, w, nfeat):
    P = k.P
    P.I("dve", "tensor_scalar", [b_stat], [b_tmp], out=tmp[:, :w], in0=stat_ps[:, :w], scalar1=1.0 / nfeat, scalar2=EPS,
        op0=ALU.mult, op1=ALU.add)
    P.I("act", "sqrt", [b_tmp], [b_tmp], out=tmp[:, :w], in_=tmp[:, :w])
    P.I("dve", "reciprocal", [b_tmp], [b_rstd], out=rstd[:, :w], in_=tmp[:, :w])


def setup_consts(k):
    P = k.P
    k.ones = k.sb([128, 128], BF16, "ones")
    k.b_const = Buf("const")
    P.I("dve", "memset", [], [k.b_const], k.ones[:], 1.0)
    k.ones32 = k.sb([128, 128], F32, "ones32")
    P.I("dve", "memset", [], [k.b_const], k.ones32[:], 1.0)


def setup_vecs(k, modv_d, npre_d, npost_d):
    P = k.P
    modv = k.sb([128, 2, 9, KC], F32, "modv")
    npre = k.sb([128, 3, KC], F32, "npre")
    npost = k.sb([128, 3, KC], F32, "npost")
    gs = k.sb([128, 2, 3, KC], F32, "gs")
    hg = k.sb([128, 2, 3, KC], F32, "hg")
    b_in = Buf("vec_in")
    k.b_vec = Buf("vec")
    P.dma("sp", modv[:], modv_d, [], [b_in], "ld_modv")
    P.dma("sp", npre[:], npre_d, [], [b_in], "ld_npre")
    P.dma("sp", npost[:], npost_d, [], [b_in], "ld_npost")
    vec = {}
    for v in range(2):
        for s in range(3):
            P.I("dve", "scalar_tensor_tensor", [b_in], [k.b_vec], out=gs[:, v, s, :], in0=modv[:, v, 3 * s + 1, :], scalar=1.0,
                in1=npre[:, s, :], op0=ALU.add, op1=ALU.mult)
            P.I("dve", "scalar_tensor_tensor", [b_in], [k.b_vec], out=hg[:, v, s, :], in0=modv[:, v, 3 * s + 2, :],
                scalar=(1.0 if s == 1 else 0.5), in1=npost[:, s, :], op0=ALU.mult, op1=ALU.mult)
            vec[(s, v)] = dict(gs=gs[:, v, s, :], sh=modv[:, v, 3 * s, :], hg=hg[:, v, s, :])
    k.b_vec_in = b_in
    return vec


def norm_tile(k, xT_d, xkey, off, w, V, xy, b_xy, h, b_h, sq, tmpf, stat, rstd, b_rstd, rtmp, b_rtmp):
    P = k.P
    xv = xT_d[:, off:off + w].rearrange("(kc p) t -> p kc t", p=128)
    for qd in range(4):
        P.dma("sp", xy[:, 4 * qd:4 * qd + 4, :w], xv[:, 4 * qd:4 * qd + 4, :], [k.dbuf((xkey, off))], b_xy[4 * qd:4 * qd + 4], f"ld_xy{qd}")
    st, b_st = stat
    for kk in range(KC):
        s_, b_s = sq.next()
        P.I("act", "activation", [b_xy[kk]], [b_s], out=s_[:, :w], in_=xy[:, kk, :w], func=AF.Square)
        P.I("pe", "matmul", [b_s, k.b_const], [b_st], st[:, :w], lhsT=k.ones[:], rhs=s_[:, :w], start=(kk == 0), stop=(kk == KC - 1))
    emit_rstd(k, st, b_st, rstd, b_rstd, rtmp, b_rtmp, w, D)
    for kk in range(KC):
        t_, b_t = tmpf.next()
        P.I("dve", "tensor_tensor", [b_xy[kk], b_rstd], [b_t], out=t_[:, :w], in0=xy[:, kk, :w], in1=rstd[:, :w], op=ALU.mult)
        P.I("act", "activation", [b_t, k.b_vec, k.b_vec_in], [b_h[kk]], out=h[:, kk, :w], in_=t_[:, :w], func=AF.Identity,
            bias=V["sh"][:, kk:kk + 1], scale=V["gs"][:, kk:kk + 1])


def residual_update(k, xin_d, xin_key, xout_d, xout_key, off, w, V, xy, b_xy, rstd, b_rstd, tmpf, xst, xo):
    P = k.P
    lds = []

    def ld(n):
        x_, b_x = xst.next()
        P.dma("sp", x_[:, :w], xin_d[n * 128:(n + 1) * 128, off:off + w], [k.dbuf((xin_key, off))], [b_x], f"xst{(xst.i - 1) % len(xst.items)}")
        lds.append((x_, b_x))
    ld(0)
    ld(1)
    for n in range(KC):
        x_, b_x = lds[n]
        t_, b_t = tmpf.next()
        P.I("dve", "tensor_tensor", [b_xy[n], b_rstd], [b_t], out=t_[:, :w], in0=xy[:, n, :w], in1=rstd[:, :w], op=ALU.mult)
        o_, b_o = xo.next()
        oi = (xo.i - 1) % len(xo.items)
        P.I("dve", "scalar_tensor_tensor", [b_t, b_x, k.b_vec], [b_o], out=o_[:, :w], in0=t_[:, :w], scalar=V["hg"][:, n:n + 1],
            in1=x_[:, :w], op0=ALU.mult, op1=ALU.add)
        if n + 2 < KC:
            ld(n + 2)
        P.dma("sp", xout_d[n * 128:(n + 1) * 128, off:off + w], o_[:, :w], [b_o], [k.dbuf((xout_key, off))], f"xo{oi}")


def ffn_sublayer(k, xin_d, xin_key, xout_d, xout_key, w1_d, w2_d, vec, s):
    P = k.P
    with k.phase():
        xy = k.sb([128, KC, W], F32, "xy"); b_xy = [Buf() for _ in range(KC)]
        h = k.sb([128, KC, W], BF16, "h"); b_h = [Buf() for _ in range(KC)]
        G = k.sb([128, JC, W], BF16, "G"); b_G = [Buf() for _ in range(JC)]
        w1r = k.ring(2, [128, KC, 2, 256], BF16, "w1r")
        w2r = k.ring(2, [128, JC, 256], BF16, "w2r")
        sq = k.ring(2, [128, W], BF16, "sq")
        tmpf = k.ring(2, [128, W], F32, "tmpf")
        sg = k.ring(2, [128, W], F32, "sg")
        rstd = k.sb([128, W], F32, "rstd"); b_rstd = Buf()
        rtmp = k.sb([128, W], F32, "rtmp"); b_rtmp = Buf()
        xst = k.ring(3, [128, W], F32, "xst")
        xo = k.ring(2, [128, W], F32, "xo")
        gp = Ring([(k.ps([128, W]), Buf(), k.ps([128, W]), Buf()) for _ in range(2)])
        yp = k.ring(3, [128, W], F32, "yp", psum=True)
        stat = (k.ps([128, W]), Buf())
        w1v = w1_d.rearrange("(kc p) n -> p kc n", p=128)
        w2v = w2_d.rearrange("(jc p) n -> p jc n", p=128)
        k.wscr("w1", JC // 2, [KC, 2, 256]); k.wscr("w2", KC // 2, [JC, 256])
        for ti, (off, w, is_ctx) in enumerate(tiles_list()):
            first = ti == 0
            V = vec[(s, 1 if is_ctx else 0)]
            norm_tile(k, xin_d, xin_key, off, w, V, xy, b_xy, h, b_h, sq, tmpf, stat, rstd, b_rstd, rtmp, b_rtmp)
            for jp in range(JC // 2):
                wt, b_w = w1r.next()
                si = (w1r.i - 1) % 2
                k.wload("w1", jp, wt, b_w, si, first, [(wt[:, :, 0, :], w1v[:, :, jp * 256:(jp + 1) * 256]),
                                                        (wt[:, :, 1, :], w1v[:, :, DFF + jp * 256:DFF + (jp + 1) * 256])])
                for jj in range(2):
                    j = jp * 2 + jj
                    pg, b_pg, pu, b_pu = gp.next()
                    for kk in range(KC):
                        P.I("pe", "matmul", [b_w, b_h[kk]], [b_pg], pg[:, :w], lhsT=wt[:, kk, 0, jj * 128:(jj + 1) * 128],
                            rhs=h[:, kk, :w], start=(kk == 0), stop=(kk == KC - 1))
                    for kk in range(KC):
                        P.I("pe", "matmul", [b_w, b_h[kk]], [b_pu], pu[:, :w], lhsT=wt[:, kk, 1, jj * 128:(jj + 1) * 128],
                            rhs=h[:, kk, :w], start=(kk == 0), stop=(kk == KC - 1))
                    s_, b_s = sg.next()
                    P.I("act", "activation", [b_pg], [b_s], out=s_[:, :w], in_=pg[:, :w], func=AF.Silu)
                    P.I("dve", "tensor_tensor", [b_pu, b_s], [b_G[j]], out=G[:, j, :w], in0=pu[:, :w], in1=s_[:, :w], op=ALU.mult)
            st, b_st = stat
            pend_st = None
            k.wflush("w1")
            for npair in range(KC // 2):
                wt, b_w = w2r.next()
                si = (w2r.i - 1) % 2
                k.wload("w2", npair, wt, b_w, si, first, [(wt[:], w2v[:, :, npair * 256:(npair + 1) * 256])])
                for nn in range(2):
                    n = npair * 2 + nn
                    y_, b_y = yp.next()
                    for j in range(JC):
                        P.I("pe", "matmul", [b_w, b_G[j]], [b_y], y_[:, :w], lhsT=wt[:, j, nn * 128:(nn + 1) * 128],
                            rhs=G[:, j, :w], start=(j == 0), stop=(j == JC - 1))
                    if pend_st is not None:
                        pend_st()
                    P.I("dve", "tensor_copy", [b_y], [b_xy[n]], out=xy[:, n, :w], in_=y_[:, :w])
                    q_, b_q = sq.next()
                    P.I("act", "activation", [b_xy[n]], [b_q], out=q_[:, :w], in_=xy[:, n, :w], func=AF.Square)

                    def _st(n=n, q_=q_, b_q=b_q, w=w):
                        P.I("pe", "matmul", [b_q, k.b_const], [b_st], st[:, :w], lhsT=k.ones[:], rhs=q_[:, :w], start=(n == 0), stop=(n == KC - 1))
                    pend_st = _st
            pend_st()
            pend_st = None
            k.wflush("w2")
            emit_rstd(k, st, b_st, rstd, b_rstd, rtmp, b_rtmp, w, D)
            residual_update(k, xin_d, xin_key, xout_d, xout_key, off, w, V, xy, b_xy, rstd, b_rstd, tmpf, xst, xo)


def chunk_role(c):
    if c < 8: return ("q", "a", c)
    if c < 10: return ("k", "a", c - 8)
    if c < 12: return ("v", "a", c - 10)
    if c < 20: return ("q", "b", c - 12)
    if c < 22: return ("k", "b", c - 20)
    if c < 24: return ("v", "b", c - 22)
    if c < 32: return ("q", "c", c - 24)
    if c < 40: return ("k", "c", c - 32)
    if c < 48: return ("v", "c", c - 40)
    return ("g", None, c - 48)


QBASE = {"a": 0, "b": 8, "c": 16}
KBASE = {"a": 0, "b": 2, "c": 4}


def inproj_stage(k, x_d, xkey, win_d, vec, T):
    P = k.P
    with k.phase():
        xy = k.sb([128, KC, W], F32, "xy"); b_xy = [Buf() for _ in range(KC)]
        h = k.sb([128, KC, W], BF16, "h"); b_h = [Buf() for _ in range(KC)]
        wr = k.ring(3, [128, KC, 256], BF16, "wr")
        sq = k.ring(2, [128, W], BF16, "sq")
        tmpf = k.ring(2, [128, W], F32, "tmpf")
        rstd = k.sb([128, W], F32, "rstd"); b_rstd = Buf()
        rtmp = k.sb([128, W], F32, "rtmp"); b_rtmp = Buf()
        hrs = k.sb([128, W], F32, "hrs"); b_hrs = Buf()
        hrt = k.sb([128, W], F32, "hrt"); b_hrt = Buf()
        rope = k.sb([128, 4, W], F32, "rope"); b_rope = Buf()
        rmat = k.sb([128, 2, 128], F32, "rmat"); b_rmat = Buf()
        qkn = k.sb([128, 2], F32, "qkn"); b_qkn = Buf()
        qn = k.ring(2, [128, W], F32, "qn")
        t1 = k.ring(2, [128, W], F32, "t1")
        t2 = k.ring(2, [128, W], F32, "t2")
        ob = k.ring(3, [128, W], BF16, "ob")
        vt = k.ring(2, [128, 256], BF16, "vt")
        acc = k.ring(3, [128, W], F32, "acc", psum=True)
        stat = (k.ps([128, W]), Buf())
        hst = stat
        rot = k.ring(2, [128, W], F32, "rot", psum=True)
        vps = k.ring(2, [128, 256], F32, "vps", psum=True)
        P.dma("sp", rmat[:], T["rmat"].rearrange("a p m -> p a m"), [], [b_rmat], "ld_rmat")
        P.dma("sp", qkn[:], T["qkn"], [], [b_qkn], "ld_qkn")
        wv = win_d.rearrange("(kc p) n -> p kc n", p=128)
        due = []

        def step():
            cur = list(due)
            del due[:]
            for f in cur:
                f()

        def make_stage1(kind, br, idx, a_, b_a, off, w, is_ctx):
            def stage1():
                if kind == "g":
                    o_, b_o = ob.next()
                    oi = (ob.i - 1) % 3
                    P.I("act", "activation", [b_a], [b_o], out=o_[:, :w], in_=a_[:, :w], func=AF.Sigmoid)
                    P.dma("sp", T["gT"][idx * 128:(idx + 1) * 128, off:off + w], o_[:, :w], [b_o], [k.dbuf("gT")], f"ob{oi}")
                    return
                q_, b_q = qn.next()
                if br == "a":
                    s_, b_s = sq.next()
                    P.I("act", "activation", [b_a], [b_s], out=s_[:, :w], in_=a_[:, :w], func=AF.Square)
                    P.I("pe", "matmul", [b_s, k.b_const], [hst[1]], hst[0][:, :w], lhsT=k.ones[:], rhs=s_[:, :w], start=True, stop=True)
                    emit_rstd(k, hst[0], hst[1], hrs, b_hrs, hrt, b_hrt, w, HD)
                    col = 0 if kind == "q" else 1
                    P.I("dve", "scalar_tensor_tensor", [b_a, b_hrs, b_qkn], [b_q], out=q_[:, :w], in0=a_[:, :w], scalar=qkn[:, col:col + 1],
                        in1=hrs[:, :w], op0=ALU.mult, op1=ALU.mult)
                else:
                    P.I("act", "copy", [b_a], [b_q], out=q_[:, :w], in_=a_[:, :w])
                due.append(make_stage2(kind, br, idx, q_, b_q, off, w, is_ctx))
            return stage1

        def make_stage2(kind, br, idx, q_, b_q, off, w, is_ctx):
            def stage2():
                o_, b_o = ob.next()
                oi = (ob.i - 1) % 3
                if is_ctx:
                    P.I("dve", "tensor_copy", [b_q], [b_o], out=o_[:, :w], in_=q_[:, :w])
                else:
                    ri = 1 if br == "c" else 0
                    r_, b_r = rot.next()
                    P.I("pe", "matmul", [b_q, b_rmat], [b_r], r_[:, :w], lhsT=rmat[:, ri, :], rhs=q_[:, :w], start=True, stop=True)
                    a1, b_1 = t1.next()
                    a2, b_2 = t2.next()
                    P.I("dve", "tensor_tensor", [b_q, b_rope], [b_1], out=a1[:, :w], in0=q_[:, :w], in1=rope[:, 2 * ri, :w], op=ALU.mult)
                    P.I("dve", "tensor_tensor", [b_r, b_rope], [b_2], out=a2[:, :w], in0=r_[:, :w], in1=rope[:, 2 * ri + 1, :w], op=ALU.mult)
                    P.I("dve", "tensor_tensor", [b_1, b_2], [b_o], out=o_[:, :w], in0=a1[:, :w], in1=a2[:, :w], op=ALU.add)
                if kind == "q":
                    row = (QBASE[br] + idx) * 128
                    P.dma("sp", T["qT"][row:row + 128, off:off + w], o_[:, :w], [b_o], [k.dbuf("qT")], f"ob{oi}")
                else:
                    row = (KBASE[br] + idx) * 128
                    P.dma("sp", T["kT"][row:row + 128, off:off + w], o_[:, :w], [b_o], [k.dbuf("kT")], f"ob{oi}")
            return stage2

        k.wscr("win", 48, [KC, 256])
        for ti, (off, w, is_ctx) in enumerate(tiles_list()):
            first = ti == 0
            k.wflush("win")
            V = vec[(1, 1 if is_ctx else 0)]
            norm_tile(k, x_d, xkey, off, w, V, xy, b_xy, h, b_h, sq, tmpf, stat, rstd, b_rstd, rtmp, b_rtmp)
            if not is_ctx:
                P.dma("sp", rope[:, :, :w], T["rope"][:, :, off:off + w].rearrange("a p t -> p a t"), [], [b_rope], "ld_rope")
            for u in range(48):
                wt, b_w = wr.next()
                wi = (wr.i - 1) % 3
                k.wload("win", u, wt, b_w, wi, first, [(wt[:], wv[:, :, u * 256:(u + 1) * 256])])
                role0 = chunk_role(2 * u)
                if role0[0] == "v":
                    vbase = {"a": 0, "b": 2, "c": 4}[role0[1]] + role0[2]
                    for tb in range((w + 127) // 128):
                        m = min(128, w - tb * 128)
                        vp, b_vp = vps.next()
                        for kk in range(KC):
                            P.I("pe", "matmul", [b_w, b_h[kk]], [b_vp], vp[:m, :], lhsT=h[:, kk, tb * 128:tb * 128 + m], rhs=wt[:, kk, :],
                                start=(kk == 0), stop=(kk == KC - 1))
                        v_, b_v = vt.next()
                        vi = (vt.i - 1) % 2
                        P.I("act", "copy", [b_vp], [b_v], out=v_[:m, :], in_=vp[:m, :])
                        ti = (off // 128 + tb)
                        P.dma("sp", T["vimg"][vbase:vbase + 2, :m, ti, :].rearrange("c p d -> p c d"),
                              v_[:m, :].rearrange("p (c d) -> p c d", c=2), [b_v], [k.dbuf("vimg")], f"vt{vi}")
                        step()
                    continue
                for cc in range(2):
                    c = 2 * u + cc
                    kind, br, idx = chunk_role(c)
                    a_, b_a = acc.next()
                    for kk in range(KC):
                        P.I("pe", "matmul", [b_w, b_h[kk]], [b_a], a_[:, :w], lhsT=wt[:, kk, cc * 128:(cc + 1) * 128], rhs=h[:, kk, :w],
                            start=(kk == 0), stop=(kk == KC - 1))
                    step()
                    due.append(make_stage1(kind, br, idx, a_, b_a, off, w, is_ctx))
            step()
            step()
            step()


def attn_stage(k, T):
    P = k.P
    NLT = TL // 128
    with k.phase():
        Kr = k.ring(2, [128, NR, TLC], BF16, "K")
        Vr = k.ring(2, [128, NR, NTI, 128], BF16, "V")
        qr = k.ring(3, [128, W], BF16, "q")
        pr = k.ring(6, [128, W], BF16, "p")
        pm = k.ring(2, [128, W], BF16, "pm")
        bm = k.sb([128, 6, W], BF16, "bm"); hm = k.sb([128, 2, W], BF16, "hm"); b_bm = Buf()
        obr = k.ring(2, [128, W], BF16, "ob")
        f1 = k.ring(2, [128, W], F32, "f1")
        f2 = k.ring(2, [128, W], F32, "f2")
        f3 = k.ring(2, [128, W], F32, "f3")
        sqb = k.ring(2, [128, W], BF16, "sqb")
        lacc = k.ring(4, [128, W], F32, "lacc")
        rstd = k.sb([128, W], F32, "rstd"); b_rstd = Buf()
        rtmp = k.sb([128, W], F32, "rtmp"); b_rtmp = Buf()
        kh = k.sb([128, 2, 2, 128], BF16, "kh"); vh = k.sb([128, 2, 2, 128], BF16, "vh"); b_halo = Buf()
        esink = k.sb([128, 8], F32, "esink"); b_sink = Buf()
        lam = k.sb([128, 4, 64], F32, "lam"); b_lam = Buf()
        lt = k.sb([128, 2, 64], F32, "lt"); ls = k.sb([128, 2], F32, "ls"); nlam = k.sb([128, 1], F32, "nlam")
        linit = k.sb([128, 1], F32, "linit"); subln = k.sb([128, 1], F32, "subln"); sl = k.sb([128, 1], F32, "sl")
        S = k.ring(4, [128, W], F32, "S", psum=True)
        ACC = k.ring(4, [128, W], F32, "ACC", psum=True)
        scale_ab = float(HD) ** -0.5
        scale_c = float(HD // 2) ** -0.5

        P.dma("sp", esink[:], T["sink"], [], [b_sink], "ld_sink")
        P.I("act", "activation", [b_sink], [b_sink], out=esink[:], in_=esink[:], func=AF.Exp)
        P.dma("sp", lam[:], T["lam"], [], [b_lam], "ld_lam")
        P.dma("sp", linit[:], T["linit"], [], [b_lam], "ld_linit")
        P.dma("sp", subln[:], T["subln"], [], [b_lam], "ld_subln")
        P.I("dve", "tensor_tensor", [b_lam], [b_lam], out=lt[:, 0, :], in0=lam[:, 0, :], in1=lam[:, 1, :], op=ALU.mult)
        P.I("dve", "tensor_tensor", [b_lam], [b_lam], out=lt[:, 1, :], in0=lam[:, 2, :], in1=lam[:, 3, :], op=ALU.mult)
        P.I("dve", "reduce_sum", [b_lam], [b_lam], out=ls[:], in_=lt[:], axis=AX.X)
        P.I("act", "activation", [b_lam], [b_lam], out=ls[:], in_=ls[:], func=AF.Exp)
        P.I("dve", "tensor_tensor", [b_lam], [b_lam], out=nlam[:], in0=ls[:, 1:2], in1=ls[:, 0:1], op=ALU.subtract)
        P.I("dve", "tensor_tensor", [b_lam], [b_lam], out=nlam[:], in0=nlam[:], in1=linit[:], op=ALU.subtract)
        P.I("dve", "tensor_scalar", [b_lam], [b_lam], out=sl[:], in0=linit[:], scalar1=-1.0, scalar2=1.0, op0=ALU.mult, op1=ALU.add)
        P.I("dve", "tensor_tensor", [b_lam], [b_lam], out=sl[:], in0=sl[:], in1=subln[:], op=ALU.mult)
        P.dma("sp", bm[:], T["bmask"].rearrange("m p t -> p m t"), [], [b_bm], "ld_bm")
        P.dma("sp", hm[:], T["hmask"].rearrange("m p t -> p m t"), [], [b_bm], "ld_hm")
        P.dma("sp", kh[:], T["khalo"].rearrange("g s p n -> p g s n"), [], [b_halo], "ld_kh")
        P.dma("sp", vh[:], T["vhalo"].rearrange("g s p n -> p g s n"), [], [b_halo], "ld_vh")

        def load_kv(kc, vc):
            Kt, b_K = Kr.next(); ki = (Kr.i - 1) % 2
            Vt, b_V = Vr.next(); vi = (Vr.i - 1) % 2
            for r in range(NR):
                P.dma("sp", Kt[:, r, :], T["kT_all"][r, kc * 128:(kc + 1) * 128, :], [k.dbuf("kT")], [b_K], f"K{ki}")
                P.dma("sp", Vt[:, r, :, :], T["v_all"][r, vc], [k.dbuf("vimg")], [b_V], f"V{vi}")
            return Kt, b_K, Vt, b_V

        def load_q(qc, off, w):
            q_, b_q = qr.next(); qi = (qr.i - 1) % 3
            P.dma("sp", q_[:, :w], T["qT"][qc * 128:(qc + 1) * 128, off:off + w], [k.dbuf("qT")], [b_q], f"q{qi}")
            return q_, b_q

        def store_o(oc, off, w, o_, b_o):
            oi = (obr.i - 1) % 2
            P.dma("sp", T["attnT"][oc * 128:(oc + 1) * 128, off:off + w], o_[:, :w], [b_o], [k.dbuf("attnT")], f"ao{oi}")

        def all_keys(is_ctx):
            ks = []
            if not is_ctx:
                for r in range(NR):
                    for t in range(NLT):
                        ks.append((r, t, 128))
            for r in range(NR):
                for ct in range(NCT):
                    ks.append((r, NLT + ct, min(128, CTXL - ct * 128)))
            return ks

        def softmax_loop(keys, w, scale, O, b_O, L, b_L):
            n = len(keys)
            la, b_la = lacc.next()
            P.I("pool", "memset", [], [b_la], la[:, :w], 0.0)
            LA = 3

            def issue_s(i):
                kd = keys[i]
                s_, b_s = S.next()
                P.I("pe", "matmul", kd["kbufs"] + [kd["q"][1]], [b_s], s_[:kd["nk"], :w], lhsT=kd["kT"], rhs=kd["q"][0], start=True, stop=True)
                return s_, b_s
            sq_ = [issue_s(i) for i in range(min(LA, n))]
            for i, kd in enumerate(keys):
                nk = kd["nk"]
                s_, b_s = sq_[i]
                p_, b_p = pr.next()
                P.I("act", "activation", [b_s], [b_p], out=p_[:nk, :w], in_=s_[:nk, :w], func=AF.Exp, scale=scale)
                if i + LA < n:
                    sq_.append(issue_s(i + LA))
                if kd.get("mask") is not None:
                    m_, b_m = kd["mask"]
                    p2, b_p2 = pm.next()
                    P.I("dve", "tensor_tensor", [b_p, b_m], [b_p2], out=p2[:nk, :w], in0=p_[:nk, :w], in1=m_[:nk, :w], op=ALU.mult)
                    p_, b_p = p2, b_p2
                P.I("pe", "matmul", kd["vbufs"] + [b_p], [b_O], O[:, :w], lhsT=kd["v"], rhs=p_[:nk, :w], start=(i == 0), stop=(i == n - 1))
                if i % 2 == 0:
                    P.I("pe", "matmul", [k.b_const, b_p], [b_L], L[:, :w], lhsT=k.ones[:nk, :], rhs=p_[:nk, :w], start=(i == 0), stop=False)
                else:
                    P.I("dve", "tensor_tensor", [b_la, b_p], [b_la], out=la[:nk, :w], in0=la[:nk, :w], in1=p_[:nk, :w], op=ALU.add)
            P.I("pe", "matmul", [k.b_const, b_la], [b_L], L[:, :w], lhsT=k.ones32[:], rhs=la[:, :w], start=False, stop=True)

        for g in range(2):
            Kt, b_K, Vt, b_V = load_kv(g, g)
            for hh in range(4):
                hd = 4 * g + hh
                for (off, w, is_ctx) in tiles_list():
                    q_, b_q = load_q(QBASE["a"] + hd, off, w)
                    keys = []
                    for (r, t, nk) in all_keys(is_ctx):
                        keys.append(dict(kT=Kt[:, r, t * 128:t * 128 + nk], kbufs=[b_K], v=Vt[:nk, r, t, :], vbufs=[b_V], nk=nk,
                                         q=(q_[:, :w], b_q)))
                    O, b_O = ACC.next(); L, b_L = ACC.next()
                    softmax_loop(keys, w, scale_ab, O, b_O, L, b_L)
                    r_, b_r = f1.next()
                    P.I("dve", "reciprocal", [b_L], [b_r], out=r_[:, :w], in_=L[:, :w])
                    o_, b_o = obr.next()
                    P.I("dve", "tensor_tensor", [b_O, b_r], [b_o], out=o_[:, :w], in0=O[:, :w], in1=r_[:, :w], op=ALU.mult)
                    store_o(hd, off, w, o_, b_o)

        Kb = k.sb([128, TL], BF16, "Kb"); Vb = k.sb([128, NLT, 128], BF16, "Vb")
        Kbc = k.sb([128, NR, CTXL], BF16, "Kbc"); Vbc = k.sb([128, NR, NCT, 128], BF16, "Vbc")
        b_Kb = Buf(); b_Vb = Buf()
        for g in range(2):
            P.dma("sp", Kb[:], T["kT"][(2 + g) * 128:(3 + g) * 128, 0:TL], [k.dbuf("kT")], [b_Kb], "Kb")
            P.dma("sp", Vb[:], T["vimg"][2 + g, :, 0:NLT, :], [k.dbuf("vimg")], [b_Vb], "Vb")
            for r in range(NR):
                P.dma("sp", Kbc[:, r, :], T["kT_all"][r, (2 + g) * 128:(3 + g) * 128, TL:TLC], [k.dbuf("kT")], [b_Kb], "Kb")
                P.dma("sp", Vbc[:, r, :, :], T["v_all"][r, 2 + g, :, NLT:NLT + NCT, :], [k.dbuf("vimg")], [b_Vb], "Vb")
            for hh in range(4):
                hd = 4 * g + hh
                for ti, (off, w, is_ctx) in enumerate(tiles_list()):
                    q_, b_q = load_q(QBASE["b"] + hd, off, w)
                    qq = (q_[:, :w], b_q)
                    keys = []
                    if not is_ctx:
                        t0 = off // 128
                        for rel in range(-1, 5):
                            kt = t0 + rel
                            if kt < 0 or kt >= NLT:
                                continue
                            keys.append(dict(kT=Kb[:, kt * 128:(kt + 1) * 128], kbufs=[b_Kb], v=Vb[:, kt, :], vbufs=[b_Vb], nk=128,
                                             mask=(bm[:, rel + 1, :], b_bm), q=qq))
                        for side, cond in ((0, off == 0), (1, off + w == TL)):
                            if cond and NR > 1:
                                keys.append(dict(kT=kh[:, g, side, :], kbufs=[b_halo], v=vh[:, g, side, :], vbufs=[b_halo], nk=128,
                                                 mask=(hm[:, side, :], b_bm), q=qq))
                    for r in range(NR):
                        for ct in range(NCT):
                            nk = min(128, CTXL - ct * 128)
                            keys.append(dict(kT=Kbc[:, r, ct * 128:ct * 128 + nk], kbufs=[b_Kb], v=Vbc[:nk, r, ct, :], vbufs=[b_Vb], nk=nk, q=qq))
                    O, b_O = ACC.next(); L, b_L = ACC.next()
                    softmax_loop(keys, w, scale_ab, O, b_O, L, b_L)
                    r_, b_r = f1.next()
                    P.I("dve", "tensor_scalar", [b_L, b_sink], [b_r], out=r_[:, :w], in0=L[:, :w], scalar1=esink[:, hd:hd + 1], scalar2=None,
                        op0=ALU.add)
                    r2, b_r2 = f2.next()
                    P.I("dve", "reciprocal", [b_r], [b_r2], out=r2[:, :w], in_=r_[:, :w])
                    o_, b_o = obr.next()
                    P.I("dve", "tensor_tensor", [b_O, b_r2], [b_o], out=o_[:, :w], in0=O[:, :w], in1=r2[:, :w], op=ALU.mult)
                    store_o(8 + hd, off, w, o_, b_o)

        for hd in range(8):
            Kt, b_K, Vt, b_V = load_kv(4 + hd, 4 + hd)
            for (off, w, is_ctx) in tiles_list():
                q_, b_q = load_q(QBASE["c"] + hd, off, w)
                parts = []
                for half in range(2):
                    lo, hi = half * 64, half * 64 + 64
                    keys = []
                    for (r, t, nk) in all_keys(is_ctx):
                        keys.append(dict(kT=Kt[lo:hi, r, t * 128:t * 128 + nk], kbufs=[b_K], v=Vt[:nk, r, t, :], vbufs=[b_V], nk=nk,
                                         q=(q_[lo:hi, :w], b_q)))
                    O, b_O = ACC.next(); L, b_L = ACC.next()
                    softmax_loop(keys, w, scale_c, O, b_O, L, b_L)
                    rr, b_rr = f1.next()
                    P.I("dve", "reciprocal", [b_L], [b_rr], out=rr[:, :w], in_=L[:, :w])
                    aa, b_aa = f2.next()
                    P.I("dve", "tensor_tensor", [b_O, b_rr], [b_aa], out=aa[:, :w], in0=O[:, :w], in1=rr[:, :w], op=ALU.mult)
                    parts.append((aa, b_aa))
                (a1, b_a1), (a2, b_a2) = parts
                o3, b_o3 = f3.next()
                P.I("dve", "scalar_tensor_tensor", [b_a2, b_a1, b_lam], [b_o3], out=o3[:, :w], in0=a2[:, :w], scalar=nlam[:, 0:1], in1=a1[:, :w],
                    op0=ALU.mult, op1=ALU.add)
                s_, b_s = sqb.next()
                P.I("act", "activation", [b_o3], [b_s], out=s_[:, :w], in_=o3[:, :w], func=AF.Square)
                st, b_st = S.next()
                P.I("pe", "matmul", [b_s, k.b_const], [b_st], st[:, :w], lhsT=k.ones[:], rhs=s_[:, :w], start=True, stop=True)
                emit_rstd(k, st, b_st, rstd, b_rstd, rtmp, b_rtmp, w, HD)
                o_, b_o = obr.next()
                P.I("dve", "scalar_tensor_tensor", [b_o3, b_rstd, b_lam], [b_o], out=o_[:, :w], in0=o3[:, :w], scalar=sl[:, 0:1], in1=rstd[:, :w],
                    op0=ALU.mult, op1=ALU.mult)
                store_o(16 + hd, off, w, o_, b_o)


def merge_stage(k, xin_d, xin_key, xout_d, xout_key, wbr_d, wout_d, vec, T):
    P = k.P
    with k.phase():
        at = k.sb([128, 24, W], BF16, "at"); b_at = Buf()
        gt = k.sb([128, 48, W], BF16, "gt"); b_gt = Buf()
        ym = k.sb([128, KC, W], BF16, "ym"); b_ym = [Buf() for _ in range(KC)]
        xy = k.sb([128, KC, W], F32, "xy"); b_xy = [Buf() for _ in range(KC)]
        wbr = k.ring(2, [128, 3, 8, 256], BF16, "wbr")
        wor = k.ring(2, [128, KC, 256], BF16, "wor")
        sq = k.ring(2, [128, W], BF16, "sq")
        tmpf = k.ring(2, [128, W], F32, "tmpf")
        m1 = k.ring(2, [128, W], F32, "m1")
        m2 = k.ring(2, [128, W], F32, "m2")
        rstd = k.sb([128, W], F32, "rstd"); b_rstd = Buf()
        rtmp = k.sb([128, W], F32, "rtmp"); b_rtmp = Buf()
        xst = k.ring(3, [128, W], F32, "xst")
        xo = k.ring(2, [128, W], F32, "xo")
        bp = k.ring(6, [128, W], F32, "bp", psum=True)
        stat = (k.ps([128, W]), Buf())
        wbv = wbr_d.rearrange("r (kc p) n -> p r kc n", p=128)
        wov = wout_d.rearrange("(kc p) n -> p kc n", p=128)
        k.wscr("wbr", KC // 2, [3, 8, 256]); k.wscr("wor", KC // 2, [KC, 256])
        for ti, (off, w, is_ctx) in enumerate(tiles_list()):
            first = ti == 0
            V = vec[(1, 1 if is_ctx else 0)]
            P.dma("sp", at[:, :, :w], T["attnT"][:, off:off + w].rearrange("(c p) t -> p c t", p=128), [k.dbuf("attnT")], [b_at], "ld_at")
            P.dma("sp", gt[:, :, :w], T["gT"][:, off:off + w].rearrange("(c p) t -> p c t", p=128), [k.dbuf("gT")], [b_gt], "ld_gt")
            for npair in range(KC // 2):
                wt, b_w = wbr.next(); wi = (wbr.i - 1) % 2
                k.wload("wbr", npair, wt, b_w, wi, first, [(wt[:, r, :, :], wbv[:, r, :, npair * 256:(npair + 1) * 256]) for r in range(3)])
                for nn in range(2):
                    n = npair * 2 + nn
                    pss = [bp.next() for _ in range(3)]
                    for r in range(3):
                        p_, b_p = pss[r]
                        for kk in range(8):
                            P.I("pe", "matmul", [b_w, b_at], [b_p], p_[:, :w], lhsT=wt[:, r, kk, nn * 128:(nn + 1) * 128], rhs=at[:, r * 8 + kk, :w],
                                start=(kk == 0), stop=(kk == 7))
                    a_, b_a = m1.next()
                    P.I("dve", "tensor_tensor", [pss[0][1], b_gt], [b_a], out=a_[:, :w], in0=pss[0][0][:, :w], in1=gt[:, n, :w], op=ALU.mult)
                    c_, b_c = m2.next()
                    P.I("dve", "tensor_tensor", [pss[1][1], b_gt], [b_c], out=c_[:, :w], in0=pss[1][0][:, :w], in1=gt[:, 16 + n, :w], op=ALU.mult)
                    P.I("dve", "tensor_tensor", [b_a, b_c], [b_a], out=a_[:, :w], in0=a_[:, :w], in1=c_[:, :w], op=ALU.add)
                    c2, b_c2 = m2.next()
                    P.I("dve", "tensor_tensor", [pss[2][1], b_gt], [b_c2], out=c2[:, :w], in0=pss[2][0][:, :w], in1=gt[:, 32 + n, :w], op=ALU.mult)
                    P.I("dve", "tensor_tensor", [b_a, b_c2], [b_ym[n]], out=ym[:, n, :w], in0=a_[:, :w], in1=c2[:, :w], op=ALU.add)
            k.wflush("wbr")
            st, b_st = stat
            for npair in range(KC // 2):
                wt, b_w = wor.next(); wi = (wor.i - 1) % 2
                k.wload("wor", npair, wt, b_w, wi, first, [(wt[:], wov[:, :, npair * 256:(npair + 1) * 256])])
                for nn in range(2):
                    n = npair * 2 + nn
                    y_, b_y = bp.next()
                    for kk in range(KC):
                        P.I("pe", "matmul", [b_w, b_ym[kk]], [b_y], y_[:, :w], lhsT=wt[:, kk, nn * 128:(nn + 1) * 128], rhs=ym[:, kk, :w],
                            start=(kk == 0), stop=(kk == KC - 1))
                    P.I("dve", "tensor_copy", [b_y], [b_xy[n]], out=xy[:, n, :w], in_=y_[:, :w])
                    q_, b_q = sq.next()
                    P.I("act", "activation", [b_xy[n]], [b_q], out=q_[:, :w], in_=xy[:, n, :w], func=AF.Square)
                    P.I("pe", "matmul", [b_q, k.b_const], [b_st], st[:, :w], lhsT=k.ones[:], rhs=q_[:, :w], start=(n == 0), stop=(n == KC - 1))
            k.wflush("wor")
            emit_rstd(k, st, b_st, rstd, b_rstd, rtmp, b_rtmp, w, D)
            residual_update(k, xin_d, xin_key, xout_d, xout_key, off, w, V, xy, b_xy, rstd, b_rstd, tmpf, xst, xo)


def new_k():
    nc = bass.Bass("TRN2", target_bir_lowering=False)
    es = contextlib.ExitStack()
    k = K(nc, es)
    k.es_cur = es
    return k


def mod_stage(k, cT_d, wada_d, bada_d, mo, b_mo):
    P = k.P
    NCH = 9 * KC
    with k.phase():
        cs = k.sb([128, KC, 2], F32, "cs"); b_cs = Buf()
        bs = k.sb([128, DEPTH, NCH], F32, "bs"); b_bs = Buf()
        wr = k.ring(4, [128, KC, 256], F32, "wr")
        pp = k.ring(4, [128, 2], F32, "pp", psum=True)
        P.dma("sp", cs[:], cT_d, [], [b_cs], "ld_c")
        P.dma("sp", bs[:], bada_d, [], [b_bs], "ld_b")
        P.I("act", "activation", [b_cs], [b_cs], out=cs[:], in_=cs[:], func=AF.Silu)
        for l in range(DEPTH):
            wv = wada_d[l].rearrange("(kc p) n -> p kc n", p=128)
            for u in range(NCH // 2):
                wt, b_w = wr.next(); wi = (wr.i - 1) % 4
                P.dma("sp", wt[:], wv[:, :, u * 256:(u + 1) * 256], [], [b_w], f"wa{wi}")
                for cc in range(2):
                    c = 2 * u + cc
                    p_, b_p = pp.next()
                    for kk in range(KC):
                        P.I("pe", "matmul", [b_w, b_cs], [b_p], p_[:, 0:2], lhsT=wt[:, kk, cc * 128:(cc + 1) * 128], rhs=cs[:, kk, :],
                            start=(kk == 0), stop=(kk == KC - 1))
                    P.I("dve", "tensor_scalar", [b_p, b_bs], [b_mo], out=mo[:, l, c, :], in0=p_[:, 0:2], scalar1=bs[:, l, c:c + 1], scalar2=None,
                        op0=ALU.add)


def layer_vecs(k, l, mo, b_mo, npre, npost, b_np, gs, hg):
    P = k.P
    vec = {}
    for v in range(2):
        for s in range(3):
            def m(i):
                return mo[:, l, i * KC:(i + 1) * KC, v]
            P.I("dve", "scalar_tensor_tensor", [b_mo, b_np], [k.b_vec], out=gs[:, v, s, :], in0=m(3 * s + 1), scalar=1.0,
                in1=npre[:, l, s, :], op0=ALU.add, op1=ALU.mult)
            P.I("dve", "scalar_tensor_tensor", [b_mo, b_np], [k.b_vec], out=hg[:, v, s, :], in0=m(3 * s + 2),
                scalar=(1.0 if s == 1 else 0.5), in1=npost[:, l, s, :], op0=ALU.mult, op1=ALU.mult)
            vec[(s, v)] = dict(gs=gs[:, v, s, :], sh=m(3 * s), hg=hg[:, v, s, :])
    return vec


def build_fused():
    k = new_k(); nc = k.nc; P = k.P
    xin = k.dram_in("xT", [D, TLC], F32)
    cT = k.dram_in("cT", [128, KC, 2], F32)
    wada = k.dram_in("wada", [DEPTH, D, 9 * D], F32)
    bada = k.dram_in("bada", [128, DEPTH, 9 * KC], F32)
    npre_d = k.dram_in("npre", [128, DEPTH, 3, KC], F32)
    npost_d = k.dram_in("npost", [128, DEPTH, 3, KC], F32)
    wffin = k.dram_in("wffin", [DEPTH, 2, D, 2 * DFF], F32)
    wffout = k.dram_in("wffout", [DEPTH, 2, DFF, D], F32)
    win = k.dram_in("win", [DEPTH, D, INW], F32)
    wbr = k.dram_in("wbr", [DEPTH, 3, 1024, D], F32)
    wout = k.dram_in("wout", [DEPTH, D, D], F32)
    qkn = k.dram_in("qkn", [DEPTH, 128, 2], F32)
    sink = k.dram_in("sink", [DEPTH, 128, 8], F32)
    lam = k.dram_in("lam", [DEPTH, 128, 4, 64], F32)
    linit = k.dram_in("linit", [DEPTH, 128, 1], F32)
    subln = k.dram_in("subln", [DEPTH, 128, 1], F32)
    rope = k.dram_in("rope", [4, 128, TL], F32)
    rmat = k.dram_in("rmat", [2, 128, 128], F32)
    bmask = k.dram_in("bmask", [6, 128, W], BF16)
    hmask = k.dram_in("hmask", [2, 128, W], BF16)
    khalo = k.dram_in("khalo", [2, 2, 128, 128], BF16)
    vhalo = k.dram_in("vhalo", [2, 2, 128, 128], BF16)
    xout = k.dram_out("xout", [D, TLC], F32)

    def scratch(name, shape, dt):
        return nc.dram_tensor(name, list(shape), dt, kind="Internal").ap()
    xs = [scratch("xs0", [D, TLC], F32), scratch("xs1", [D, TLC], F32)]
    kT = scratch("kT", [NR, NKC * 128, TLC], BF16)
    vimg = scratch("vimg", [NR, NVC, 128, NTI, 128], BF16)
    Tb = dict(rope=rope, rmat=rmat, bmask=bmask, hmask=hmask, khalo=khalo, vhalo=vhalo,
              qT=scratch("qT", [NQC * 128, TLC], BF16), gT=scratch("gT", [48 * 128, TLC], BF16),
              attnT=scratch("attnT", [24 * 128, TLC], BF16), kT=kT[0], vimg=vimg[0], kT_all=kT, v_all=vimg)

    setup_consts(k)
    mo = k.sb([128, DEPTH, 9 * KC, 2], F32, "mo"); b_mo = Buf("mo")
    npre = k.sb([128, DEPTH, 3, KC], F32, "npre"); npost = k.sb([128, DEPTH, 3, KC], F32, "npost"); b_np = Buf("np")
    gs = k.sb([128, 2, 3, KC], F32, "gs"); hg = k.sb([128, 2, 3, KC], F32, "hg")
    k.b_vec = Buf("vec"); k.b_vec_in = b_mo
    P.dma("sp", npre[:], npre_d, [], [b_np], "ld_npre")
    P.dma("sp", npost[:], npost_d, [], [b_np], "ld_npost")
    mod_stage(k, cT, wada, bada, mo, b_mo)

    seq = []
    nsub = 3 * DEPTH
    for i in range(nsub + 1):
        if i == 0:
            seq.append((xin, "xin"))
        elif i == nsub:
            seq.append((xout, "xout"))
        else:
            seq.append((xs[(i - 1) % 2], f"xs{(i - 1) % 2}"))
    si = 0
    for l in range(DEPTH):
        vec = layer_vecs(k, l, mo, b_mo, npre, npost, b_np, gs, hg)
        T = dict(Tb)
        T.update(qkn=qkn[l], sink=sink[l], lam=lam[l], linit=linit[l], subln=subln[l])
        (a, ak), (b, bk) = seq[si], seq[si + 1]
        ffn_sublayer(k, a, ak, b, bk, wffin[l, 0], wffout[l, 0], vec, 0)
        si += 1
        inproj_stage(k, b, bk, win[l], vec, T)
        attn_stage(k, T)
        (a, ak), (b, bk) = seq[si], seq[si + 1]
        merge_stage(k, a, ak, b, bk, wbr[l], wout[l], vec, T)
        si += 1
        (a, ak), (b, bk) = seq[si], seq[si + 1]
        ffn_sublayer(k, a, ak, b, bk, wffin[l, 1], wffout[l, 1], vec, 2)
        si += 1
    st = k.P.finish()
    k.es.close()
    k.stats = st
    return nc


def fm(v):
    v = np.asarray(v)
    lead = v.shape[:-1]
    return np.ascontiguousarray(np.moveaxis(v.reshape(*lead, KC, 128), -1, 0))


def rope_consts(tok0):
    def tables(rot_dim):
        pos = tok0 + np.arange(TL)
        row = (pos // GRID_W).astype(np.float32)
        col = (pos % GRID_W).astype(np.float32)
        d_ax = rot_dim // 2
        inv = (10000.0 ** (-np.arange(0, d_ax, 2, dtype=np.float32) / d_ax)).astype(np.float32)
        ang_r = row[:, None] * inv[None, :]
        ang_c = col[:, None] * inv[None, :]
        ang = np.concatenate([ang_r, ang_r, ang_c, ang_c], axis=-1).astype(np.float32)
        return np.cos(ang).T.astype(np.float32), np.sin(ang).T.astype(np.float32)
    cab, sab = tables(128)
    cc, sc = tables(64)
    return np.ascontiguousarray(np.stack([cab, sab, np.concatenate([cc, cc], 0), np.concatenate([sc, sc], 0)], 0))


def rot_mats():
    def rt(dim, n):
        q = dim // 4
        m = np.zeros((n, n), np.float32)
        for base in range(0, n, dim):
            for i in range(dim):
                quarter = i // q
                if quarter == 0: src, sgn = i + q, -1.0
                elif quarter == 1: src, sgn = i - q, 1.0
                elif quarter == 2: src, sgn = i + q, -1.0
                else: src, sgn = i - q, 1.0
                m[base + src, base + i] = sgn
        return m
    return np.stack([rt(128, 128), rt(64, 128)], 0)


def band_masks():
    j = np.arange(128)[:, None]
    i = np.arange(W)[None, :]
    ms = []
    for rel in range(-1, 5):
        ms.append((np.abs(rel * 128 + j - i) <= 128))
    return np.stack(ms, 0)


_CACHE = {}


def get_prog(name, builder):
    key = (name, SEQ, DEPTH)
    if key not in _CACHE:
        _CACHE[key] = builder()
    return _CACHE[key]


def kernel(x, c, ctx, c_ctx, w_ada, b_ada, norm_pre, norm_post, w_ff_in, w_ff_out, w_in,
           qk_norm_a, sink_b, lam_c, subln_c, w_branch, w_out):
    f32 = np.float32
    A = lambda v: np.ascontiguousarray(np.asarray(v, f32))
    x = A(x); ctx = A(ctx); c = A(c); c_ctx = A(c_ctx)
    depth = np.asarray(w_ada).shape[0]
    assert depth == DEPTH and x.shape[1] == SEQ
    nb = x.shape[0]
    cores = list(range(nb))
    nc = get_prog("fused", build_fused)
    bc = lambda v, shape: np.ascontiguousarray(np.broadcast_to(v, shape))
    linit = np.array([0.8 - 0.6 * float(np.exp(-0.3 * l)) for l in range(depth)], f32)
    shared = dict(
        wada=A(w_ada), bada=np.ascontiguousarray(A(b_ada).reshape(depth, 9 * KC, 128).transpose(2, 0, 1)),
        npre=fm(A(norm_pre)), npost=fm(A(norm_post)),
        wffin=A(w_ff_in), wffout=A(w_ff_out), win=A(w_in), wbr=A(w_branch), wout=A(w_out),
        qkn=np.ascontiguousarray(A(qk_norm_a).transpose(0, 2, 1)),
        sink=bc(A(sink_b)[:, None, :], (depth, 128, 8)), lam=bc(A(lam_c)[:, None], (depth, 128, 4, 64)),
        linit=bc(linit[:, None, None], (depth, 128, 1)), subln=np.ascontiguousarray(A(subln_c).reshape(depth, 128, 1)),
        rope=rope_consts(0), rmat=rot_mats(), bmask=band_masks().astype(NPBF),
        hmask=np.zeros((2, 128, W), NPBF), khalo=np.zeros((2, 2, 128, 128), NPBF), vhalo=np.zeros((2, 2, 128, 128), NPBF))
    in_maps = []
    for b in cores:
        m = dict(shared)
        m["xT"] = np.ascontiguousarray(np.concatenate([x[b], ctx[b]], 0).T)
        m["cT"] = np.ascontiguousarray(np.stack([c[b], c_ctx], 0).reshape(2, KC, 128).transpose(2, 1, 0))
        in_maps.append(m)
    res = run_bass_kernel_spmd(nc, in_maps, core_ids=cores).results
    out = np.zeros((nb, SEQ, D), f32)
    for b in cores:
        out[b] = np.asarray(res[b]["xout"])[:, :TL].T
    return out
```
